# Optimizing a Trainium2 kernel written in Bass

```python
import math
import jax, jax.numpy as jnp
from jax import lax
import numpy as np

D_MODEL = 1024
BATCH = 8
SEQ = 4096
DEPTH = 2

N_MIXERS = 2
N_A_LAYERS = (DEPTH + N_MIXERS - 1) // N_MIXERS
N_B_LAYERS = DEPTH // N_MIXERS

D_FF = 4 * D_MODEL
RMS_EPS = 1e-6
ROPE_THETA = 10000.0
N_MOD = 6

D_RNN = 1280
LRU_BLOCKS = 5
LRU_BLOCK = D_RNN // LRU_BLOCKS
CONV_WIDTH = 4
LRU_C = 8.0

N_HEADS = 16
HEAD_DIM = D_MODEL // N_HEADS
KV_DIM = HEAD_DIM
IDX_HEADS = 8
IDX_DIM = 64
TOPK_MAX = 256
Q_BLOCK = 128
B_WIDTHS = (N_HEADS * HEAD_DIM, KV_DIM, KV_DIM, IDX_HEADS * IDX_DIM, IDX_DIM, IDX_HEADS)
B_IN = sum(B_WIDTHS)
B_SPLITS = tuple(int(v) for v in np.cumsum(B_WIDTHS)[:-1])

kernel_name = "hybrid_rglru_dsa_adaln_trunk"


def rmsnorm(x, g):
    x32 = x.astype(jnp.float32)
    y = x32 * lax.rsqrt(jnp.mean(x32 * x32, axis=-1, keepdims=True) + RMS_EPS)
    return (y * g.astype(jnp.float32)).astype(x.dtype)


def rope_tables(seq, dim):
    inv = ROPE_THETA ** (-jnp.arange(0, dim, 2, dtype=jnp.float32) / dim)
    ang = jnp.arange(seq, dtype=jnp.float32)[:, None] * inv[None, :]
    return jnp.cos(ang), jnp.sin(ang)


def apply_rope(x, cos, sin):
    half = x.shape[-1] // 2
    cos = cos[:, None, :].astype(x.dtype)
    sin = sin[:, None, :].astype(x.dtype)
    x1, x2 = x[..., :half], x[..., half:]
    return jnp.concatenate([x1 * cos - x2 * sin, x2 * cos + x1 * sin], axis=-1)


def rglru_mixer(h, w_in, conv_w, conv_b, wr, br, wi, bi, lam, w_out):
    B, S, _ = h.shape
    u = h @ w_in
    xb, gb = jnp.split(u, 2, axis=-1)
    gate = jax.nn.gelu(gb)
    xc = lax.conv_general_dilated(
        xb, conv_w[:, None, :].astype(xb.dtype), window_strides=(1,),
        padding=[(CONV_WIDTH - 1, 0)], dimension_numbers=("NWC", "WIO", "NWC"),
        feature_group_count=D_RNN) + conv_b
    xr = xc.reshape(B, S, LRU_BLOCKS, LRU_BLOCK)
    r = jax.nn.sigmoid(jnp.einsum("bsnc,ncd->bsnd", xr, wr).reshape(B, S, D_RNN) + br)
    i = jax.nn.sigmoid(jnp.einsum("bsnc,ncd->bsnd", xr, wi).reshape(B, S, D_RNN) + bi)
    log_a = -LRU_C * r.astype(jnp.float32) * jax.nn.softplus(-lam.astype(jnp.float32))
    a = jnp.exp(log_a)
    b = jnp.sqrt(-jnp.expm1(2.0 * log_a)) * (i * xc).astype(jnp.float32)

    def combine(left, right):
        a_l, b_l = left
        a_r, b_r = right
        return a_l * a_r, a_r * b_l + b_r

    _, hs = lax.associative_scan(combine, (a, b), axis=1)
    y = hs.astype(h.dtype) * gate
    return y @ w_out


def dsa_mixer(h, w_in, q_g, k_g, w_out):
    B, S, _ = h.shape
    topk = min(TOPK_MAX, S // 4)
    cos, sin = rope_tables(S, HEAD_DIM)
    cos_i, sin_i = rope_tables(S, IDX_DIM)
    u = h @ w_in
    q, k, v, qi, ki, wi = jnp.split(u, B_SPLITS, axis=-1)
    q = apply_rope(rmsnorm(q.reshape(B, S, N_HEADS, HEAD_DIM), q_g), cos, sin)
    k = apply_rope(rmsnorm(k, k_g)[:, :, None, :], cos, sin)[:, :, 0]
    qi = apply_rope(qi.reshape(B, S, IDX_HEADS, IDX_DIM), cos_i, sin_i)
    ki = apply_rope(ki[:, :, None, :], cos_i, sin_i)[:, :, 0]
    wi = wi * (IDX_HEADS ** -0.5)
    pos = jnp.arange(S)
    nb = S // Q_BLOCK

    def to_blocks(t):
        return t.reshape((B, nb, Q_BLOCK) + t.shape[2:]).swapaxes(0, 1)

    def block_fn(args):
        qb, qib, wib, posb = args
        logits = jnp.einsum("bqhd,bsd->bqhs", qib, ki) * (IDX_DIM ** -0.5)
        score = jnp.einsum("bqh,bqhs->bqs", wib, jax.nn.relu(logits)).astype(jnp.float32)
        causal = pos[None, :] <= posb[:, None]
        score = jnp.where(causal[None], score, -jnp.inf)
        _, idx = lax.top_k(score, topk)
        kg = jax.vmap(lambda kk, ii: kk[ii])(k, idx)
        vg = jax.vmap(lambda vv, ii: vv[ii])(v, idx)
        valid = idx <= posb[None, :, None]
        s = jnp.einsum("bqhd,bqkd->bqhk", qb, kg).astype(jnp.float32) * (HEAD_DIM ** -0.5)
        s = jnp.where(valid[:, :, None, :], s, -jnp.inf)
        p = jax.nn.softmax(s, axis=-1).astype(vg.dtype)
        return jnp.einsum("bqhk,bqkd->bqhd", p, vg)

    out = lax.map(block_fn, (to_blocks(q), to_blocks(qi), to_blocks(wi), pos.reshape(nb, Q_BLOCK)))
    out = out.swapaxes(0, 1).reshape(B, S, N_HEADS * HEAD_DIM)
    return out @ w_out


def setup_inputs(seed: int = 0) -> dict:
    key = jax.random.key(seed)
    ks = iter(jax.random.split(key, 32))
    f32 = jnp.float32

    def nrm(shape, scale):
        return jax.random.normal(next(ks), shape, f32) * scale

    x = nrm((BATCH, SEQ, D_MODEL), 1.0)
    c = nrm((BATCH, D_MODEL), 1.0)
    norm_mix_g = 1.0 + nrm((DEPTH, D_MODEL), 0.02)
    norm_ffn_g = 1.0 + nrm((DEPTH, D_MODEL), 0.02)
    ada_w = nrm((DEPTH, D_MODEL, N_MOD * D_MODEL), D_MODEL ** -0.5)
    ada_b = nrm((DEPTH, N_MOD * D_MODEL), 0.01)
    a_w_in = nrm((N_A_LAYERS, D_MODEL, 2 * D_RNN), D_MODEL ** -0.5)
    a_conv_w = nrm((N_A_LAYERS, CONV_WIDTH, D_RNN), CONV_WIDTH ** -0.5)
    a_conv_b = nrm((N_A_LAYERS, D_RNN), 0.01)
    a_gate_r_w = nrm((N_A_LAYERS, LRU_BLOCKS, LRU_BLOCK, LRU_BLOCK), LRU_BLOCK ** -0.5)
    a_gate_r_b = nrm((N_A_LAYERS, D_RNN), 0.01)
    a_gate_i_w = nrm((N_A_LAYERS, LRU_BLOCKS, LRU_BLOCK, LRU_BLOCK), LRU_BLOCK ** -0.5)
    a_gate_i_b = nrm((N_A_LAYERS, D_RNN), 0.01)
    a_pow = jax.random.uniform(next(ks), (N_A_LAYERS, D_RNN), f32, 0.9, 0.999)
    s = a_pow ** (1.0 / LRU_C)
    a_lambda = jnp.log(s) - jnp.log1p(-s)
    a_w_out = nrm((N_A_LAYERS, D_RNN, D_MODEL), D_RNN ** -0.5)
    b_w_in = nrm((N_B_LAYERS, D_MODEL, B_IN), D_MODEL ** -0.5)
    b_q_norm_g = 1.0 + nrm((N_B_LAYERS, HEAD_DIM), 0.02)
    b_k_norm_g = 1.0 + nrm((N_B_LAYERS, KV_DIM), 0.02)
    b_w_out = nrm((N_B_LAYERS, N_HEADS * HEAD_DIM, D_MODEL), (N_HEADS * HEAD_DIM) ** -0.5)
    ffn_w1 = nrm((DEPTH, D_MODEL, D_FF), D_MODEL ** -0.5)
    ffn_w2 = nrm((DEPTH, D_FF, D_MODEL), D_FF ** -0.5)
    return {"x": x, "c": c, "norm_mix_g": norm_mix_g, "norm_ffn_g": norm_ffn_g,
            "ada_w": ada_w, "ada_b": ada_b, "a_w_in": a_w_in, "a_conv_w": a_conv_w,
            "a_conv_b": a_conv_b, "a_gate_r_w": a_gate_r_w, "a_gate_r_b": a_gate_r_b,
            "a_gate_i_w": a_gate_i_w, "a_gate_i_b": a_gate_i_b, "a_lambda": a_lambda,
            "a_w_out": a_w_out, "b_w_in": b_w_in, "b_q_norm_g": b_q_norm_g,
            "b_k_norm_g": b_k_norm_g, "b_w_out": b_w_out, "ffn_w1": ffn_w1, "ffn_w2": ffn_w2}


def reference(x, c, norm_mix_g, norm_ffn_g, ada_w, ada_b, a_w_in, a_conv_w, a_conv_b,
              a_gate_r_w, a_gate_r_b, a_gate_i_w, a_gate_i_b, a_lambda, a_w_out,
              b_w_in, b_q_norm_g, b_k_norm_g, b_w_out, ffn_w1, ffn_w2):
    cond = jax.nn.silu(c)
    for i in range(DEPTH):
        mod = (cond @ ada_w[i] + ada_b[i])[:, None, :]
        sh1, sc1, g1, sh2, sc2, g2 = jnp.split(mod, N_MOD, axis=-1)
        j = i // N_MIXERS
        h = rmsnorm(x, norm_mix_g[i]) * (1.0 + sc1) + sh1
        if i % N_MIXERS == 0:
            y = rglru_mixer(h, a_w_in[j], a_conv_w[j], a_conv_b[j], a_gate_r_w[j], a_gate_r_b[j],
                            a_gate_i_w[j], a_gate_i_b[j], a_lambda[j], a_w_out[j])
        else:
            y = dsa_mixer(h, b_w_in[j], b_q_norm_g[j], b_k_norm_g[j], b_w_out[j])
        x = x + g1 * y
        h = rmsnorm(x, norm_ffn_g[i]) * (1.0 + sc2) + sh2
        x = x + g2 * (jnp.square(jax.nn.relu(h @ ffn_w1[i])) @ ffn_w2[i])
    return x
```

```python
import math
from contextlib import ExitStack

import numpy as np
import concourse.bass as bass
import concourse.mybir as mybir
from concourse.bass_utils import run_bass_kernel_spmd

F32 = mybir.dt.float32
BF16 = mybir.dt.bfloat16
ALU = mybir.AluOpType
AF = mybir.ActivationFunctionType
AX = mybir.AxisListType

S_FULL = 4096
D = 1024
T = 512
NB = 4
DFF = 4096
DRNN = 1280
B_IN = 1736
NSLOT = 8
PIECE = 2048
NITER = 10
TOPK = 256
EPS = 1e-6
NEG = -1.0e30

_ESZ = {F32: 4, BF16: 2}
_WKEYS = ("out", "accum_out", "ap")


class _Op:
    __slots__ = ("stream", "chan", "fn", "is_dma", "cpos", "sig", "waits", "idx")


def _is_ap(v):
    return hasattr(v, "ap") and hasattr(v, "tensor") and hasattr(v, "offset")


class Prog:
    STREAMS = ("pe", "act", "dve", "pool", "sp")

    def __init__(self, nc):
        self.nc = nc
        self.ops = []
        self.stream_ops = {s: [] for s in self.STREAMS}
        self.chan_count = {}
        self.chan_last = {}
        self.track = {}
        self.waited = {s: {} for s in self.STREAMS}
        self.chan_ops = {}

    def region(self, ap):
        name = ap.tensor.name
        esz = _ESZ.get(ap.dtype, 4)
        dims = ap.ap
        off = ap.offset
        space = str(ap.space)
        if "PSUM" in space:
            return (name, 0, 128, 0, 1 << 30, True)
        if "SB" in space.upper():
            pstep, npart = dims[0]
            if pstep == 0:
                p0, free0 = 0, off
                p1 = 128
            else:
                p0 = off // pstep
                free0 = off % pstep
                p1 = p0 + npart
            ext = 0
            for st, n in dims[1:]:
                ext += abs(st) * (n - 1)
            return (name, p0, p1, free0 * esz, (free0 + ext + 1) * esz, False)
        ext = 0
        for st, n in dims:
            ext += abs(st) * (n - 1)
        return (name, 0, 1, off * esz, (off + ext + 1) * esz, False)

    def add(self, stream, fn, reads, writes, chan=None):
        op = _Op()
        op.idx = len(self.ops)
        op.stream = stream
        op.is_dma = chan is not None
        op.chan = chan if chan is not None else stream
        op.fn = fn
        op.sig = op.is_dma
        op.waits = []
        self.chan_count[op.chan] = self.chan_count.get(op.chan, 0) + 1
        op.cpos = self.chan_count[op.chan]
        deps = {}

        def add_dep(pidx):
            p = self.ops[pidx]
            if deps.get(p.chan, (0, None))[0] < p.cpos:
                deps[p.chan] = (p.cpos, p)

        if op.is_dma and chan in self.chan_last:
            add_dep(self.chan_last[chan])
        for real_w, regs in ((False, reads), (True, writes)):
            for reg in regs:
                name, p0, p1, lo, hi, psum = reg
                eff_w = real_w or psum
                for ent in self.track.get(name, ()):
                    ep0, ep1, elo, ehi, eidx, e_eff, e_real = ent
                    overlap = ep0 < p1 and p0 < ep1 and elo < hi and lo < ehi
                    if overlap and (eff_w or e_eff):
                        prod = self.ops[eidx]
                        same = (not prod.is_dma) and (not op.is_dma) and prod.stream == stream
                        if same:
                            if stream != "pe" and e_real and not real_w:
                                add_dep(eidx)
                        else:
                            add_dep(eidx)
        for real_w, regs in ((False, reads), (True, writes)):
            for reg in regs:
                name, p0, p1, lo, hi, psum = reg
                eff_w = real_w or psum
                lst = self.track.get(name, [])
                new = []
                for ent in lst:
                    ep0, ep1, elo, ehi, eidx, e_eff, e_real = ent
                    covered = p0 <= ep0 and ep1 <= p1 and lo <= elo and ehi <= hi
                    if eidx == op.idx:
                        if covered:
                            continue
                        new.append(ent)
                        continue
                    if covered and eff_w:
                        continue
                    if covered and (not e_eff) and (not eff_w):
                        prod = self.ops[eidx]
                        if (not prod.is_dma) and (not op.is_dma) and prod.stream == stream:
                            continue
                    new.append(ent)
                new.append((p0, p1, lo, hi, op.idx, eff_w, real_w))
                self.track[name] = new
        w = self.waited[stream]
        for ch, (cpos, prod) in deps.items():
            if w.get(ch, 0) >= cpos:
                continue
            w[ch] = cpos
            prod.sig = True
            op.waits.append(prod)
        if not op.is_dma:
            pass
        self.ops.append(op)
        self.stream_ops[stream].append(op)
        self.chan_ops.setdefault(op.chan, []).append(op)
        if op.is_dma:
            self.chan_last[chan] = op.idx
        return op

    def I(self, stream, method, chan=None, xr=(), xw=(), **kw):
        reads, writes = [], []
        for k, v in kw.items():
            if _is_ap(v):
                (writes if k in _WKEYS else reads).append(self.region(v))
        for v in xr:
            reads.append(self.region(v))
        for v in xw:
            writes.append(self.region(v))

        def fn(e, method=method, kw=kw):
            return getattr(e, method)(**kw)

        return self.add(stream, fn, reads, writes, chan=chan)

    def dma(self, stream, chan, out, in_, slow=False):
        if slow:
            return self.I(stream, "dma_start", chan=chan, out=out, in_=in_, allow_slow_non_contiguous=True)
        return self.I(stream, "dma_start", chan=chan, out=out, in_=in_)

    def fence(self, stream, aps):
        return self.add(stream, None, [self.region(a) for a in aps], [])

    def emit(self):
        nc = self.nc
        sigval = {}
        for ch, ops in self.chan_ops.items():
            c = 0
            for op in ops:
                if op.sig:
                    c += 16 if op.is_dma else 1
                sigval[op.idx] = c
        with ExitStack() as es:
            sems = {}
            for ch in self.chan_ops:
                sems[ch] = es.enter_context(nc.semaphore("s_" + ch))
            block = es.enter_context(nc.Block())

            def run(stream, e):
                for op in self.stream_ops[stream]:
                    for prod in op.waits:
                        e.wait_ge(sems[prod.chan], sigval[prod.idx])
                    if op.fn is None:
                        continue
                    ins = op.fn(e)
                    if op.sig:
                        ins.then_inc(sems[op.chan], 16 if op.is_dma else 1)

            @block.tensor
            def _(e):
                run("pe", e)

            @block.scalar
            def _(e):
                run("act", e)

            @block.vector
            def _(e):
                run("dve", e)

            @block.gpsimd
            def _(e):
                run("pool", e)

            @block.sync
            def _(e):
                run("sp", e)


def piece_list(nc_in):
    W = nc_in
    pieces = []

    def kview(w):
        return w.rearrange("(k p) c -> p k c", p=128)

    a_w_in = kview(W["a_w_in"])
    for n in range(5):
        pieces.append(("a_xb%d" % n, [(128, 8, 256, a_w_in[:, :, n * 256:(n + 1) * 256], 0, 256)]))
        pieces.append(("a_gb%d" % n, [(128, 8, 256, a_w_in[:, :, DRNN + n * 256:DRNN + (n + 1) * 256], 0, 256)]))
        gr = W["a_gate_r_w"][n].rearrange("(k p) c -> p k c", p=128)
        gi = W["a_gate_i_w"][n].rearrange("(k p) c -> p k c", p=128)
        pieces.append(("a_gt%d" % n, [(128, 2, 256, gr, 0, 256), (128, 2, 256, gi, 512, 256)]))
    a_w_out = kview(W["a_w_out"])
    for h in range(2):
        for kp in range(3):
            k0, k1 = kp * 4, min(kp * 4 + 4, 10)
            pieces.append(("a_wo%d_%d" % (h, kp), [(128, k1 - k0, 512, a_w_out[:, k0:k1, h * 512:(h + 1) * 512], 0, 512)]))

    def ffn(l):
        w1 = kview(W["ffn_w1_%d" % l])
        w2 = kview(W["ffn_w2_%d" % l])
        for pc in range(16):
            pieces.append(("f%d_w1_%d" % (l, pc), [(128, 8, 256, w1[:, :, pc * 256:(pc + 1) * 256], 0, 256)]))
        for h in range(2):
            for kp in range(8):
                pieces.append(("f%d_w2_%d_%d" % (l, h, kp), [(128, 4, 512, w2[:, kp * 4:(kp + 1) * 4, h * 512:(h + 1) * 512], 0, 512)]))

    ffn(0)
    b_w_in = kview(W["b_w_in"])
    for j in range(4):
        w = min(512, B_IN - j * 512)
        for kp in range(2):
            pieces.append(("b_wi%d_%d" % (j, kp), [(128, 4, w, b_w_in[:, kp * 4:(kp + 1) * 4, j * 512:j * 512 + w], 0, 512)]))
    b_w_out = kview(W["b_w_out"])
    for h in range(2):
        for kp in range(2):
            pieces.append(("b_wo%d_%d" % (h, kp), [(128, 4, 512, b_w_out[:, kp * 4:(kp + 1) * 4, h * 512:(h + 1) * 512], 0, 512)]))
    ffn(1)
    return pieces


def build_nc(S_run=S_FULL, debug=None):
    NT = S_run // T
    nc = bass.Bass("TRN2", target_bir_lowering=False)
    dbg = debug or {}

    def din(name, shape, dt=F32):
        return nc.dram_tensor(name, list(shape), dt, kind="ExternalInput").ap()

    x_d = din("x", [S_FULL, D])
    ccol_d = din("ccol", [128, 8])
    ident_d = din("ident", [128, 128])
    caus_d = din("caus", [128, 128])
    cos_d = din("cosT", [S_FULL, 32])
    sin_d = din("sinT", [S_FULL, 32])
    pw2_d = din("pw2", [128, NITER])
    W = {}
    W["norm_mix_g"] = din("norm_mix_g", [2, D])
    W["norm_ffn_g"] = din("norm_ffn_g", [2, D])
    W["ada_w"] = din("ada_w", [2, D, 6 * D])
    W["ada_b"] = din("ada_b", [2, 6 * D])
    W["a_w_in"] = din("a_w_in", [D, 2 * DRNN])
    W["a_conv_w"] = din("a_conv_w", [4, DRNN])
    W["a_conv_b"] = din("a_conv_b", [1, DRNN])
    W["a_gate_r_w"] = din("a_gate_r_w", [5, 256, 256])
    W["a_gate_r_b"] = din("a_gate_r_b", [1, DRNN])
    W["a_gate_i_w"] = din("a_gate_i_w", [5, 256, 256])
    W["a_gate_i_b"] = din("a_gate_i_b", [1, DRNN])
    W["a_lambda"] = din("a_lambda", [1, DRNN])
    W["a_w_out"] = din("a_w_out", [DRNN, D])
    W["b_w_in"] = din("b_w_in", [D, B_IN])
    W["b_q_norm_g"] = din("b_q_norm_g", [1, 64])
    W["b_k_norm_g"] = din("b_k_norm_g", [1, 64])
    W["b_w_out"] = din("b_w_out", [D, D])
    for l in range(2):
        W["ffn_w1_%d" % l] = din("ffn_w1_%d" % l, [D, DFF])
        W["ffn_w2_%d" % l] = din("ffn_w2_%d" % l, [DFF, D])
    out_d = nc.dram_tensor("out", [S_FULL, D], F32, kind="ExternalOutput").ap()

    pieces = piece_list(W)
    NP_ = len(pieces)
    pidx = {p[0]: i for i, p in enumerate(pieces)}
    tape = nc.dram_tensor("tape", [NP_, 128, PIECE], BF16, kind="Internal").ap()

    dbg_out = {}

    def dbg_tensor(name, shape):
        dbg_out[name] = nc.dram_tensor("dbg_" + name, list(shape), F32, kind="ExternalOutput").ap()
        return dbg_out[name]

    P = Prog(nc)
    es = ExitStack()

    def sb(name, shape, dt=F32):
        return es.enter_context(nc.sbuf_tensor(name, list(shape), dt))

    XT = sb("XT", [128, NB, D])
    HT = sb("HT", [128, 8, T], BF16)
    WS = sb("WS", [128, NSLOT, PIECE], BF16)
    BIG = sb("BIG", [128, 4224])
    ARENA = sb("ARENA", [128, 8192])
    KTE = sb("KTE", [128, S_FULL], BF16)
    KTO = sb("KTO", [128, S_FULL], BF16)
    KITE = sb("KITE", [128, S_FULL], BF16)
    KITO = sb("KITO", [128, S_FULL], BF16)
    VW = sb("VW", [128, 32, 130], BF16)
    COS = sb("COS", [128, NB, 32])
    SIN = sb("SIN", [128, NB, 32])
    IDENT = sb("IDENT", [128, 128])
    CAUS = sb("CAUS", [128, 128])
    ONES = sb("ONES", [128, 128])
    PW2 = sb("PW2", [128, NITER])
    FCOL = sb("FCOL", [128, 2, 4, 8])
    GCOL = sb("GCOL", [128, 2, 2, 8])
    CCOL = sb("CCOL", [128, 8])
    CONDB = sb("CONDB", [128, 8, 128])
    LCOL = sb("LCOL", [128, 13, 10])
    CARRY = sb("CARRY", [128, 10, 3])
    HST = sb("HST", [128, 10])
    QG = sb("QG", [128, 64])
    KG = sb("KG", [128, 64])
    STAT = sb("STAT", [128, 64])
    BIS = sb("BIS", [128, 10 + NITER])
    L1T = sb("L1T", [128, 4, 512])
    L1U = sb("L1U", [128, 4, 512])
    _g01 = L1T[:].rearrange("p a b -> p (a b)")
    _g23 = L1U[:].rearrange("p a b -> p (a b)")
    GATEV = [_g01[:, 0:1024], _g01[:, 1024:2048], _g23[:, 0:1024], _g23[:, 1024:2048]]
    EB = sb("EB", [128, 4, 512], BF16)
    PT = sb("PT", [128, 4, 512], BF16)
    SEL = sb("SEL", [128, 128])
    RTQ = sb("RTQ", [128, NB, 4, 32])
    RTK = sb("RTK", [128, NB, 4, 32])
    L0X = sb("L0X", [128, 1024])
    SEL2 = sb("SEL2", [128, 128])
    DRT = sb("DRT", [128, NB, 128])
    IDENTB = sb("IDENTB", [128, 128], BF16)
    DSG = sb("DSG", [128, 8, 128], BF16)
    RLB = sb("RLB", [128, 4, 512], BF16)
    QIT = sb("QIT", [128, 4, T], BF16)
    WIS = sb("WIS", [128, NB, 3, 8])
    LROW = sb("LROW", [128, 512])
    RB = sb("RB", [128, 512])

    PS = [es.enter_context(nc.psum_tensor("PS%d" % i, [128, 512], F32)) for i in range(8)]
    ps_rr = [0]

    def bank():
        b = PS[ps_rr[0] % 8]
        ps_rr[0] += 1
        return b

    ARENA_bf = ARENA[:].bitcast(BF16)
    H1T = ARENA_bf.rearrange("p (c t) -> p c t", t=T)
    YT = H1T[:, 0:10, :]
    QT = H1T[:, 0:8, :]
    OUTT2 = H1T[:, 8:16, :]
    MASKT = ARENA_bf[:, 24 * 512:32 * 512].rearrange("p (k q) -> p k q", q=128)
    JUNK = ARENA_bf[:, 24 * 512:32 * 512]

    rr = {"dve_pool": 0}

    def alt(*names):
        i = rr.get(names, 0)
        rr[names] = i + 1
        return names[i % len(names)]

    cch = [0]

    def cdma(out, in_, slow=False):
        ch = "c%d" % (cch[0] % 6)
        cch[0] += 1
        P.dma("sp", ch, out, in_, slow=slow)

    cdma(IDENT[:], ident_d)
    cdma(CAUS[:], caus_d)
    cdma(PW2[:], pw2_d)
    cdma(CCOL[:], ccol_d)
    P.I("pool", "memset", ap=ONES[:], constant=1.0)
    P.I("pool", "memset", ap=CARRY[:], constant=0.0)
    P.I("pool", "memset", ap=HST[:], constant=0.0)
    P.I("pool", "memset", ap=VW[:], constant=0.0)
    P.I("pool", "memset", ap=VW[:, :, 0:1], constant=1.0)
    P.I("pool", "memset", ap=VW[:, :, 128:129], constant=1.0)
    P.I("pool", "memset", ap=SEL2[:], constant=0.0)
    P.I("pool", "memset", ap=SEL2[0:1, :], constant=1.0)
    for tz in (KTE, KTO, KITE, KITO):
        P.I("pool", "memset", ap=tz[:], constant=0.0)
    P.I("pool", "memset", ap=SEL[:], constant=0.0)
    P.I("pool", "memset", ap=SEL[64:65, :], constant=1.0)
    P.I("pool", "memset", ap=LROW[:], constant=0.0)
    P.I("dve", "tensor_copy", out=IDENTB[:], in_=IDENT[:])
    for j in range(4):
        cdma(LCOL[:, j, :], W["a_conv_w"][j].rearrange("(c p) -> p c", p=128), slow=True)
    for j, nm in enumerate(["a_conv_b", "a_gate_r_b", "a_gate_i_b", "a_lambda"]):
        cdma(LCOL[:, 4 + j, :], W[nm][0].rearrange("(c p) -> p c", p=128), slow=True)
    for l in range(2):
        cdma(GCOL[:, l, 0, :], W["norm_mix_g"][l].rearrange("(c p) -> p c", p=128), slow=True)
        cdma(GCOL[:, l, 1, :], W["norm_ffn_g"][l].rearrange("(c p) -> p c", p=128), slow=True)
    cdma(QG[:], W["b_q_norm_g"][0:1, :].partition_broadcast(128))
    cdma(KG[:], W["b_k_norm_g"][0:1, :].partition_broadcast(128))

    P.I("act", "activation", out=LCOL[:, 8, :], in_=LCOL[:, 7, :], func=AF.Exp, scale=-1.0)
    P.I("act", "activation", out=LCOL[:, 8, :], in_=LCOL[:, 8, :], func=AF.Ln, bias=1.0)
    P.I("dve", "tensor_scalar", out=LCOL[:, 8, :], in0=LCOL[:, 8, :], scalar1=-8.0, scalar2=None, op0=ALU.mult)
    P.I("dve", "tensor_scalar", out=LCOL[:, 9, :], in0=LCOL[:, 5, :], scalar1=0.5, scalar2=None, op0=ALU.mult)
    P.I("dve", "tensor_scalar", out=LCOL[:, 10, :], in0=LCOL[:, 6, :], scalar1=0.5, scalar2=None, op0=ALU.mult)
    P.I("dve", "tensor_scalar", out=LCOL[:, 11, :], in0=LCOL[:, 8, :], scalar1=2.0, scalar2=None, op0=ALU.mult)
    P.I("pool", "memset", ap=STAT[:, 48:49], constant=0.5)
    P.I("dve", "tensor_scalar", out=LCOL[:, 12, :], in0=LCOL[:, 8, :], scalar1=0.5, scalar2=None, op0=ALU.mult)

    P.I("act", "activation", out=STAT[:, 0:8], in_=CCOL[:], func=AF.Sigmoid)
    P.I("dve", "tensor_tensor", out=CCOL[:], in0=CCOL[:], in1=STAT[:, 0:8], op=ALU.mult)
    P.I("dve", "tensor_copy", out=CONDB[:], in_=CCOL[:].unsqueeze(2).to_broadcast([128, 8, 128]))

    XTF0 = XT[:].rearrange("p a b -> p (a b)")
    STGA = [BIG[:, 0:2048], BIG[:, 2048:4096], XTF0[:, 0:2048], XTF0[:, 2048:4096]]
    MODP = ARENA[:, 0:2048]
    ADAB = ARENA[:, 2048:4096]
    ada_steps = [(l, cp, k) for l in range(2) for cp in range(3) for k in range(8)]

    def ada_load(i):
        l, cp, k = ada_steps[i]
        P.dma("sp", "stg%d" % (i % 4), STGA[i % 4], W["ada_w"][l, k * 128:(k + 1) * 128, cp * 2048:(cp + 1) * 2048])

    for i in range(3):
        ada_load(i)
    ada_i = [0]
    for l in range(2):
        for cp in range(3):
            cdma(ADAB, W["ada_b"][l:l + 1, cp * 2048:(cp + 1) * 2048].partition_broadcast(128))
            banks = [bank() for _ in range(4)]
            for k in range(8):
                i = ada_i[0]
                ada_i[0] += 1
                stg = STGA[i % 4]
                if i + 3 < len(ada_steps):
                    ada_load(i + 3)
                for q in range(4):
                    P.I("pe", "matmul", out=banks[q][:], lhsT=CONDB[:, k, :], rhs=stg[:, q * 512:(q + 1) * 512],
                        start=(k == 0), stop=(k == 7))
            for q in range(4):
                P.I("dve", "tensor_tensor", out=MODP[:, q * 512:(q + 1) * 512], in0=banks[q][:],
                    in1=ADAB[:, q * 512:(q + 1) * 512], op=ALU.add)
            for sgi in range(2):
                seg = 2 * cp + sgi
                src = MODP[:, sgi * 1024:(sgi + 1) * 1024]
                if seg == 2:
                    P.I("pool", "tensor_copy", out=GATEV[2 * l + 0], in_=src)
                elif seg == 5:
                    P.I("pool", "tensor_copy", out=GATEV[2 * l + 1], in_=src)
                else:
                    slot = {0: 1, 1: 0, 3: 3, 4: 2}[seg]
                    b = bank()
                    for c in range(8):
                        P.I("pe", "transpose", out=b[:, c * 8:c * 8 + 8], in_=MODP[0:8, sgi * 1024 + c * 128:sgi * 1024 + (c + 1) * 128],
                            identity=IDENT[0:8, 0:8])
                    P.I("act", "activation", out=FCOL[:, l, slot, :], in_=b[:, 0:64].rearrange("p (c j) -> p c j", j=8)[:, :, 0],
                        func=AF.Identity)
        for (a_i, g_i) in ((0, 0), (2, 1)):
            P.I("dve", "tensor_scalar", out=FCOL[:, l, a_i, :], in0=FCOL[:, l, a_i, :], scalar1=1.0, scalar2=None, op0=ALU.add)
            P.I("dve", "tensor_tensor", out=FCOL[:, l, a_i, :], in0=FCOL[:, l, a_i, :], in1=GCOL[:, l, g_i, :], op=ALU.mult)

    CB = [ARENA_bf[:, 8192 + i * PIECE:8192 + (i + 1) * PIECE] for i in range(4)]
    XTF = XT[:].rearrange("p a b -> p (a b)")
    STG4 = [BIG[:, 0:2048], BIG[:, 2048:4096], XTF[:, 0:2048], XTF[:, 2048:4096]]
    def tape_load(i):
        nm, parts = pieces[i]
        stg = STG4[i % 4]
        for (npart, kk_, cc_, src, off, cst) in parts:
            dst = stg[0:npart, off:off + kk_ * cst].rearrange("p (k c) -> p k c", c=cst)[:, :, 0:cc_]
            P.dma("sp", "stg%d" % (i % 4), dst, src)

    for i in range(min(3, NP_)):
        tape_load(i)
    for i in range(NP_):
        stg = STG4[i % 4]
        cb = CB[i % 4]
        nm_i = pieces[i][0]
        gi = None
        if nm_i.startswith("a_wo"):
            gi, hh = 0, int(nm_i[4])
        elif nm_i.startswith("b_wo"):
            gi, hh = 2, int(nm_i[4])
        elif "_w2_" in nm_i:
            gi, hh = 2 * int(nm_i[1]) + 1, int(nm_i.split("_")[2])
        if gi is not None:
            P.I("dve", "tensor_tensor", out=cb.rearrange("p (k c) -> p k c", c=512), in0=stg.rearrange("p (k c) -> p k c", c=512),
                in1=GATEV[gi][:, hh * 512:(hh + 1) * 512].unsqueeze(1).to_broadcast([128, 4, 512]), op=ALU.mult)
        elif i % 2 == 0:
            P.I("act", "activation", out=cb, in_=stg, func=AF.Copy)
        else:
            P.I("dve", "tensor_copy", out=cb, in_=stg)
        if i + 3 < NP_:
            tape_load(i + 3)
        P.dma("sp", "tp%d" % (i % 4), tape[i], cb)

    wq = {"n": 0}

    def wload(name):
        s = wq["n"] % NSLOT
        wq["n"] += 1
        slot = WS[:, s, :]
        P.dma("sp", "w%d" % s, slot, tape[pidx[name]])
        return slot

    class Stream:
        def __init__(self, names, depth=NSLOT - 2):
            self.names = names
            self.depth = depth
            self.issued = []
            self.pos = 0
            self.released = set()
            self.auto_prev = None

        def _prefetch(self):
            while len(self.issued) < min(len(self.names), self.pos + self.depth):
                p = len(self.issued)
                prior = p - NSLOT
                if prior >= 0 and prior not in self.released:
                    break
                self.issued.append(wload(self.names[p]))

        def get(self, hold=False):
            if self.auto_prev is not None:
                self.released.add(self.auto_prev)
                self.auto_prev = None
            self._prefetch()
            assert len(self.issued) > self.pos, "weight ring deadlock"
            s = self.issued[self.pos]
            if not hold:
                self.auto_prev = self.pos
            tok = self.pos
            self.pos += 1
            return (s, tok) if hold else s

        def done(self, tok):
            self.released.add(tok)

    stop_after = dbg.get("stop_after", None)
    phases = {"l0mix": ("a_",), "l0": ("a_", "f0"), "l1mix": ("a_", "f0", "b_"), None: ("a_", "f0", "b_", "f1")}[stop_after]
    tile_names = [p[0] for p in pieces if p[0][:2] in phases]
    all_names = tile_names * NT
    WST = Stream(all_names)

    def run_pipelined(gen_fns, width=2):
        it = iter(gen_fns)
        active = []

        def start():
            try:
                f = next(it)
            except StopIteration:
                return
            active.append(f())

        for _ in range(width):
            start()
        while active:
            for g in list(active):
                try:
                    next(g)
                except StopIteration:
                    active.remove(g)
                    start()

    def rms_to_HT(l, which):
        a_i, s_i = (0, 1) if which == 0 else (2, 3)
        XN = L1T
        for b in range(NB):
            ss = STAT[:, b:b + 1]
            P.I("act", "activation", out=JUNK[:, 0:D], in_=XT[:, b, :], func=AF.Square, accum_out=ss)
            P.I("act", "activation", out=STAT[:, 8 + b:9 + b], in_=ss, func=AF.Sqrt, scale=1.0 / D, bias=EPS)
            P.I("dve", "reciprocal", out=STAT[:, 16 + b:17 + b], in_=STAT[:, 8 + b:9 + b])
        for b in range(NB):
            P.I("dve", "tensor_scalar", out=DRT[:, b, :], in0=IDENT[:], scalar1=STAT[:, 16 + b:17 + b], scalar2=None, op0=ALU.mult)
        for cg in range(2):
            banks4 = [bank() for _ in range(4)]
            for ci in range(4):
                c = cg * 4 + ci
                for b in range(NB):
                    P.I("pe", "matmul", out=banks4[ci][:, b * 128:(b + 1) * 128], lhsT=XT[:, b, c * 128:(c + 1) * 128], rhs=DRT[:, b, :],
                        start=True, stop=True)
                P.I("act", "activation", out=HT[:, c, :], in_=banks4[ci][:], func=AF.Identity,
                    scale=FCOL[:, l, a_i, c:c + 1], bias=FCOL[:, l, s_i, c:c + 1])

    def residual(banks4, h, gate_idx):
        for b in range(NB):
            P.I("dve", "tensor_tensor", out=XT[:, b, h * 512:(h + 1) * 512], in0=banks4[b][:], in1=XT[:, b, h * 512:(h + 1) * 512], op=ALU.add)

    def ffn(l):
        rms_to_HT(l, 1)
        RL = L1T
        for pc in range(16):
            slot = WST.get().rearrange("p (k c) -> p k c", c=256)
            for fc in range(2):
                pb = bank()
                for k in range(8):
                    P.I("pe", "matmul", out=pb[:], lhsT=slot[:, k, fc * 128:(fc + 1) * 128], rhs=HT[:, k, :],
                        start=(k == 0), stop=(k == 7))
                r = RL[:, (pc * 2 + fc) % 4, :]
                P.I("act", "activation", out=r, in_=pb[:], func=AF.Relu)
                P.I("pool", "tensor_tensor", out=H1T[:, pc * 2 + fc, :], in0=r, in1=r, op=ALU.mult)
        for h in range(2):
            banks4 = [bank() for _ in range(4)]
            for kp in range(8):
                slot = WST.get().rearrange("p (k c) -> p k c", c=512)
                for b in range(NB):
                    for kk in range(4):
                        kc = kp * 4 + kk
                        P.I("pe", "matmul", out=banks4[b][:], lhsT=H1T[:, kc, b * 128:(b + 1) * 128], rhs=slot[:, kk, :],
                            start=(kc == 0), stop=(kc == 31))
            residual(banks4, h, 2 * l + 1)

    XBP = BIG[:, 0:1030].rearrange("p (c t) -> p c t", t=515)
    XC2 = BIG[:, 1030:3078].rearrange("p (s c t) -> p s c t", s=2, t=512)
    AF32 = ARENA[:, 2560:8192]
    _cs0 = [BIG[:, 3078:3590], BIG[:, 3590:4102]] + [AF32[:, i * 512:(i + 1) * 512] for i in range(4)]
    _cs1 = [AF32[:, 2048 + i * 512:2048 + (i + 1) * 512] for i in range(6)]
    CSET = [_cs0, _cs1]
    XCB = ARENA_bf[:, 2 * (2560 + 5120):2 * (2560 + 5120) + 1024].rearrange("p (c t) -> p c t", t=512)
    HALFB = STAT[:, 48:49].to_broadcast([128, 512])

    def l0_mixer():
        rms_to_HT(0, 0)
        st = {"bs_done": set(), "gates": {}, "cdone": set(), "slots": {}}

        def bstage(n):
            while n >= 1 and (n - 1) not in st["bs_done"]:
                yield
            (s_xb_, t_xb) = WST.get(hold=True)
            (s_gb_, t_gb) = WST.get(hold=True)
            (s_gt_, t_gt) = WST.get(hold=True)
            s_xb = s_xb_.rearrange("p (k c) -> p k c", c=256)
            st["slots"][n] = (s_gb_.rearrange("p (k c) -> p k c", c=256), t_gb,
                              s_gt_[:, 0:1024].rearrange("p (g k c) -> p g k c", g=2, k=2), t_gt)
            XC = XC2[:, n % 2]
            while n >= 2 and not ((n - 2, 0) in st["cdone"] and (n - 2, 1) in st["cdone"]):
                yield
            for ci in range(2):
                c = 2 * n + ci
                pb = bank()
                for k in range(8):
                    P.I("pe", "matmul", out=pb[:], lhsT=s_xb[:, k, ci * 128:(ci + 1) * 128], rhs=HT[:, k, :],
                        start=(k == 0), stop=(k == 7))
                P.I("pool", "tensor_copy", out=XBP[:, ci, 0:3], in_=CARRY[:, c, :])
                P.I("act", "activation", out=XBP[:, ci, 3:515], in_=pb[:], func=AF.Copy)
                P.I("act", "activation", out=XC[:, ci, :], in_=pb[:], func=AF.Identity, scale=LCOL[:, 3, c:c + 1],
                    bias=LCOL[:, 4, c:c + 1])
                yield
                P.I("pool", "tensor_copy", out=CARRY[:, c, :], in_=XBP[:, ci, 512:515])
                for j in range(3):
                    P.I("dve", "scalar_tensor_tensor", out=XC[:, ci, :], in0=XBP[:, ci, j:j + 512], scalar=LCOL[:, j, c:c + 1],
                        in1=XC[:, ci, :], op0=ALU.mult, op1=ALU.add)
                    yield
            WST.done(t_xb)
            while n >= 1 and st["gates"].get(n - 1, 0) < 2:
                yield
            P.I("dve", "tensor_copy", out=XCB[:], in_=XC[:])
            st["bs_done"].add(n)
            yield

        def chunk(n, oc):
            c = 2 * n + oc
            R_, I_, B_, H_, GB_, T1_ = CSET[c % 2]
            while n not in st["bs_done"]:
                yield
            while c >= 2 and ((c - 2) // 2, (c - 2) % 2) not in st["cdone"]:
                yield
            s_gb, t_gb, s_gt, t_gt = st["slots"][n]
            XC = XC2[:, n % 2]
            for g, dst, bj in ((0, R_, 9), (1, I_, 10)):
                pb = bank()
                for kc in range(2):
                    P.I("pe", "matmul", out=pb[:], lhsT=s_gt[:, g, kc, oc * 128:(oc + 1) * 128], rhs=XCB[:, kc, :],
                        start=(kc == 0), stop=(kc == 1))
                P.I("act", "activation", out=dst, in_=pb[:], func=AF.Tanh, scale=0.5, bias=LCOL[:, bj, c:c + 1])
            st["gates"][n] = st["gates"].get(n, 0) + 1
            if st["gates"][n] == 2:
                WST.done(t_gt)
            yield
            pb = bank()
            for k in range(8):
                P.I("pe", "matmul", out=pb[:], lhsT=s_gb[:, k, oc * 128:(oc + 1) * 128], rhs=HT[:, k, :],
                    start=(k == 0), stop=(k == 7))
            P.I("act", "activation", out=GB_, in_=pb[:], func=AF.Copy)
            P.I("act", "activation", out=T1_, in_=pb[:], func=AF.Square, scale=math.sqrt(0.044715))
            if oc == 1:
                WST.done(t_gb)
            yield
            P.I("act", "activation", out=B_, in_=R_, func=AF.Exp, scale=LCOL[:, 8, c:c + 1], bias=LCOL[:, 8, c:c + 1])
            yield
            P.I("act", "activation", out=R_, in_=R_, func=AF.Exp, scale=LCOL[:, 12, c:c + 1], bias=LCOL[:, 12, c:c + 1])
            yield
            sqq = st.setdefault(("sq", n), [])
            sqq.append(B_)
            if len(sqq) == 2:
                for bb in sqq:
                    P.I("act", "activation", out=bb, in_=bb, func=AF.Sqrt, scale=-1.0, bias=1.0)
                st[("sqdone", n)] = True
            while not st.get(("sqdone", n)):
                yield
            yield
            P.I("dve", "scalar_tensor_tensor", out=I_, in0=I_, scalar=1.0, in1=XC[:, oc, :], op0=ALU.add, op1=ALU.mult)
            yield
            P.I("dve", "scalar_tensor_tensor", out=B_, in0=B_, scalar=0.5, in1=I_, op0=ALU.mult, op1=ALU.mult)
            yield
            P.I("dve", "scalar_tensor_tensor", out=T1_, in0=T1_, scalar=1.0, in1=GB_, op0=ALU.add, op1=ALU.mult)
            yield
            P.I("act", "activation", out=T1_, in_=T1_, func=AF.Tanh, scale=0.7978845608028654)
            yield
            P.I("dve", "scalar_tensor_tensor", out=T1_, in0=T1_, scalar=1.0, in1=GB_, op0=ALU.add, op1=ALU.mult)
            yield
            P.I("dve", "tensor_tensor_scan", out=H_, data0=R_, data1=B_, initial=HST[:, c:c + 1], op0=ALU.mult, op1=ALU.add)
            yield
            P.I("dve", "tensor_copy", out=HST[:, c:c + 1], in_=H_[:, 511:512])
            P.I("dve", "scalar_tensor_tensor", out=YT[:, c, :], in0=H_, scalar=0.5, in1=T1_, op0=ALU.mult, op1=ALU.mult)
            st["cdone"].add((n, oc))
            yield

        gens = []
        for n in range(5):
            gens.append(lambda n=n: bstage(n))
            gens.append(lambda n=n: chunk(n, 0))
            gens.append(lambda n=n: chunk(n, 1))
        run_pipelined(gens, width=3)
        for h in range(2):
            banks4 = [bank() for _ in range(4)]
            for kp in range(3):
                slot = WST.get().rearrange("p (k c) -> p k c", c=512)
                for b in range(NB):
                    for kk in range(min(4, 10 - kp * 4)):
                        kc = kp * 4 + kk
                        P.I("pe", "matmul", out=banks4[b][:], lhsT=YT[:, kc, b * 128:(b + 1) * 128], rhs=slot[:, kk, :],
                            start=(kc == 0), stop=(kc == 9))
            residual(banks4, h, 0)

    def rope(dst, src, b, nh, eng_a, eng_b, tmp, tabs=None):
        if tabs is None:
            tabs = (COS[:, b:b + 1, :], SIN[:, b:b + 1, :], COS[:, b:b + 1, :], SIN[:, b:b + 1, :])
        c1, s2, c2, s1 = (tt.to_broadcast([128, nh, 32]) for tt in tabs)
        x1, x2 = src[:, :, 0:32], src[:, :, 32:64]
        P.I(eng_a, "tensor_tensor", out=tmp[:, :, 0:32], in0=x2, in1=s2, op=ALU.mult)
        P.I(eng_b, "tensor_tensor", out=dst[:, :, 0:32], in0=x1, in1=c1, op=ALU.mult)
        yield
        P.I(eng_a, "tensor_tensor", out=tmp[:, :, 32:64], in0=x1, in1=s1, op=ALU.mult)
        P.I(eng_b, "tensor_tensor", out=dst[:, :, 32:64], in0=x2, in1=c2, op=ALU.mult)
        yield
        P.I(eng_b, "tensor_tensor", out=dst[:, :, 0:32], in0=dst[:, :, 0:32], in1=tmp[:, :, 0:32], op=ALU.subtract)
        yield
        P.I(eng_b, "tensor_tensor", out=dst[:, :, 32:64], in0=dst[:, :, 32:64], in1=tmp[:, :, 32:64], op=ALU.add)
        yield

    def headnorm(q3, nh, st0, SQ):
        sq = SQ[:, 0:nh * 64].rearrange("p (h d) -> p h d", d=64)
        ss = STAT[:, st0:st0 + nh]
        P.I("act", "activation", out=sq, in_=q3, func=AF.Square)
        yield
        P.I("dve", "tensor_reduce", out=ss, in_=sq, axis=AX.X, op=ALU.add)
        yield
        P.I("act", "activation", out=ss, in_=ss, func=AF.Sqrt, scale=1.0 / 64, bias=EPS)
        yield
        P.I("dve", "reciprocal", out=ss, in_=ss)
        yield
        P.I("dve", "tensor_tensor", out=q3, in0=q3, in1=ss.unsqueeze(2).to_broadcast([128, nh, 64]), op=ALU.mult)
        yield

    def l1_mixer(t):
        rms_to_HT(1, 0)
        P.dma("sp", "rope0", COS[:], cos_d[t * T:(t + 1) * T, :].rearrange("(b p) f -> p b f", p=128))
        P.dma("sp", "rope1", SIN[:], sin_d[t * T:(t + 1) * T, :].rearrange("(b p) f -> p b f", p=128))
        for RT_, G_t in ((RTQ, QG), (RTK, KG)):
            g1 = G_t[:, 0:32].unsqueeze(1).to_broadcast([128, NB, 32])
            g2 = G_t[:, 32:64].unsqueeze(1).to_broadcast([128, NB, 32])
            P.I("pool", "tensor_tensor", out=RT_[:, :, 0, :], in0=COS[:], in1=g1, op=ALU.mult)
            P.I("pool", "tensor_tensor", out=RT_[:, :, 1, :], in0=SIN[:], in1=g2, op=ALU.mult)
            P.I("pool", "tensor_tensor", out=RT_[:, :, 2, :], in0=COS[:], in1=g2, op=ALU.mult)
            P.I("pool", "tensor_tensor", out=RT_[:, :, 3, :], in0=SIN[:], in1=g1, op=ALU.mult)
        QIR = BIG[:, 0:2048].rearrange("p (b f) -> p b f", f=512)
        jbanks = {}
        evac_count = {}

        def mm_j(j):
            w = min(512, B_IN - j * 512)
            banks4 = [bank() for _ in range(4)]
            jbanks[j] = banks4
            for kp in range(2):
                slot = WST.get().rearrange("p (k c) -> p k c", c=512)
                for b in range(NB):
                    for kk in range(4):
                        kc = kp * 4 + kk
                        P.I("pe", "matmul", out=banks4[b][:, 0:w], lhsT=HT[:, kc, b * 128:(b + 1) * 128], rhs=slot[:, kk, 0:w],
                            start=(kc == 0), stop=(kc == 7))

        def step(j, b):
            w = min(512, B_IN - j * 512)
            if b == 0:
                while j > 0 and evac_count[j - 1] < NB:
                    yield
                ps_rr[0] = 4
                mm_j(j)
                ps_rr[0] = 0
            while j not in jbanks:
                yield
            banks4 = jbanks[j]
            gb = t * NB + b
            par = (j * NB + b) % 2
            LX = L1T if par == 0 else L1U
            QF = LX[:, 0, :]
            QR = LX[:, 1, :]
            TM = LX[:, 2, :]
            SQ = LX[:, 3, :]
            P.I("act", "activation", out=QF[:, 0:w], in_=banks4[b][:, 0:w], func=AF.Copy)
            evac_count[j] = evac_count.get(j, 0) + 1
            yield
            if j < 2:
                q3 = QF.rearrange("p (h d) -> p h d", d=64)
                yield from headnorm(q3, 8, 24 + 8 * par, SQ)
                yield from rope(QR.rearrange("p (h d) -> p h d", d=64), q3, b, 8, "pool", "dve", TM.rearrange("p (h d) -> p h d", d=64),
                                tabs=tuple(RTQ[:, b:b + 1, i, :] for i in range(4)))
                pb = PS[par]
                for pr in range(4):
                    P.I("pe", "transpose", out=pb[:, pr * 128:(pr + 1) * 128], in_=QR[:, pr * 128:(pr + 1) * 128], identity=IDENT[:])
                yield
                P.I("act", "activation", out=QT[:, 4 * j:4 * j + 4, b * 128:(b + 1) * 128],
                    in_=pb[:].rearrange("p (a q) -> p a q", q=128), func=AF.Copy)
                yield
            elif j == 2:
                k3 = QF[:, 0:64].rearrange("p (h d) -> p h d", d=64)
                yield from headnorm(k3, 1, 40 + par, SQ)
                KR2 = QR[:, 0:128].rearrange("p (h d) -> p h d", d=64)
                yield from rope(KR2[:, 0:1, :], k3, b, 1, "pool", "dve", TM[:, 0:64].rearrange("p (h d) -> p h d", d=64),
                                tabs=tuple(RTK[:, b:b + 1, i, :] for i in range(4)))
                P.I("pool", "tensor_copy", out=KR2[:, 1:2, :], in_=KR2[:, 0:1, :])
                yield
                pb = PS[par]
                P.I("pe", "transpose", out=pb[:, 0:128], in_=QR[:, 0:128], identity=IDENT[:])
                yield
                P.I("act", "activation", out=KTE[0:64, gb * 128:(gb + 1) * 128], in_=pb[0:64, 0:128], func=AF.Copy)
                P.I("act", "activation", out=KTO[64:128, gb * 128:(gb + 1) * 128], in_=pb[64:128, 0:128], func=AF.Copy)
                P.I("pool", "tensor_copy", out=VW[:, gb, 64:128], in_=QF[:, 64:128])
                yield
                yield from rope(QIR[:, b, 0:384].rearrange("p (h d) -> p h d", d=64), QF[:, 128:512].rearrange("p (h d) -> p h d", d=64),
                                b, 6, "pool", "dve", TM[:, 128:512].rearrange("p (h d) -> p h d", d=64))
            else:
                yield from rope(QIR[:, b, 384:512].rearrange("p (h d) -> p h d", d=64), QF[:, 0:128].rearrange("p (h d) -> p h d", d=64),
                                b, 2, "pool", "dve", TM[:, 0:128].rearrange("p (h d) -> p h d", d=64))
                KI2 = QR[:, 0:128].rearrange("p (h d) -> p h d", d=64)
                yield from rope(KI2[:, 0:1, :], QF[:, 128:192].rearrange("p (h d) -> p h d", d=64), b, 1, "pool", "dve",
                                TM[:, 128:192].rearrange("p (h d) -> p h d", d=64))
                P.I("pool", "tensor_copy", out=KI2[:, 1:2, :], in_=KI2[:, 0:1, :])
                yield
                pb = PS[par]
                P.I("pe", "transpose", out=pb[:, 0:128], in_=QR[:, 0:128], identity=IDENT[:])
                yield
                P.I("act", "activation", out=KITE[0:64, gb * 128:(gb + 1) * 128], in_=pb[0:64, 0:128], func=AF.Copy)
                P.I("act", "activation", out=KITO[64:128, gb * 128:(gb + 1) * 128], in_=pb[64:128, 0:128], func=AF.Copy)
                yield
                wsc = (8 ** -0.5) / 8.0
                P.I("dve", "tensor_scalar", out=WIS[:, b, 0, :], in0=QF[:, 192:200], scalar1=wsc, scalar2=None, op0=ALU.mult)
                yield
                P.I("dve", "tensor_scalar", out=WIS[:, b, 2, :], in0=WIS[:, b, 0, :], scalar1=0.0, scalar2=2.0, op0=ALU.is_ge, op1=ALU.mult)
                yield
                P.I("dve", "tensor_scalar", out=WIS[:, b, 2, :], in0=WIS[:, b, 2, :], scalar1=-1.0, scalar2=None, op0=ALU.add)
                yield
                P.I("dve", "tensor_tensor", out=WIS[:, b, 1, :], in0=WIS[:, b, 0, :], in1=WIS[:, b, 2, :], op=ALU.mult)
                yield
                pb = PS[2 + par]
                for pr in range(4):
                    P.I("pe", "transpose", out=pb[:, pr * 128:(pr + 1) * 128], in_=QIR[:, b, pr * 128:(pr + 1) * 128], identity=IDENT[:])
                yield
                P.I("act", "activation", out=QIT[:, :, b * 128:(b + 1) * 128], in_=pb[:].rearrange("p (a q) -> p a q", q=128), func=AF.Copy)
                yield

        run_pipelined([(lambda j=j, b=b: step(j, b)) for j in range(4) for b in range(NB)], width=2)

        L1TF = L1T[:].rearrange("p a b -> p (a b)")
        L1UF = L1U[:].rearrange("p a b -> p (a b)")
        LO, HI, CNT, G_, MID = (BIS[:, i:i + 1] for i in range(5))
        SGN, TT_, LO2, HI2 = BIS[:, 5:6], BIS[:, 6:7], BIS[:, 7:8], BIS[:, 8 + NITER:9 + NITER]
        WK = BIS[:, 8:8 + NITER]

        def sc(buf, lo, hi):
            if buf == 0:
                return BIG[:, lo:hi]
            if hi <= 2048:
                return L1TF[:, lo:hi]
            assert lo >= 2048
            return L1UF[:, lo - 2048:hi - 2048]

        def idx_gen(b):
            gb = t * NB + b
            buf = gb % 2
            nk = (gb + 1) * 128
            nkc = (nk + 511) // 512
            P.I("dve", "tensor_tensor", out=DSG[:], in0=IDENTB[:].unsqueeze(1).to_broadcast([128, 8, 128]),
                in1=WIS[:, b, 2, :].unsqueeze(2).to_broadcast([128, 8, 128]), op=ALU.mult)
            yield
            ixs = [(kc, h) for kc in range(nkc) for h in range(8)]

            def emit_L(i):
                kc, h = ixs[i]
                w = min(512, nk - kc * 512)
                P.I("pe", "matmul", out=PS[i % 4][:, 0:w], lhsT=QIT[:, h // 2, b * 128:(b + 1) * 128],
                    rhs=(KITE if h % 2 == 0 else KITO)[:, kc * 512:kc * 512 + w], start=True, stop=True)

            for i in range(min(3, len(ixs))):
                emit_L(i)
            for i, (kc, h) in enumerate(ixs):
                w = min(512, nk - kc * 512)
                rl = RLB[:, i % 4, 0:w]
                if i % 3 != 2:
                    P.I("act", "activation", out=rl, in_=PS[i % 4][:, 0:w], func=AF.Relu, scale=WIS[:, b, 1, h:h + 1])
                else:
                    P.I("dve", "tensor_scalar", out=rl, in0=PS[i % 4][:, 0:w], scalar1=0.0, scalar2=WIS[:, b, 1, h:h + 1],
                        op0=ALU.max, op1=ALU.mult)
                scb = PS[4 + (kc % 2)]
                P.I("pe", "matmul", out=scb[:, 0:w], lhsT=DSG[:, h, :], rhs=rl, start=(h == 0), stop=(h == 7))
                if i + 3 < len(ixs):
                    emit_L(i + 3)
                if h == 7:
                    P.I("act", "activation", out=sc(buf, kc * 512, kc * 512 + w), in_=scb[:, 0:w], func=AF.Copy)
                yield

        def bisect_gen(b):
            gb = t * NB + b
            buf = gb % 2
            nk = (gb + 1) * 128
            split = buf == 1 and nk > 2048
            if nk > TOPK:
                if split:
                    P.I("dve", "tensor_reduce", out=LO, in_=sc(buf, 0, 2048), axis=AX.X, op=ALU.min)
                    P.I("dve", "tensor_reduce", out=LO2, in_=sc(buf, 2048, nk), axis=AX.X, op=ALU.min)
                    yield
                    P.I("dve", "tensor_tensor", out=LO, in0=LO, in1=LO2, op=ALU.min)
                else:
                    P.I("dve", "tensor_reduce", out=LO, in_=sc(buf, 0, nk), axis=AX.X, op=ALU.min)
                yield
            dg = sc(buf, gb * 128, (gb + 1) * 128)
            P.I("dve", "tensor_tensor", out=dg, in0=dg, in1=CAUS[:], op=ALU.add)
            yield
            if nk > TOPK:
                if split:
                    P.I("dve", "tensor_reduce", out=HI, in_=sc(buf, 0, 2048), axis=AX.X, op=ALU.max)
                    P.I("dve", "tensor_reduce", out=HI2, in_=sc(buf, 2048, nk), axis=AX.X, op=ALU.max)
                    yield
                    P.I("dve", "tensor_tensor", out=HI, in0=HI, in1=HI2, op=ALU.max)
                else:
                    P.I("dve", "tensor_reduce", out=HI, in_=sc(buf, 0, nk), axis=AX.X, op=ALU.max)
                yield
                P.I("dve", "tensor_tensor", out=HI, in0=HI, in1=LO, op=ALU.subtract)
                yield
                P.I("dve", "tensor_tensor", out=WK, in0=PW2[:], in1=HI.to_broadcast([128, NITER]), op=ALU.mult)
                yield
                n1 = 2048 if split else max(128, (int(nk * 0.45) // 128) * 128)
                n2 = nk - n1
                P.I("dve", "tensor_tensor", out=MID, in0=LO, in1=WK[:, 0:1], op=ALU.add)
                yield
                thr_c = float(2 * TOPK - n2)
                for it in range(NITER):
                    P.I("act", "activation", out=JUNK[:, n1:nk], in_=sc(buf, n1, nk), func=AF.Sign, scale=-1.0, bias=MID, accum_out=SGN)
                    P.I("dve", "tensor_scalar", out=JUNK[:, 0:n1], in0=sc(buf, 0, n1), scalar1=MID, scalar2=None, op0=ALU.is_ge,
                        op1=ALU.add, accum_out=CNT)
                    yield
                    yield
                    P.I("dve", "scalar_tensor_tensor", out=TT_, in0=CNT, scalar=2.0, in1=SGN, op0=ALU.mult, op1=ALU.subtract)
                    yield
                    if it + 1 < NITER:
                        P.I("dve", "scalar_tensor_tensor", out=G_, in0=TT_, scalar=thr_c, in1=WK[:, it:it + 1], op0=ALU.is_ge, op1=ALU.mult)
                        yield
                        P.I("dve", "scalar_tensor_tensor", out=MID, in0=MID, scalar=WK[:, it + 1:it + 2], in1=G_, op0=ALU.subtract, op1=ALU.add)
                        yield
                    else:
                        P.I("dve", "scalar_tensor_tensor", out=G_, in0=TT_, scalar=thr_c, in1=WK[:, it:it + 1], op0=ALU.is_lt, op1=ALU.mult)
                        yield
                        P.I("dve", "tensor_tensor", out=LO, in0=MID, in1=G_, op=ALU.subtract)
                        yield
            else:
                P.I("dve", "memset", ap=LO, constant=-1.0e29)
                yield

        def idx_len(b):
            nk = (t * NB + b + 1) * 128
            return 8 * ((nk + 511) // 512) + 1

        def bisect_len(b):
            nk = (t * NB + b + 1) * 128
            return (6 * NITER + 8) if nk > TOPK else 2

        def mask_build(b):
            gb = t * NB + b
            buf = gb % 2
            nk = (gb + 1) * 128
            nkc = (nk + 511) // 512
            for kc in range(nkc):
                w = min(512, nk - kc * 512)
                mk = L0X[:, (kc % 2) * 512:(kc % 2) * 512 + w]
                P.I("dve", "tensor_scalar", out=mk, in0=sc(buf, kc * 512, kc * 512 + w), scalar1=LO, scalar2=None, op0=ALU.is_ge)
                pb = PS[2 + (kc % 2)]
                for q in range(w // 128):
                    P.I("pe", "transpose", out=pb[:, q * 128:(q + 1) * 128], in_=mk[:, q * 128:(q + 1) * 128], identity=IDENT[:])
                P.I("act", "activation", out=MASKT[:, kc * 4:kc * 4 + w // 128, :], in_=pb[:, 0:w].rearrange("p (a q) -> p a q", q=128), func=AF.Copy)

        def attention(b):
            gb = t * NB + b
            ACC = PS[4:8]
            its = [(kt, g) for kt in range(gb + 1) for g in range(4)]
            DEPTH = 3

            def emit_S(i):
                kt, g = its[i]
                half, pairset = g // 2, g % 2
                P.I("pe", "matmul", out=PS[i % 4][:], lhsT=(KTE if half == 0 else KTO)[:, kt * 128:(kt + 1) * 128],
                    rhs=QT[:, pairset * 4:pairset * 4 + 4, b * 128:(b + 1) * 128], start=True, stop=True)

            for i in range(min(DEPTH, len(its))):
                emit_S(i)
            for i, (kt, g) in enumerate(its):
                eb = EB[:, i % 4, :]
                pt = PT[:, i % 4, :]
                P.I("act", "activation", out=eb, in_=PS[i % 4][:], func=AF.Exp, scale=0.125)
                P.I("dve", "tensor_tensor", out=pt.rearrange("p (a q) -> p a q", q=128), in0=eb.rearrange("p (a q) -> p a q", q=128),
                    in1=MASKT[:, kt:kt + 1, :].to_broadcast([128, 4, 128]), op=ALU.mult)
                if g // 2 == 0:
                    P.I("pe", "matmul", out=ACC[g][0:65, :], lhsT=VW[:, kt, 64:129], rhs=pt, start=(kt == 0), stop=(kt == gb))
                else:
                    P.I("pe", "matmul", out=ACC[g][:], lhsT=VW[:, kt, 0:128], rhs=pt, start=(kt == 0), stop=(kt == gb))
                if i + DEPTH < len(its):
                    emit_S(i + DEPTH)
            for g in range(4):
                half, pairset = g // 2, g % 2
                lr = 64 if half == 0 else 0
                p0, p1 = (0, 64) if half == 0 else (64, 128)
                P.I("act", "activation", out=LROW[lr:lr + 1, :], in_=ACC[g][lr:lr + 1, :], func=AF.Copy)
                pb = PS[g % 4]
                P.I("pe", "matmul", out=pb[:], lhsT=(SEL if half == 0 else SEL2)[:], rhs=LROW[:], start=True, stop=True)
                P.I("dve", "reciprocal", out=RB[p0:p1, :], in_=pb[p0:p1, :])
                P.I("dve", "tensor_tensor", out=OUTT2[p0:p1, pairset * 4:(pairset + 1) * 4, b * 128:(b + 1) * 128],
                    in0=ACC[g][p0:p1, :].rearrange("p (a q) -> p a q", q=128), in1=RB[p0:p1, :].rearrange("p (a q) -> p a q", q=128), op=ALU.mult)

        def run_weighted(ga, na, gb_, nb):
            ia = ib = 0
            a_done = b_done = False
            while not (a_done and b_done):
                if b_done or (not a_done and ia * nb <= ib * na):
                    try:
                        next(ga)
                        ia += 1
                    except StopIteration:
                        a_done = True
                else:
                    try:
                        next(gb_)
                        ib += 1
                    except StopIteration:
                        b_done = True

        for _ in idx_gen(0):
            pass
        for b in range(NB):
            if b + 1 < NB:
                run_weighted(bisect_gen(b), bisect_len(b), idx_gen(b + 1), idx_len(b + 1))
            else:
                for _ in bisect_gen(b):
                    pass
            mask_build(b)
            attention(b)
        for h in range(2):
            banks4 = [bank() for _ in range(4)]
            for kp in range(2):
                slot = WST.get().rearrange("p (k c) -> p k c", c=512)
                for b in range(NB):
                    for kk in range(4):
                        hs = kp * 4 + kk
                        P.I("pe", "matmul", out=banks4[b][:], lhsT=OUTT2[:, hs, b * 128:(b + 1) * 128], rhs=slot[:, kk, :],
                            start=(hs == 0), stop=(hs == 7))
            residual(banks4, h, 2)

    for t in range(NT):
        for b in range(NB):
            P.dma("sp", "xin%d" % b, XT[:, b, :], x_d[t * T + b * 128:t * T + (b + 1) * 128, :])
        l0_mixer()
        if stop_after != "l0mix":
            ffn(0)
            if stop_after != "l0":
                l1_mixer(t)
                if stop_after != "l1mix":
                    ffn(1)
        for b in range(NB):
            P.dma("sp", "xout%d" % b, out_d[t * T + b * 128:t * T + (b + 1) * 128, :], XT[:, b, :])
    P.fence("sp", [out_d[0:NT * T, :]])
    if stop_after is not None:
        pass
    P.emit()
    es.close()
    return nc


def host_consts():
    ident = np.eye(128, dtype=np.float32)
    q = np.arange(128)[:, None]
    k = np.arange(128)[None, :]
    caus = np.where(k <= q, 0.0, NEG).astype(np.float32)
    inv = (np.float32(10000.0) ** (-(np.arange(0, 64, 2, dtype=np.float32)) / np.float32(64))).astype(np.float32)
    ang = (np.arange(S_FULL, dtype=np.float32)[:, None] * inv[None, :]).astype(np.float32)
    cosT = np.cos(ang).astype(np.float32)
    sinT = np.sin(ang).astype(np.float32)
    pw2 = np.tile((0.5 ** np.arange(1, NITER + 1)).astype(np.float32)[None, :], (128, 1))
    return {"ident": ident, "caus": caus, "cosT": cosT, "sinT": sinT, "pw2": pw2}


def make_in_map(inputs, b, consts):
    m = dict(consts)
    m["x"] = np.ascontiguousarray(inputs["x"][b])
    m["ccol"] = np.ascontiguousarray(inputs["c"][b].reshape(8, 128).T)
    for nm in ("norm_mix_g", "norm_ffn_g", "ada_w", "ada_b"):
        m[nm] = np.ascontiguousarray(inputs[nm])
    for nm in ("a_w_in", "a_conv_w", "a_gate_r_w", "a_gate_i_w", "a_w_out", "b_w_in", "b_w_out"):
        m[nm] = np.ascontiguousarray(inputs[nm][0])
    for nm in ("a_conv_b", "a_gate_r_b", "a_gate_i_b", "a_lambda", "b_q_norm_g", "b_k_norm_g"):
        m[nm] = np.ascontiguousarray(inputs[nm])
    for l in range(2):
        m["ffn_w1_%d" % l] = np.ascontiguousarray(inputs["ffn_w1"][l])
        m["ffn_w2_%d" % l] = np.ascontiguousarray(inputs["ffn_w2"][l])
    return m


_NC_CACHE = {}


def kernel(**inputs):
    inputs = {k: np.asarray(v) for k, v in inputs.items()}
    if "full" not in _NC_CACHE:
        _NC_CACHE["full"] = build_nc(S_FULL)
    nc = _NC_CACHE["full"]
    consts = host_consts()
    nb = inputs["x"].shape[0]
    in_maps = [make_in_map(inputs, b, consts) for b in range(nb)]
    res = run_bass_kernel_spmd(nc, in_maps, core_ids=list(range(nb)))
    out = np.stack([np.asarray(r["out"]) for r in res.results], axis=0)
    return out.astype(np.float32)
```

```python
import math
from contextlib import ExitStack

import numpy as np
import concourse.bass as bass
import concourse.mybir as mybir
from concourse.bass_utils import run_bass_kernel_spmd

F32 = mybir.dt.float32
BF16 = mybir.dt.bfloat16
ALU = mybir.AluOpType
AF = mybir.ActivationFunctionType
AX = mybir.AxisListType

S_FULL = 4096
D = 1024
T = 512
NB = 4
DFF = 4096
DRNN = 1280
B_IN = 1736
NSLOT = 8
PIECE = 2048
NITER = 10
TOPK = 256
EPS = 1e-6
NEG = -1.0e30

_ESZ = {F32: 4, BF16: 2}
_WKEYS = ("out", "accum_out", "ap")


class _Op:
    __slots__ = ("stream", "chan", "fn", "is_dma", "cpos", "sig", "waits", "idx")


def _is_ap(v):
    return hasattr(v, "ap") and hasattr(v, "tensor") and hasattr(v, "offset")


class Prog:
    STREAMS = ("pe", "act", "dve", "pool", "sp")

    def __init__(self, nc):
        self.nc = nc
        self.ops = []
        self.stream_ops = {s: [] for s in self.STREAMS}
        self.chan_count = {}
        self.chan_last = {}
        self.track = {}
        self.waited = {s: {} for s in self.STREAMS}
        self.chan_ops = {}

    def region(self, ap):
        name = ap.tensor.name
        esz = _ESZ.get(ap.dtype, 4)
        dims = ap.ap
        off = ap.offset
        space = str(ap.space)
        if "PSUM" in space:
            return (name, 0, 128, 0, 1 << 30, True)
        if "SB" in space.upper():
            pstep, npart = dims[0]
            if pstep == 0:
                p0, free0 = 0, off
                p1 = 128
            else:
                p0 = off // pstep
                free0 = off % pstep
                p1 = p0 + npart
            ext = 0
            for st, n in dims[1:]:
                ext += abs(st) * (n - 1)
            return (name, p0, p1, free0 * esz, (free0 + ext + 1) * esz, False)
        ext = 0
        for st, n in dims:
            ext += abs(st) * (n - 1)
        return (name, 0, 1, off * esz, (off + ext + 1) * esz, False)

    def add(self, stream, fn, reads, writes, chan=None):
        op = _Op()
        op.idx = len(self.ops)
        op.stream = stream
        op.is_dma = chan is not None
        op.chan = chan if chan is not None else stream
        op.fn = fn
        op.sig = op.is_dma
        op.waits = []
        self.chan_count[op.chan] = self.chan_count.get(op.chan, 0) + 1
        op.cpos = self.chan_count[op.chan]
        deps = {}

        def add_dep(pidx):
            p = self.ops[pidx]
            if deps.get(p.chan, (0, None))[0] < p.cpos:
                deps[p.chan] = (p.cpos, p)

        if op.is_dma and chan in self.chan_last:
            add_dep(self.chan_last[chan])
        for real_w, regs in ((False, reads), (True, writes)):
            for reg in regs:
                name, p0, p1, lo, hi, psum = reg
                eff_w = real_w or psum
                for ent in self.track.get(name, ()):
                    ep0, ep1, elo, ehi, eidx, e_eff, e_real = ent
                    overlap = ep0 < p1 and p0 < ep1 and elo < hi and lo < ehi
                    if overlap and (eff_w or e_eff):
                        prod = self.ops[eidx]
                        same = (not prod.is_dma) and (not op.is_dma) and prod.stream == stream
                        if same:
                            if stream != "pe" and (e_real or real_w):
                                add_dep(eidx)
                        else:
                            add_dep(eidx)
        for real_w, regs in ((False, reads), (True, writes)):
            for reg in regs:
                name, p0, p1, lo, hi, psum = reg
                eff_w = real_w or psum
                lst = self.track.get(name, [])
                new = []
                for ent in lst:
                    ep0, ep1, elo, ehi, eidx, e_eff, e_real = ent
                    covered = p0 <= ep0 and ep1 <= p1 and lo <= elo and ehi <= hi
                    if eidx == op.idx:
                        if covered:
                            continue
                        new.append(ent)
                        continue
                    if covered and eff_w:
                        continue
                    if covered and (not e_eff) and (not eff_w):
                        prod = self.ops[eidx]
                        if (not prod.is_dma) and (not op.is_dma) and prod.stream == stream:
                            continue
                    new.append(ent)
                new.append((p0, p1, lo, hi, op.idx, eff_w, real_w))
                self.track[name] = new
        w = self.waited[stream]
        for ch, (cpos, prod) in deps.items():
            if w.get(ch, 0) >= cpos:
                continue
            w[ch] = cpos
            prod.sig = True
            op.waits.append(prod)
        if not op.is_dma:
            pass
        self.ops.append(op)
        self.stream_ops[stream].append(op)
        self.chan_ops.setdefault(op.chan, []).append(op)
        if op.is_dma:
            self.chan_last[chan] = op.idx
        return op

    def I(self, stream, method, chan=None, xr=(), xw=(), **kw):
        reads, writes = [], []
        for k, v in kw.items():
            if _is_ap(v):
                (writes if k in _WKEYS else reads).append(self.region(v))
        for v in xr:
            reads.append(self.region(v))
        for v in xw:
            writes.append(self.region(v))

        def fn(e, method=method, kw=kw):
            return getattr(e, method)(**kw)

        return self.add(stream, fn, reads, writes, chan=chan)

    def dma(self, stream, chan, out, in_, slow=False):
        if slow:
            return self.I(stream, "dma_start", chan=chan, out=out, in_=in_, allow_slow_non_contiguous=True)
        return self.I(stream, "dma_start", chan=chan, out=out, in_=in_)

    def fence(self, stream, aps):
        return self.add(stream, None, [self.region(a) for a in aps], [])

    def emit(self):
        nc = self.nc
        sigval = {}
        for ch, ops in self.chan_ops.items():
            c = 0
            for op in ops:
                if op.sig:
                    c += 16 if op.is_dma else 1
                sigval[op.idx] = c
        with ExitStack() as es:
            sems = {}
            for ch in self.chan_ops:
                sems[ch] = es.enter_context(nc.semaphore("s_" + ch))
            block = es.enter_context(nc.Block())

            def run(stream, e):
                for op in self.stream_ops[stream]:
                    for prod in op.waits:
                        e.wait_ge(sems[prod.chan], sigval[prod.idx])
                    if op.fn is None:
                        continue
                    ins = op.fn(e)
                    if op.sig:
                        ins.then_inc(sems[op.chan], 16 if op.is_dma else 1)

            @block.tensor
            def _(e):
                run("pe", e)

            @block.scalar
            def _(e):
                run("act", e)

            @block.vector
            def _(e):
                run("dve", e)

            @block.gpsimd
            def _(e):
                run("pool", e)

            @block.sync
            def _(e):
                run("sp", e)


def piece_list(nc_in):
    W = nc_in
    pieces = []

    def kview(w):
        return w.rearrange("(k p) c -> p k c", p=128)

    a_w_in = kview(W["a_w_in"])
    for n in range(5):
        pieces.append(("a_xb%d" % n, [(128, 8, 256, a_w_in[:, :, n * 256:(n + 1) * 256], 0, 256)]))
        pieces.append(("a_gb%d" % n, [(128, 8, 256, a_w_in[:, :, DRNN + n * 256:DRNN + (n + 1) * 256], 0, 256)]))
        gr = W["a_gate_r_w"][n].rearrange("(k p) c -> p k c", p=128)
        gi = W["a_gate_i_w"][n].rearrange("(k p) c -> p k c", p=128)
        pieces.append(("a_gt%d" % n, [(128, 2, 256, gr, 0, 256), (128, 2, 256, gi, 512, 256)]))
    a_w_out = kview(W["a_w_out"])
    for h in range(2):
        for kp in range(3):
            k0, k1 = kp * 4, min(kp * 4 + 4, 10)
            pieces.append(("a_wo%d_%d" % (h, kp), [(128, k1 - k0, 512, a_w_out[:, k0:k1, h * 512:(h + 1) * 512], 0, 512)]))

    def ffn(l):
        w1 = kview(W["ffn_w1_%d" % l])
        w2 = kview(W["ffn_w2_%d" % l])
        for pc in range(16):
            pieces.append(("f%d_w1_%d" % (l, pc), [(128, 8, 256, w1[:, :, pc * 256:(pc + 1) * 256], 0, 256)]))
        for h in range(2):
            for kp in range(8):
                pieces.append(("f%d_w2_%d_%d" % (l, h, kp), [(128, 4, 512, w2[:, kp * 4:(kp + 1) * 4, h * 512:(h + 1) * 512], 0, 512)]))

    ffn(0)
    b_w_in = kview(W["b_w_in"])
    for j in range(4):
        w = min(512, B_IN - j * 512)
        for kp in range(2):
            pieces.append(("b_wi%d_%d" % (j, kp), [(128, 4, w, b_w_in[:, kp * 4:(kp + 1) * 4, j * 512:j * 512 + w], 0, 512)]))
    b_w_out = kview(W["b_w_out"])
    for h in range(2):
        for kp in range(2):
            pieces.append(("b_wo%d_%d" % (h, kp), [(128, 4, 512, b_w_out[:, kp * 4:(kp + 1) * 4, h * 512:(h + 1) * 512], 0, 512)]))
    ffn(1)
    return pieces


def build_nc(S_run=S_FULL, debug=None):
    NT = S_run // T
    nc = bass.Bass("TRN2", target_bir_lowering=False)
    dbg = debug or {}

    def din(name, shape, dt=F32):
        return nc.dram_tensor(name, list(shape), dt, kind="ExternalInput").ap()

    x_d = din("x", [S_FULL, D])
    ccol_d = din("ccol", [128, 8])
    ident_d = din("ident", [128, 128])
    caus_d = din("caus", [128, 128])
    cos_d = din("cosT", [S_FULL, 32])
    sin_d = din("sinT", [S_FULL, 32])
    pw2_d = din("pw2", [128, NITER])
    W = {}
    W["norm_mix_g"] = din("norm_mix_g", [2, D])
    W["norm_ffn_g"] = din("norm_ffn_g", [2, D])
    W["ada_w"] = din("ada_w", [2, D, 6 * D])
    W["ada_b"] = din("ada_b", [2, 6 * D])
    W["a_w_in"] = din("a_w_in", [D, 2 * DRNN])
    W["a_conv_w"] = din("a_conv_w", [4, DRNN])
    W["a_conv_b"] = din("a_conv_b", [1, DRNN])
    W["a_gate_r_w"] = din("a_gate_r_w", [5, 256, 256])
    W["a_gate_r_b"] = din("a_gate_r_b", [1, DRNN])
    W["a_gate_i_w"] = din("a_gate_i_w", [5, 256, 256])
    W["a_gate_i_b"] = din("a_gate_i_b", [1, DRNN])
    W["a_lambda"] = din("a_lambda", [1, DRNN])
    W["a_w_out"] = din("a_w_out", [DRNN, D])
    W["b_w_in"] = din("b_w_in", [D, B_IN])
    W["b_q_norm_g"] = din("b_q_norm_g", [1, 64])
    W["b_k_norm_g"] = din("b_k_norm_g", [1, 64])
    W["b_w_out"] = din("b_w_out", [D, D])
    for l in range(2):
        W["ffn_w1_%d" % l] = din("ffn_w1_%d" % l, [D, DFF])
        W["ffn_w2_%d" % l] = din("ffn_w2_%d" % l, [DFF, D])
    out_d = nc.dram_tensor("out", [S_FULL, D], F32, kind="ExternalOutput").ap()

    pieces = piece_list(W)
    NP_ = len(pieces)
    pidx = {p[0]: i for i, p in enumerate(pieces)}
    tape = nc.dram_tensor("tape", [NP_, 128, PIECE], BF16, kind="Internal").ap()

    dbg_out = {}

    def dbg_tensor(name, shape):
        dbg_out[name] = nc.dram_tensor("dbg_" + name, list(shape), F32, kind="ExternalOutput").ap()
        return dbg_out[name]

    P = Prog(nc)
    es = ExitStack()

    def sb(name, shape, dt=F32):
        return es.enter_context(nc.sbuf_tensor(name, list(shape), dt))

    XT = sb("XT", [128, NB, D])
    HT = sb("HT", [128, 8, T], BF16)
    WS = sb("WS", [128, NSLOT, PIECE], BF16)
    BIG = sb("BIG", [128, 4224])
    ARENA = sb("ARENA", [128, 8192])
    KTE = sb("KTE", [128, S_FULL], BF16)
    KTO = sb("KTO", [128, S_FULL], BF16)
    KITE = sb("KITE", [128, S_FULL], BF16)
    KITO = sb("KITO", [128, S_FULL], BF16)
    VW = sb("VW", [128, 32, 130], BF16)
    COS = sb("COS", [128, NB, 32])
    SIN = sb("SIN", [128, NB, 32])
    IDENT = sb("IDENT", [128, 128])
    CAUS = sb("CAUS", [128, 128])
    ONES = sb("ONES", [128, 128])
    PW2 = sb("PW2", [128, NITER])
    FCOL = sb("FCOL", [128, 2, 4, 8])
    GCOL = sb("GCOL", [128, 2, 2, 8])
    CCOL = sb("CCOL", [128, 8])
    CONDB = sb("CONDB", [128, 8, 128])
    LCOL = sb("LCOL", [128, 13, 10])
    CARRY = sb("CARRY", [128, 10, 3])
    HST = sb("HST", [128, 10])
    QG = sb("QG", [128, 64])
    KG = sb("KG", [128, 64])
    STAT = sb("STAT", [128, 64])
    BIS = sb("BIS", [128, 10 + NITER])
    L1T = sb("L1T", [128, 4, 512])
    L1U = sb("L1U", [128, 4, 512])
    _g01 = L1T[:].rearrange("p a b -> p (a b)")
    _g23 = L1U[:].rearrange("p a b -> p (a b)")
    GATEV = [_g01[:, 0:1024], _g01[:, 1024:2048], _g23[:, 0:1024], _g23[:, 1024:2048]]
    EB = sb("EB", [128, 4, 512], BF16)
    PT = sb("PT", [128, 4, 512], BF16)
    SEL = sb("SEL", [128, 128])
    RTQ = sb("RTQ", [128, NB, 4, 32])
    RTK = sb("RTK", [128, NB, 4, 32])
    L0X = sb("L0X", [128, 1024])
    SEL2 = sb("SEL2", [128, 128])
    DRT = sb("DRT", [128, NB, 128])
    IDENTB = sb("IDENTB", [128, 128], BF16)
    DSG = sb("DSG", [128, 8, 128], BF16)
    RLB = sb("RLB", [128, 4, 512], BF16)
    QIT = sb("QIT", [128, 4, T], BF16)
    WIS = sb("WIS", [128, NB, 3, 8])
    LROW = sb("LROW", [128, 512])
    RB = sb("RB", [128, 512])
    LROWB = sb("LROWB", [128, 512])
    RBB = sb("RBB", [128, 512])

    PS = [es.enter_context(nc.psum_tensor("PS%d" % i, [128, 512], F32)) for i in range(8)]
    ps_rr = [0]

    def bank():
        b = PS[ps_rr[0] % 8]
        ps_rr[0] += 1
        return b

    ARENA_bf = ARENA[:].bitcast(BF16)
    H1T = ARENA_bf.rearrange("p (c t) -> p c t", t=T)
    YT = H1T[:, 0:10, :]
    QT = H1T[:, 0:8, :]
    OUTT2 = H1T[:, 8:16, :]
    MASKT = ARENA_bf[:, 24 * 512:32 * 512].rearrange("p (k q) -> p k q", q=128)
    JUNK = ARENA_bf[:, 24 * 512:32 * 512]

    rr = {"dve_pool": 0}

    def alt(*names):
        i = rr.get(names, 0)
        rr[names] = i + 1
        return names[i % len(names)]

    cch = [0]

    def cdma(out, in_, slow=False):
        ch = "c%d" % (cch[0] % 6)
        cch[0] += 1
        P.dma("sp", ch, out, in_, slow=slow)

    cdma(IDENT[:], ident_d)
    cdma(CAUS[:], caus_d)
    cdma(PW2[:], pw2_d)
    cdma(CCOL[:], ccol_d)
    P.I("pool", "memset", ap=ONES[:], constant=1.0)
    P.I("pool", "memset", ap=CARRY[:], constant=0.0)
    P.I("pool", "memset", ap=HST[:], constant=0.0)
    P.I("pool", "memset", ap=VW[:], constant=0.0)
    P.I("pool", "memset", ap=VW[:, :, 0:1], constant=1.0)
    P.I("pool", "memset", ap=VW[:, :, 128:129], constant=1.0)
    P.I("pool", "memset", ap=SEL2[:], constant=0.0)
    P.I("pool", "memset", ap=SEL2[0:1, :], constant=1.0)
    for tz in (KTE, KTO, KITE, KITO):
        P.I("pool", "memset", ap=tz[:], constant=0.0)
    P.I("pool", "memset", ap=SEL[:], constant=0.0)
    P.I("pool", "memset", ap=SEL[64:65, :], constant=1.0)
    P.I("pool", "memset", ap=LROW[:], constant=0.0)
    P.I("pool", "memset", ap=LROWB[:], constant=0.0)
    P.I("dve", "tensor_copy", out=IDENTB[:], in_=IDENT[:])
    for j in range(4):
        cdma(LCOL[:, j, :], W["a_conv_w"][j].rearrange("(c p) -> p c", p=128), slow=True)
    for j, nm in enumerate(["a_conv_b", "a_gate_r_b", "a_gate_i_b", "a_lambda"]):
        cdma(LCOL[:, 4 + j, :], W[nm][0].rearrange("(c p) -> p c", p=128), slow=True)
    for l in range(2):
        cdma(GCOL[:, l, 0, :], W["norm_mix_g"][l].rearrange("(c p) -> p c", p=128), slow=True)
        cdma(GCOL[:, l, 1, :], W["norm_ffn_g"][l].rearrange("(c p) -> p c", p=128), slow=True)
    cdma(QG[:], W["b_q_norm_g"][0:1, :].partition_broadcast(128))
    cdma(KG[:], W["b_k_norm_g"][0:1, :].partition_broadcast(128))

    P.I("act", "activation", out=LCOL[:, 8, :], in_=LCOL[:, 7, :], func=AF.Exp, scale=-1.0)
    P.I("act", "activation", out=LCOL[:, 8, :], in_=LCOL[:, 8, :], func=AF.Ln, bias=1.0)
    P.I("dve", "tensor_scalar", out=LCOL[:, 8, :], in0=LCOL[:, 8, :], scalar1=-8.0, scalar2=None, op0=ALU.mult)
    P.I("dve", "tensor_scalar", out=LCOL[:, 9, :], in0=LCOL[:, 5, :], scalar1=0.5, scalar2=None, op0=ALU.mult)
    P.I("dve", "tensor_scalar", out=LCOL[:, 10, :], in0=LCOL[:, 6, :], scalar1=0.5, scalar2=None, op0=ALU.mult)
    P.I("dve", "tensor_scalar", out=LCOL[:, 11, :], in0=LCOL[:, 8, :], scalar1=2.0, scalar2=None, op0=ALU.mult)
    P.I("pool", "memset", ap=STAT[:, 48:49], constant=0.5)
    P.I("dve", "tensor_scalar", out=LCOL[:, 12, :], in0=LCOL[:, 8, :], scalar1=0.5, scalar2=None, op0=ALU.mult)

    P.I("act", "activation", out=STAT[:, 0:8], in_=CCOL[:], func=AF.Sigmoid)
    P.I("dve", "tensor_tensor", out=CCOL[:], in0=CCOL[:], in1=STAT[:, 0:8], op=ALU.mult)
    P.I("dve", "tensor_copy", out=CONDB[:], in_=CCOL[:].unsqueeze(2).to_broadcast([128, 8, 128]))

    XTF0 = XT[:].rearrange("p a b -> p (a b)")
    STGA = [BIG[:, 0:2048], BIG[:, 2048:4096], XTF0[:, 0:2048], XTF0[:, 2048:4096]]
    MODP = ARENA[:, 0:2048]
    ADAB = ARENA[:, 2048:4096]
    ada_steps = [(l, cp, k) for l in range(2) for cp in range(3) for k in range(8)]

    def ada_load(i):
        l, cp, k = ada_steps[i]
        P.dma("sp", "stg%d" % (i % 4), STGA[i % 4], W["ada_w"][l, k * 128:(k + 1) * 128, cp * 2048:(cp + 1) * 2048])

    for i in range(3):
        ada_load(i)
    ada_i = [0]
    for l in range(2):
        for cp in range(3):
            cdma(ADAB, W["ada_b"][l:l + 1, cp * 2048:(cp + 1) * 2048].partition_broadcast(128))
            banks = [bank() for _ in range(4)]
            for k in range(8):
                i = ada_i[0]
                ada_i[0] += 1
                stg = STGA[i % 4]
                if i + 3 < len(ada_steps):
                    ada_load(i + 3)
                for q in range(4):
                    P.I("pe", "matmul", out=banks[q][:], lhsT=CONDB[:, k, :], rhs=stg[:, q * 512:(q + 1) * 512],
                        start=(k == 0), stop=(k == 7))
            for q in range(4):
                P.I("dve", "tensor_tensor", out=MODP[:, q * 512:(q + 1) * 512], in0=banks[q][:],
                    in1=ADAB[:, q * 512:(q + 1) * 512], op=ALU.add)
            for sgi in range(2):
                seg = 2 * cp + sgi
                src = MODP[:, sgi * 1024:(sgi + 1) * 1024]
                if seg == 2:
                    P.I("pool", "tensor_copy", out=GATEV[2 * l + 0], in_=src)
                elif seg == 5:
                    P.I("pool", "tensor_copy", out=GATEV[2 * l + 1], in_=src)
                else:
                    slot = {0: 1, 1: 0, 3: 3, 4: 2}[seg]
                    b = bank()
                    for c in range(8):
                        P.I("pe", "transpose", out=b[:, c * 8:c * 8 + 8], in_=MODP[0:8, sgi * 1024 + c * 128:sgi * 1024 + (c + 1) * 128],
                            identity=IDENT[0:8, 0:8])
                    P.I("act", "activation", out=FCOL[:, l, slot, :], in_=b[:, 0:64].rearrange("p (c j) -> p c j", j=8)[:, :, 0],
                        func=AF.Identity)
        for (a_i, g_i) in ((0, 0), (2, 1)):
            P.I("dve", "tensor_scalar", out=FCOL[:, l, a_i, :], in0=FCOL[:, l, a_i, :], scalar1=1.0, scalar2=None, op0=ALU.add)
            P.I("dve", "tensor_tensor", out=FCOL[:, l, a_i, :], in0=FCOL[:, l, a_i, :], in1=GCOL[:, l, g_i, :], op=ALU.mult)

    CB = [ARENA_bf[:, 8192 + i * PIECE:8192 + (i + 1) * PIECE] for i in range(4)]
    XTF = XT[:].rearrange("p a b -> p (a b)")
    STG4 = [BIG[:, 0:2048], BIG[:, 2048:4096], XTF[:, 0:2048], XTF[:, 2048:4096]]
    def tape_load(i):
        nm, parts = pieces[i]
        stg = STG4[i % 4]
        for (npart, kk_, cc_, src, off, cst) in parts:
            dst = stg[0:npart, off:off + kk_ * cst].rearrange("p (k c) -> p k c", c=cst)[:, :, 0:cc_]
            P.dma("sp", "stg%d" % (i % 4), dst, src)

    for i in range(min(3, NP_)):
        tape_load(i)
    for i in range(NP_):
        stg = STG4[i % 4]
        cb = CB[i % 4]
        nm_i = pieces[i][0]
        gi = None
        if nm_i.startswith("a_wo"):
            gi, hh = 0, int(nm_i[4])
        elif nm_i.startswith("b_wo"):
            gi, hh = 2, int(nm_i[4])
        elif "_w2_" in nm_i:
            gi, hh = 2 * int(nm_i[1]) + 1, int(nm_i.split("_")[2])
        if gi is not None:
            P.I("dve", "tensor_tensor", out=cb.rearrange("p (k c) -> p k c", c=512), in0=stg.rearrange("p (k c) -> p k c", c=512),
                in1=GATEV[gi][:, hh * 512:(hh + 1) * 512].unsqueeze(1).to_broadcast([128, 4, 512]), op=ALU.mult)
        elif i % 2 == 0:
            P.I("act", "activation", out=cb, in_=stg, func=AF.Copy)
        else:
            P.I("dve", "tensor_copy", out=cb, in_=stg)
        if i + 3 < NP_:
            tape_load(i + 3)
        P.dma("sp", "tp%d" % (i % 4), tape[i], cb)

    wq = {"n": 0}

    def wload(name):
        s = wq["n"] % NSLOT
        wq["n"] += 1
        slot = WS[:, s, :]
        P.dma("sp", "w%d" % s, slot, tape[pidx[name]])
        return slot

    class Stream:
        def __init__(self, names, depth=NSLOT - 2):
            self.names = names
            self.depth = depth
            self.issued = []
            self.pos = 0
            self.released = set()
            self.auto_prev = None

        def _prefetch(self):
            while len(self.issued) < min(len(self.names), self.pos + self.depth):
                p = len(self.issued)
                prior = p - NSLOT
                if prior >= 0 and prior not in self.released:
                    break
                self.issued.append(wload(self.names[p]))

        def get(self, hold=False):
            if self.auto_prev is not None:
                self.released.add(self.auto_prev)
                self.auto_prev = None
            self._prefetch()
            assert len(self.issued) > self.pos, "weight ring deadlock"
            s = self.issued[self.pos]
            if not hold:
                self.auto_prev = self.pos
            tok = self.pos
            self.pos += 1
            return (s, tok) if hold else s

        def done(self, tok):
            self.released.add(tok)

    stop_after = dbg.get("stop_after", None)
    phases = {"l0mix": ("a_",), "l0": ("a_", "f0"), "l1mix": ("a_", "f0", "b_"), None: ("a_", "f0", "b_", "f1")}[stop_after]
    tile_names = [p[0] for p in pieces if p[0][:2] in phases]
    all_names = tile_names * NT
    WST = Stream(all_names)

    def run_pipelined(gen_fns, width=2):
        it = iter(gen_fns)
        active = []

        def start():
            try:
                f = next(it)
            except StopIteration:
                return
            active.append(f())

        for _ in range(width):
            start()
        while active:
            for g in list(active):
                try:
                    next(g)
                except StopIteration:
                    active.remove(g)
                    start()

    def rms_to_HT(l, which):
        a_i, s_i = (0, 1) if which == 0 else (2, 3)
        XN = L1T
        for b in range(NB):
            ss = STAT[:, b:b + 1]
            P.I("act", "activation", out=JUNK[:, 0:D], in_=XT[:, b, :], func=AF.Square, accum_out=ss)
            P.I("act", "activation", out=STAT[:, 8 + b:9 + b], in_=ss, func=AF.Sqrt, scale=1.0 / D, bias=EPS)
            P.I("dve", "reciprocal", out=STAT[:, 16 + b:17 + b], in_=STAT[:, 8 + b:9 + b])
        for b in range(NB):
            P.I("dve", "tensor_scalar", out=DRT[:, b, :], in0=IDENT[:], scalar1=STAT[:, 16 + b:17 + b], scalar2=None, op0=ALU.mult)
        for cg in range(2):
            banks4 = [bank() for _ in range(4)]
            for ci in range(4):
                c = cg * 4 + ci
                for b in range(NB):
                    P.I("pe", "matmul", out=banks4[ci][:, b * 128:(b + 1) * 128], lhsT=XT[:, b, c * 128:(c + 1) * 128], rhs=DRT[:, b, :],
                        start=True, stop=True)
                P.I("act", "activation", out=HT[:, c, :], in_=banks4[ci][:], func=AF.Identity,
                    scale=FCOL[:, l, a_i, c:c + 1], bias=FCOL[:, l, s_i, c:c + 1])

    def residual(banks4, h, gate_idx):
        for b in range(NB):
            P.I("dve", "tensor_tensor", out=XT[:, b, h * 512:(h + 1) * 512], in0=banks4[b][:], in1=XT[:, b, h * 512:(h + 1) * 512], op=ALU.add)

    def ffn(l):
        rms_to_HT(l, 1)
        RL = L1T
        for pc in range(16):
            slot = WST.get().rearrange("p (k c) -> p k c", c=256)
            for fc in range(2):
                pb = bank()
                for k in range(8):
                    P.I("pe", "matmul", out=pb[:], lhsT=slot[:, k, fc * 128:(fc + 1) * 128], rhs=HT[:, k, :],
                        start=(k == 0), stop=(k == 7))
                r = RL[:, (pc * 2 + fc) % 4, :]
                P.I("act", "activation", out=r, in_=pb[:], func=AF.Relu)
                P.I("pool", "tensor_tensor", out=H1T[:, pc * 2 + fc, :], in0=r, in1=r, op=ALU.mult)
        for h in range(2):
            banks4 = [bank() for _ in range(4)]
            for kp in range(8):
                slot = WST.get().rearrange("p (k c) -> p k c", c=512)
                for b in range(NB):
                    for kk in range(4):
                        kc = kp * 4 + kk
                        P.I("pe", "matmul", out=banks4[b][:], lhsT=H1T[:, kc, b * 128:(b + 1) * 128], rhs=slot[:, kk, :],
                            start=(kc == 0), stop=(kc == 31))
            residual(banks4, h, 2 * l + 1)

    XBP = BIG[:, 0:1030].rearrange("p (c t) -> p c t", t=515)
    XC2 = BIG[:, 1030:3078].rearrange("p (s c t) -> p s c t", s=2, t=512)
    AF32 = ARENA[:, 2560:8192]
    _cs0 = [BIG[:, 3078:3590], BIG[:, 3590:4102]] + [AF32[:, i * 512:(i + 1) * 512] for i in range(4)]
    _cs1 = [AF32[:, 2048 + i * 512:2048 + (i + 1) * 512] for i in range(6)]
    CSET = [_cs0, _cs1]
    XCB = ARENA_bf[:, 2 * (2560 + 5120):2 * (2560 + 5120) + 1024].rearrange("p (c t) -> p c t", t=512)
    HALFB = STAT[:, 48:49].to_broadcast([128, 512])

    def l0_mixer():
        rms_to_HT(0, 0)
        st = {"bs_done": set(), "gates": {}, "cdone": set(), "slots": {}}

        def bstage(n):
            while n >= 1 and (n - 1) not in st["bs_done"]:
                yield
            (s_xb_, t_xb) = WST.get(hold=True)
            (s_gb_, t_gb) = WST.get(hold=True)
            (s_gt_, t_gt) = WST.get(hold=True)
            s_xb = s_xb_.rearrange("p (k c) -> p k c", c=256)
            st["slots"][n] = (s_gb_.rearrange("p (k c) -> p k c", c=256), t_gb,
                              s_gt_[:, 0:1024].rearrange("p (g k c) -> p g k c", g=2, k=2), t_gt)
            XC = XC2[:, n % 2]
            while n >= 2 and not ((n - 2, 0) in st["cdone"] and (n - 2, 1) in st["cdone"]):
                yield
            for ci in range(2):
                c = 2 * n + ci
                pb = bank()
                for k in range(8):
                    P.I("pe", "matmul", out=pb[:], lhsT=s_xb[:, k, ci * 128:(ci + 1) * 128], rhs=HT[:, k, :],
                        start=(k == 0), stop=(k == 7))
                P.I("pool", "tensor_copy", out=XBP[:, ci, 0:3], in_=CARRY[:, c, :])
                P.I("act", "activation", out=XBP[:, ci, 3:515], in_=pb[:], func=AF.Copy)
                P.I("act", "activation", out=XC[:, ci, :], in_=pb[:], func=AF.Identity, scale=LCOL[:, 3, c:c + 1],
                    bias=LCOL[:, 4, c:c + 1])
                yield
                P.I("pool", "tensor_copy", out=CARRY[:, c, :], in_=XBP[:, ci, 512:515])
                for j in range(3):
                    P.I("dve", "scalar_tensor_tensor", out=XC[:, ci, :], in0=XBP[:, ci, j:j + 512], scalar=LCOL[:, j, c:c + 1],
                        in1=XC[:, ci, :], op0=ALU.mult, op1=ALU.add)
                    yield
            WST.done(t_xb)
            while n >= 1 and st["gates"].get(n - 1, 0) < 2:
                yield
            P.I("dve", "tensor_copy", out=XCB[:], in_=XC[:])
            st["bs_done"].add(n)
            yield

        def chunk(n, oc):
            c = 2 * n + oc
            R_, I_, B_, H_, GB_, T1_ = CSET[c % 2]
            while n not in st["bs_done"]:
                yield
            while c >= 2 and ((c - 2) // 2, (c - 2) % 2) not in st["cdone"]:
                yield
            s_gb, t_gb, s_gt, t_gt = st["slots"][n]
            XC = XC2[:, n % 2]
            for g, dst, bj in ((0, R_, 9), (1, I_, 10)):
                pb = bank()
                for kc in range(2):
                    P.I("pe", "matmul", out=pb[:], lhsT=s_gt[:, g, kc, oc * 128:(oc + 1) * 128], rhs=XCB[:, kc, :],
                        start=(kc == 0), stop=(kc == 1))
                P.I("act", "activation", out=dst, in_=pb[:], func=AF.Tanh, scale=0.5, bias=LCOL[:, bj, c:c + 1])
            st["gates"][n] = st["gates"].get(n, 0) + 1
            if st["gates"][n] == 2:
                WST.done(t_gt)
            yield
            pb = bank()
            for k in range(8):
                P.I("pe", "matmul", out=pb[:], lhsT=s_gb[:, k, oc * 128:(oc + 1) * 128], rhs=HT[:, k, :],
                    start=(k == 0), stop=(k == 7))
            P.I("act", "activation", out=GB_, in_=pb[:], func=AF.Copy)
            P.I("act", "activation", out=T1_, in_=pb[:], func=AF.Square, scale=math.sqrt(0.044715))
            if oc == 1:
                WST.done(t_gb)
            yield
            P.I("act", "activation", out=B_, in_=R_, func=AF.Exp, scale=LCOL[:, 8, c:c + 1], bias=LCOL[:, 8, c:c + 1])
            yield
            P.I("act", "activation", out=R_, in_=R_, func=AF.Exp, scale=LCOL[:, 12, c:c + 1], bias=LCOL[:, 12, c:c + 1])
            yield
            sqq = st.setdefault(("sq", n), [])
            sqq.append(B_)
            if len(sqq) == 2:
                for bb in sqq:
                    P.I("act", "activation", out=bb, in_=bb, func=AF.Sqrt, scale=-1.0, bias=1.0)
                st[("sqdone", n)] = True
            while not st.get(("sqdone", n)):
                yield
            yield
            P.I("dve", "scalar_tensor_tensor", out=I_, in0=I_, scalar=1.0, in1=XC[:, oc, :], op0=ALU.add, op1=ALU.mult)
            yield
            P.I("dve", "scalar_tensor_tensor", out=B_, in0=B_, scalar=0.5, in1=I_, op0=ALU.mult, op1=ALU.mult)
            yield
            P.I("dve", "scalar_tensor_tensor", out=T1_, in0=T1_, scalar=1.0, in1=GB_, op0=ALU.add, op1=ALU.mult)
            yield
            P.I("act", "activation", out=T1_, in_=T1_, func=AF.Tanh, scale=0.7978845608028654)
            yield
            P.I("dve", "scalar_tensor_tensor", out=T1_, in0=T1_, scalar=1.0, in1=GB_, op0=ALU.add, op1=ALU.mult)
            yield
            P.I("dve", "tensor_tensor_scan", out=H_, data0=R_, data1=B_, initial=HST[:, c:c + 1], op0=ALU.mult, op1=ALU.add)
            yield
            P.I("dve", "tensor_copy", out=HST[:, c:c + 1], in_=H_[:, 511:512])
            P.I("dve", "scalar_tensor_tensor", out=YT[:, c, :], in0=H_, scalar=0.5, in1=T1_, op0=ALU.mult, op1=ALU.mult)
            st["cdone"].add((n, oc))
            yield

        gens = []
        for n in range(5):
            gens.append(lambda n=n: bstage(n))
            gens.append(lambda n=n: chunk(n, 0))
            gens.append(lambda n=n: chunk(n, 1))
        run_pipelined(gens, width=3)
        for h in range(2):
            banks4 = [bank() for _ in range(4)]
            for kp in range(3):
                slot = WST.get().rearrange("p (k c) -> p k c", c=512)
                for b in range(NB):
                    for kk in range(min(4, 10 - kp * 4)):
                        kc = kp * 4 + kk
                        P.I("pe", "matmul", out=banks4[b][:], lhsT=YT[:, kc, b * 128:(b + 1) * 128], rhs=slot[:, kk, :],
                            start=(kc == 0), stop=(kc == 9))
            residual(banks4, h, 0)

    def rope(dst, src, b, nh, eng_a, eng_b, tmp, tabs=None):
        if tabs is None:
            tabs = (COS[:, b:b + 1, :], SIN[:, b:b + 1, :], COS[:, b:b + 1, :], SIN[:, b:b + 1, :])
        c1, s2, c2, s1 = (tt.to_broadcast([128, nh, 32]) for tt in tabs)
        x1, x2 = src[:, :, 0:32], src[:, :, 32:64]
        P.I(eng_a, "tensor_tensor", out=tmp[:, :, 0:32], in0=x2, in1=s2, op=ALU.mult)
        P.I(eng_b, "tensor_tensor", out=dst[:, :, 0:32], in0=x1, in1=c1, op=ALU.mult)
        yield
        P.I(eng_a, "tensor_tensor", out=tmp[:, :, 32:64], in0=x1, in1=s1, op=ALU.mult)
        P.I(eng_b, "tensor_tensor", out=dst[:, :, 32:64], in0=x2, in1=c2, op=ALU.mult)
        yield
        P.I(eng_b, "tensor_tensor", out=dst[:, :, 0:32], in0=dst[:, :, 0:32], in1=tmp[:, :, 0:32], op=ALU.subtract)
        yield
        P.I(eng_b, "tensor_tensor", out=dst[:, :, 32:64], in0=dst[:, :, 32:64], in1=tmp[:, :, 32:64], op=ALU.add)
        yield

    def headnorm(q3, nh, st0, SQ):
        sq = SQ[:, 0:nh * 64].rearrange("p (h d) -> p h d", d=64)
        ss = STAT[:, st0:st0 + nh]
        P.I("act", "activation", out=sq, in_=q3, func=AF.Square)
        yield
        P.I("dve", "tensor_reduce", out=ss, in_=sq, axis=AX.X, op=ALU.add)
        yield
        P.I("act", "activation", out=ss, in_=ss, func=AF.Sqrt, scale=1.0 / 64, bias=EPS)
        yield
        P.I("dve", "reciprocal", out=ss, in_=ss)
        yield
        P.I("dve", "tensor_tensor", out=q3, in0=q3, in1=ss.unsqueeze(2).to_broadcast([128, nh, 64]), op=ALU.mult)
        yield

    def l1_mixer(t):
        rms_to_HT(1, 0)
        P.dma("sp", "rope0", COS[:], cos_d[t * T:(t + 1) * T, :].rearrange("(b p) f -> p b f", p=128))
        P.dma("sp", "rope1", SIN[:], sin_d[t * T:(t + 1) * T, :].rearrange("(b p) f -> p b f", p=128))
        for RT_, G_t in ((RTQ, QG), (RTK, KG)):
            g1 = G_t[:, 0:32].unsqueeze(1).to_broadcast([128, NB, 32])
            g2 = G_t[:, 32:64].unsqueeze(1).to_broadcast([128, NB, 32])
            P.I("pool", "tensor_tensor", out=RT_[:, :, 0, :], in0=COS[:], in1=g1, op=ALU.mult)
            P.I("pool", "tensor_tensor", out=RT_[:, :, 1, :], in0=SIN[:], in1=g2, op=ALU.mult)
            P.I("pool", "tensor_tensor", out=RT_[:, :, 2, :], in0=COS[:], in1=g2, op=ALU.mult)
            P.I("pool", "tensor_tensor", out=RT_[:, :, 3, :], in0=SIN[:], in1=g1, op=ALU.mult)
        QIR = BIG[:, 0:2048].rearrange("p (b f) -> p b f", f=512)
        jbanks = {}
        evac_count = {}

        def mm_j(j):
            w = min(512, B_IN - j * 512)
            banks4 = [bank() for _ in range(4)]
            jbanks[j] = banks4
            for kp in range(2):
                slot = WST.get().rearrange("p (k c) -> p k c", c=512)
                for b in range(NB):
                    for kk in range(4):
                        kc = kp * 4 + kk
                        P.I("pe", "matmul", out=banks4[b][:, 0:w], lhsT=HT[:, kc, b * 128:(b + 1) * 128], rhs=slot[:, kk, 0:w],
                            start=(kc == 0), stop=(kc == 7))

        def step(j, b):
            w = min(512, B_IN - j * 512)
            if b == 0:
                while j > 0 and evac_count[j - 1] < NB:
                    yield
                ps_rr[0] = 4
                mm_j(j)
                ps_rr[0] = 0
            while j not in jbanks:
                yield
            banks4 = jbanks[j]
            gb = t * NB + b
            par = (j * NB + b) % 2
            LX = L1T if par == 0 else L1U
            QF = LX[:, 0, :]
            QR = LX[:, 1, :]
            TM = LX[:, 2, :]
            SQ = LX[:, 3, :]
            P.I("act", "activation", out=QF[:, 0:w], in_=banks4[b][:, 0:w], func=AF.Copy)
            evac_count[j] = evac_count.get(j, 0) + 1
            yield
            if j < 2:
                q3 = QF.rearrange("p (h d) -> p h d", d=64)
                yield from headnorm(q3, 8, 24 + 8 * par, SQ)
                yield from rope(QR.rearrange("p (h d) -> p h d", d=64), q3, b, 8, "pool", "dve", TM.rearrange("p (h d) -> p h d", d=64),
                                tabs=tuple(RTQ[:, b:b + 1, i, :] for i in range(4)))
                pb = PS[par]
                for pr in range(4):
                    P.I("pe", "transpose", out=pb[:, pr * 128:(pr + 1) * 128], in_=QR[:, pr * 128:(pr + 1) * 128], identity=IDENT[:])
                yield
                P.I("act", "activation", out=QT[:, 4 * j:4 * j + 4, b * 128:(b + 1) * 128],
                    in_=pb[:].rearrange("p (a q) -> p a q", q=128), func=AF.Copy)
                yield
            elif j == 2:
                k3 = QF[:, 0:64].rearrange("p (h d) -> p h d", d=64)
                yield from headnorm(k3, 1, 40 + par, SQ)
                KR2 = QR[:, 0:128].rearrange("p (h d) -> p h d", d=64)
                yield from rope(KR2[:, 0:1, :], k3, b, 1, "pool", "dve", TM[:, 0:64].rearrange("p (h d) -> p h d", d=64),
                                tabs=tuple(RTK[:, b:b + 1, i, :] for i in range(4)))
                P.I("pool", "tensor_copy", out=KR2[:, 1:2, :], in_=KR2[:, 0:1, :])
                yield
                pb = PS[par]
                P.I("pe", "transpose", out=pb[:, 0:128], in_=QR[:, 0:128], identity=IDENT[:])
                yield
                P.I("act", "activation", out=KTE[0:64, gb * 128:(gb + 1) * 128], in_=pb[0:64, 0:128], func=AF.Copy)
                P.I("act", "activation", out=KTO[64:128, gb * 128:(gb + 1) * 128], in_=pb[64:128, 0:128], func=AF.Copy)
                P.I("pool", "tensor_copy", out=VW[:, gb, 64:128], in_=QF[:, 64:128])
                yield
                yield from rope(QIR[:, b, 0:384].rearrange("p (h d) -> p h d", d=64), QF[:, 128:512].rearrange("p (h d) -> p h d", d=64),
                                b, 6, "pool", "dve", TM[:, 128:512].rearrange("p (h d) -> p h d", d=64))
            else:
                yield from rope(QIR[:, b, 384:512].rearrange("p (h d) -> p h d", d=64), QF[:, 0:128].rearrange("p (h d) -> p h d", d=64),
                                b, 2, "pool", "dve", TM[:, 0:128].rearrange("p (h d) -> p h d", d=64))
                KI2 = QR[:, 0:128].rearrange("p (h d) -> p h d", d=64)
                yield from rope(KI2[:, 0:1, :], QF[:, 128:192].rearrange("p (h d) -> p h d", d=64), b, 1, "pool", "dve",
                                TM[:, 128:192].rearrange("p (h d) -> p h d", d=64))
                P.I("pool", "tensor_copy", out=KI2[:, 1:2, :], in_=KI2[:, 0:1, :])
                yield
                pb = PS[par]
                P.I("pe", "transpose", out=pb[:, 0:128], in_=QR[:, 0:128], identity=IDENT[:])
                yield
                P.I("act", "activation", out=KITE[0:64, gb * 128:(gb + 1) * 128], in_=pb[0:64, 0:128], func=AF.Copy)
                P.I("act", "activation", out=KITO[64:128, gb * 128:(gb + 1) * 128], in_=pb[64:128, 0:128], func=AF.Copy)
                yield
                wsc = (8 ** -0.5) / 8.0
                P.I("dve", "tensor_scalar", out=WIS[:, b, 0, :], in0=QF[:, 192:200], scalar1=wsc, scalar2=None, op0=ALU.mult)
                yield
                P.I("dve", "tensor_scalar", out=WIS[:, b, 2, :], in0=WIS[:, b, 0, :], scalar1=0.0, scalar2=2.0, op0=ALU.is_ge, op1=ALU.mult)
                yield
                P.I("dve", "tensor_scalar", out=WIS[:, b, 2, :], in0=WIS[:, b, 2, :], scalar1=-1.0, scalar2=None, op0=ALU.add)
                yield
                P.I("dve", "tensor_tensor", out=WIS[:, b, 1, :], in0=WIS[:, b, 0, :], in1=WIS[:, b, 2, :], op=ALU.mult)
                yield
                pb = PS[2 + par]
                for pr in range(4):
                    P.I("pe", "transpose", out=pb[:, pr * 128:(pr + 1) * 128], in_=QIR[:, b, pr * 128:(pr + 1) * 128], identity=IDENT[:])
                yield
                P.I("act", "activation", out=QIT[:, :, b * 128:(b + 1) * 128], in_=pb[:].rearrange("p (a q) -> p a q", q=128), func=AF.Copy)
                yield

        run_pipelined([(lambda j=j, b=b: step(j, b)) for j in range(4) for b in range(NB)], width=2)

        L1TF = L1T[:].rearrange("p a b -> p (a b)")
        L1UF = L1U[:].rearrange("p a b -> p (a b)")
        LO, HI, CNT, G_, MID = (BIS[:, i:i + 1] for i in range(5))
        SGN, TT_, LO2, HI2 = BIS[:, 5:6], BIS[:, 6:7], BIS[:, 7:8], BIS[:, 8 + NITER:9 + NITER]
        WK = BIS[:, 8:8 + NITER]

        def sc(buf, lo, hi):
            if buf == 0:
                return BIG[:, lo:hi]
            if hi <= 2048:
                return L1TF[:, lo:hi]
            assert lo >= 2048
            return L1UF[:, lo - 2048:hi - 2048]

        def idx_gen(b):
            gb = t * NB + b
            buf = gb % 2
            nk = (gb + 1) * 128
            nkc = (nk + 511) // 512
            P.I("dve", "tensor_tensor", out=DSG[:], in0=IDENTB[:].unsqueeze(1).to_broadcast([128, 8, 128]),
                in1=WIS[:, b, 2, :].unsqueeze(2).to_broadcast([128, 8, 128]), op=ALU.mult)
            yield
            ixs = [(kc, h) for kc in range(nkc) for h in range(8)]

            def emit_L(i):
                kc, h = ixs[i]
                w = min(512, nk - kc * 512)
                P.I("pe", "matmul", out=PS[i % 4][:, 0:w], lhsT=QIT[:, h // 2, b * 128:(b + 1) * 128],
                    rhs=(KITE if h % 2 == 0 else KITO)[:, kc * 512:kc * 512 + w], start=True, stop=True)

            for i in range(min(3, len(ixs))):
                emit_L(i)
            for i, (kc, h) in enumerate(ixs):
                w = min(512, nk - kc * 512)
                rl = RLB[:, i % 4, 0:w]
                if i % 3 != 2:
                    P.I("act", "activation", out=rl, in_=PS[i % 4][:, 0:w], func=AF.Relu, scale=WIS[:, b, 1, h:h + 1])
                else:
                    P.I("dve", "tensor_scalar", out=rl, in0=PS[i % 4][:, 0:w], scalar1=0.0, scalar2=WIS[:, b, 1, h:h + 1],
                        op0=ALU.max, op1=ALU.mult)
                scb = PS[4 + (kc % 2)]
                P.I("pe", "matmul", out=scb[:, 0:w], lhsT=DSG[:, h, :], rhs=rl, start=(h == 0), stop=(h == 7))
                if i + 3 < len(ixs):
                    emit_L(i + 3)
                if h == 7:
                    P.I("act", "activation", out=sc(buf, kc * 512, kc * 512 + w), in_=scb[:, 0:w], func=AF.Copy)
                yield

        def bisect_gen(b):
            gb = t * NB + b
            buf = gb % 2
            nk = (gb + 1) * 128
            split = buf == 1 and nk > 2048
            if nk > TOPK:
                if split:
                    P.I("dve", "tensor_reduce", out=LO, in_=sc(buf, 0, 2048), axis=AX.X, op=ALU.min)
                    P.I("dve", "tensor_reduce", out=LO2, in_=sc(buf, 2048, nk), axis=AX.X, op=ALU.min)
                    yield
                    P.I("dve", "tensor_tensor", out=LO, in0=LO, in1=LO2, op=ALU.min)
                else:
                    P.I("dve", "tensor_reduce", out=LO, in_=sc(buf, 0, nk), axis=AX.X, op=ALU.min)
                yield
            dg = sc(buf, gb * 128, (gb + 1) * 128)
            P.I("dve", "tensor_tensor", out=dg, in0=dg, in1=CAUS[:], op=ALU.add)
            yield
            if nk > TOPK:
                if split:
                    P.I("dve", "tensor_reduce", out=HI, in_=sc(buf, 0, 2048), axis=AX.X, op=ALU.max)
                    P.I("dve", "tensor_reduce", out=HI2, in_=sc(buf, 2048, nk), axis=AX.X, op=ALU.max)
                    yield
                    P.I("dve", "tensor_tensor", out=HI, in0=HI, in1=HI2, op=ALU.max)
                else:
                    P.I("dve", "tensor_reduce", out=HI, in_=sc(buf, 0, nk), axis=AX.X, op=ALU.max)
                yield
                P.I("dve", "tensor_tensor", out=HI, in0=HI, in1=LO, op=ALU.subtract)
                yield
                P.I("dve", "tensor_tensor", out=WK, in0=PW2[:], in1=HI.to_broadcast([128, NITER]), op=ALU.mult)
                yield
                n1 = 2048 if split else max(128, (int(nk * 0.45) // 128) * 128)
                n2 = nk - n1
                P.I("dve", "tensor_tensor", out=MID, in0=LO, in1=WK[:, 0:1], op=ALU.add)
                yield
                thr_c = float(2 * TOPK - n2)
                for it in range(NITER):
                    P.I("act", "activation", out=JUNK[:, n1:nk], in_=sc(buf, n1, nk), func=AF.Sign, scale=-1.0, bias=MID, accum_out=SGN)
                    P.I("dve", "tensor_scalar", out=JUNK[:, 0:n1], in0=sc(buf, 0, n1), scalar1=MID, scalar2=None, op0=ALU.is_ge,
                        op1=ALU.add, accum_out=CNT)
                    yield
                    yield
                    P.I("dve", "scalar_tensor_tensor", out=TT_, in0=CNT, scalar=2.0, in1=SGN, op0=ALU.mult, op1=ALU.subtract)
                    yield
                    if it + 1 < NITER:
                        P.I("dve", "scalar_tensor_tensor", out=G_, in0=TT_, scalar=thr_c, in1=WK[:, it:it + 1], op0=ALU.is_ge, op1=ALU.mult)
                        yield
                        P.I("dve", "scalar_tensor_tensor", out=MID, in0=MID, scalar=WK[:, it + 1:it + 2], in1=G_, op0=ALU.subtract, op1=ALU.add)
                        yield
                    else:
                        P.I("dve", "scalar_tensor_tensor", out=G_, in0=TT_, scalar=thr_c, in1=WK[:, it:it + 1], op0=ALU.is_lt, op1=ALU.mult)
                        yield
                        P.I("dve", "tensor_tensor", out=LO, in0=MID, in1=G_, op=ALU.subtract)
                        yield
            else:
                P.I("dve", "memset", ap=LO, constant=-1.0e29)
                yield

        def idx_len(b):
            nk = (t * NB + b + 1) * 128
            return 8 * ((nk + 511) // 512) + 1

        def bisect_len(b):
            nk = (t * NB + b + 1) * 128
            return (6 * NITER + 8) if nk > TOPK else 2

        def mask_build(b):
            gb = t * NB + b
            buf = gb % 2
            nk = (gb + 1) * 128
            nkc = (nk + 511) // 512
            for kc in range(nkc):
                w = min(512, nk - kc * 512)
                mk = L0X[:, (kc % 2) * 512:(kc % 2) * 512 + w]
                P.I("dve", "tensor_scalar", out=mk, in0=sc(buf, kc * 512, kc * 512 + w), scalar1=LO, scalar2=None, op0=ALU.is_ge)
                pb = PS[2 + (kc % 2)]
                for q in range(w // 128):
                    P.I("pe", "transpose", out=pb[:, q * 128:(q + 1) * 128], in_=mk[:, q * 128:(q + 1) * 128], identity=IDENT[:])
                P.I("act", "activation", out=MASKT[:, kc * 4:kc * 4 + w // 128, :], in_=pb[:, 0:w].rearrange("p (a q) -> p a q", q=128), func=AF.Copy)

        def attention(b):
            gb = t * NB + b
            ACC = PS[4:8]
            its = [(kt, g) for kt in range(gb + 1) for g in range(4)]
            DEPTH = 3

            def emit_S(i):
                kt, g = its[i]
                half, pairset = g // 2, g % 2
                P.I("pe", "matmul", out=PS[i % 4][:], lhsT=(KTE if half == 0 else KTO)[:, kt * 128:(kt + 1) * 128],
                    rhs=QT[:, pairset * 4:pairset * 4 + 4, b * 128:(b + 1) * 128], start=True, stop=True)

            for i in range(min(DEPTH, len(its))):
                emit_S(i)
            for i, (kt, g) in enumerate(its):
                eb = EB[:, i % 4, :]
                pt = PT[:, i % 4, :]
                P.I("act", "activation", out=eb, in_=PS[i % 4][:], func=AF.Exp, scale=0.125)
                P.I("dve", "tensor_tensor", out=pt.rearrange("p (a q) -> p a q", q=128), in0=eb.rearrange("p (a q) -> p a q", q=128),
                    in1=MASKT[:, kt:kt + 1, :].to_broadcast([128, 4, 128]), op=ALU.mult)
                if g // 2 == 0:
                    P.I("pe", "matmul", out=ACC[g][0:65, :], lhsT=VW[:, kt, 64:129], rhs=pt, start=(kt == 0), stop=(kt == gb))
                else:
                    P.I("pe", "matmul", out=ACC[g][:], lhsT=VW[:, kt, 0:128], rhs=pt, start=(kt == 0), stop=(kt == gb))
                if i + DEPTH < len(its):
                    emit_S(i + DEPTH)
            for g in range(4):
                half, pairset = g // 2, g % 2
                lr = 64 if half == 0 else 0
                p0, p1 = (0, 64) if half == 0 else (64, 128)
                LR_ = LROW if g % 2 == 0 else LROWB
                RB_ = RB if g % 2 == 0 else RBB
                P.I("act", "activation", out=LR_[lr:lr + 1, :], in_=ACC[g][lr:lr + 1, :], func=AF.Copy)
                pb = PS[g % 4]
                P.I("pe", "matmul", out=pb[:], lhsT=(SEL if half == 0 else SEL2)[:], rhs=LR_[:], start=True, stop=True)
                P.I("dve", "reciprocal", out=RB_[p0:p1, :], in_=pb[p0:p1, :])
                P.I("dve", "tensor_tensor", out=OUTT2[p0:p1, pairset * 4:(pairset + 1) * 4, b * 128:(b + 1) * 128],
                    in0=ACC[g][p0:p1, :].rearrange("p (a q) -> p a q", q=128), in1=RB_[p0:p1, :].rearrange("p (a q) -> p a q", q=128), op=ALU.mult)

        def run_weighted(ga, na, gb_, nb):
            ia = ib = 0
            a_done = b_done = False
            while not (a_done and b_done):
                if b_done or (not a_done and ia * nb <= ib * na):
                    try:
                        next(ga)
                        ia += 1
                    except StopIteration:
                        a_done = True
                else:
                    try:
                        next(gb_)
                        ib += 1
                    except StopIteration:
                        b_done = True

        for _ in idx_gen(0):
            pass
        for b in range(NB):
            if b + 1 < NB:
                run_weighted(bisect_gen(b), bisect_len(b), idx_gen(b + 1), idx_len(b + 1))
            else:
                for _ in bisect_gen(b):
                    pass
            mask_build(b)
            attention(b)
        for h in range(2):
            banks4 = [bank() for _ in range(4)]
            for kp in range(2):
                slot = WST.get().rearrange("p (k c) -> p k c", c=512)
                for b in range(NB):
                    for kk in range(4):
                        hs = kp * 4 + kk
                        P.I("pe", "matmul", out=banks4[b][:], lhsT=OUTT2[:, hs, b * 128:(b + 1) * 128], rhs=slot[:, kk, :],
                            start=(hs == 0), stop=(hs == 7))
            residual(banks4, h, 2)

    for t in range(NT):
        for b in range(NB):
            P.dma("sp", "xin%d" % b, XT[:, b, :], x_d[t * T + b * 128:t * T + (b + 1) * 128, :])
        l0_mixer()
        if stop_after != "l0mix":
            ffn(0)
            if stop_after != "l0":
                l1_mixer(t)
                if stop_after != "l1mix":
                    ffn(1)
        for b in range(NB):
            P.dma("sp", "xout%d" % b, out_d[t * T + b * 128:t * T + (b + 1) * 128, :], XT[:, b, :])
    P.fence("sp", [out_d[0:NT * T, :]])
    if stop_after is not None:
        pass
    P.emit()
    es.close()
    return nc


def host_consts():
    ident = np.eye(128, dtype=np.float32)
    q = np.arange(128)[:, None]
    k = np.arange(128)[None, :]
    caus = np.where(k <= q, 0.0, NEG).astype(np.float32)
    inv = (np.float32(10000.0) ** (-(np.arange(0, 64, 2, dtype=np.float32)) / np.float32(64))).astype(np.float32)
    ang = (np.arange(S_FULL, dtype=np.float32)[:, None] * inv[None, :]).astype(np.float32)
    cosT = np.cos(ang).astype(np.float32)
    sinT = np.sin(ang).astype(np.float32)
    pw2 = np.tile((0.5 ** np.arange(1, NITER + 1)).astype(np.float32)[None, :], (128, 1))
    return {"ident": ident, "caus": caus, "cosT": cosT, "sinT": sinT, "pw2": pw2}


def make_in_map(inputs, b, consts):
    m = dict(consts)
    m["x"] = np.ascontiguousarray(inputs["x"][b])
    m["ccol"] = np.ascontiguousarray(inputs["c"][b].reshape(8, 128).T)
    for nm in ("norm_mix_g", "norm_ffn_g", "ada_w", "ada_b"):
        m[nm] = np.ascontiguousarray(inputs[nm])
    for nm in ("a_w_in", "a_conv_w", "a_gate_r_w", "a_gate_i_w", "a_w_out", "b_w_in", "b_w_out"):
        m[nm] = np.ascontiguousarray(inputs[nm][0])
    for nm in ("a_conv_b", "a_gate_r_b", "a_gate_i_b", "a_lambda", "b_q_norm_g", "b_k_norm_g"):
        m[nm] = np.ascontiguousarray(inputs[nm])
    for l in range(2):
        m["ffn_w1_%d" % l] = np.ascontiguousarray(inputs["ffn_w1"][l])
        m["ffn_w2_%d" % l] = np.ascontiguousarray(inputs["ffn_w2"][l])
    return m


_NC_CACHE = {}


def kernel(**inputs):
    inputs = {k: np.asarray(v) for k, v in inputs.items()}
    if "full" not in _NC_CACHE:
        _NC_CACHE["full"] = build_nc(S_FULL)
    nc = _NC_CACHE["full"]
    consts = host_consts()
    nb = inputs["x"].shape[0]
    in_maps = [make_in_map(inputs, b, consts) for b in range(nb)]
    res = run_bass_kernel_spmd(nc, in_maps, core_ids=list(range(nb)))
    out = np.stack([np.asarray(r["out"]) for r in res.results], axis=0)
    return out.astype(np.float32)
```

```python
import math
from contextlib import ExitStack

import numpy as np
import concourse.bass as bass
import concourse.mybir as mybir
from concourse.bass_utils import run_bass_kernel_spmd

F32 = mybir.dt.float32
BF16 = mybir.dt.bfloat16
ALU = mybir.AluOpType
AF = mybir.ActivationFunctionType
AX = mybir.AxisListType

S_FULL = 4096
D = 1024
T = 512
NB = 4
DFF = 4096
DRNN = 1280
B_IN = 1736
NSLOT = 8
PIECE = 2048
NITER = 10
TOPK = 256
EPS = 1e-6
NEG = -1.0e30

_ESZ = {F32: 4, BF16: 2}
_WKEYS = ("out", "accum_out", "ap")


class _Op:
    __slots__ = ("stream", "chan", "fn", "is_dma", "cpos", "sig", "waits", "idx")


def _is_ap(v):
    return hasattr(v, "ap") and hasattr(v, "tensor") and hasattr(v, "offset")


class Prog:
    STREAMS = ("pe", "act", "dve", "pool", "sp")

    def __init__(self, nc):
        self.nc = nc
        self.ops = []
        self.stream_ops = {s: [] for s in self.STREAMS}
        self.chan_count = {}
        self.chan_last = {}
        self.track = {}
        self.waited = {s: {} for s in self.STREAMS}
        self.chan_ops = {}

    def region(self, ap):
        name = ap.tensor.name
        esz = _ESZ.get(ap.dtype, 4)
        dims = ap.ap
        off = ap.offset
        space = str(ap.space)
        if "PSUM" in space:
            return (name, 0, 128, 0, 1 << 30, True)
        if "SB" in space.upper():
            pstep, npart = dims[0]
            if pstep == 0:
                p0, free0 = 0, off
                p1 = 128
            else:
                p0 = off // pstep
                free0 = off % pstep
                p1 = p0 + npart
            ext = 0
            for st, n in dims[1:]:
                ext += abs(st) * (n - 1)
            return (name, p0, p1, free0 * esz, (free0 + ext + 1) * esz, False)
        ext = 0
        for st, n in dims:
            ext += abs(st) * (n - 1)
        return (name, 0, 1, off * esz, (off + ext + 1) * esz, False)

    def add(self, stream, fn, reads, writes, chan=None):
        op = _Op()
        op.idx = len(self.ops)
        op.stream = stream
        op.is_dma = chan is not None
        op.chan = chan if chan is not None else stream
        op.fn = fn
        op.sig = op.is_dma
        op.waits = []
        self.chan_count[op.chan] = self.chan_count.get(op.chan, 0) + 1
        op.cpos = self.chan_count[op.chan]
        deps = {}

        def add_dep(pidx):
            p = self.ops[pidx]
            if deps.get(p.chan, (0, None))[0] < p.cpos:
                deps[p.chan] = (p.cpos, p)

        if op.is_dma and chan in self.chan_last:
            add_dep(self.chan_last[chan])
        for real_w, regs in ((False, reads), (True, writes)):
            for reg in regs:
                name, p0, p1, lo, hi, psum = reg
                eff_w = real_w or psum
                for ent in self.track.get(name, ()):
                    ep0, ep1, elo, ehi, eidx, e_eff, e_real = ent
                    overlap = ep0 < p1 and p0 < ep1 and elo < hi and lo < ehi
                    if overlap and (eff_w or e_eff):
                        prod = self.ops[eidx]
                        same = (not prod.is_dma) and (not op.is_dma) and prod.stream == stream
                        if same:
                            if stream != "pe" and (e_real or real_w):
                                add_dep(eidx)
                        else:
                            add_dep(eidx)
        for real_w, regs in ((False, reads), (True, writes)):
            for reg in regs:
                name, p0, p1, lo, hi, psum = reg
                eff_w = real_w or psum
                lst = self.track.get(name, [])
                new = []
                for ent in lst:
                    ep0, ep1, elo, ehi, eidx, e_eff, e_real = ent
                    covered = p0 <= ep0 and ep1 <= p1 and lo <= elo and ehi <= hi
                    if eidx == op.idx:
                        if covered:
                            continue
                        new.append(ent)
                        continue
                    if covered and eff_w:
                        continue
                    if covered and (not e_eff) and (not eff_w):
                        prod = self.ops[eidx]
                        if (not prod.is_dma) and (not op.is_dma) and prod.stream == stream:
                            continue
                    new.append(ent)
                new.append((p0, p1, lo, hi, op.idx, eff_w, real_w))
                self.track[name] = new
        w = self.waited[stream]
        for ch, (cpos, prod) in deps.items():
            if w.get(ch, 0) >= cpos:
                continue
            w[ch] = cpos
            prod.sig = True
            op.waits.append(prod)
        if not op.is_dma:
            pass
        self.ops.append(op)
        self.stream_ops[stream].append(op)
        self.chan_ops.setdefault(op.chan, []).append(op)
        if op.is_dma:
            self.chan_last[chan] = op.idx
        return op

    def I(self, stream, method, chan=None, xr=(), xw=(), **kw):
        reads, writes = [], []
        for k, v in kw.items():
            if _is_ap(v):
                (writes if k in _WKEYS else reads).append(self.region(v))
        for v in xr:
            reads.append(self.region(v))
        for v in xw:
            writes.append(self.region(v))

        def fn(e, method=method, kw=kw):
            return getattr(e, method)(**kw)

        return self.add(stream, fn, reads, writes, chan=chan)

    def dma(self, stream, chan, out, in_, slow=False):
        if slow:
            return self.I(stream, "dma_start", chan=chan, out=out, in_=in_, allow_slow_non_contiguous=True)
        return self.I(stream, "dma_start", chan=chan, out=out, in_=in_)

    def fence(self, stream, aps):
        return self.add(stream, None, [self.region(a) for a in aps], [])

    def emit(self):
        nc = self.nc
        sigval = {}
        for ch, ops in self.chan_ops.items():
            c = 0
            for op in ops:
                if op.sig:
                    c += 16 if op.is_dma else 1
                sigval[op.idx] = c
        with ExitStack() as es:
            sems = {}
            for ch in self.chan_ops:
                sems[ch] = es.enter_context(nc.semaphore("s_" + ch))
            block = es.enter_context(nc.Block())

            def run(stream, e):
                for op in self.stream_ops[stream]:
                    for prod in op.waits:
                        e.wait_ge(sems[prod.chan], sigval[prod.idx])
                    if op.fn is None:
                        continue
                    ins = op.fn(e)
                    if op.sig:
                        ins.then_inc(sems[op.chan], 16 if op.is_dma else 1)

            @block.tensor
            def _(e):
                run("pe", e)

            @block.scalar
            def _(e):
                run("act", e)

            @block.vector
            def _(e):
                run("dve", e)

            @block.gpsimd
            def _(e):
                run("pool", e)

            @block.sync
            def _(e):
                run("sp", e)


def piece_list(nc_in):
    W = nc_in
    pieces = []

    def kview(w):
        return w.rearrange("(k p) c -> p k c", p=128)

    a_w_in = kview(W["a_w_in"])
    for n in range(5):
        pieces.append(("a_xb%d" % n, [(128, 8, 256, a_w_in[:, :, n * 256:(n + 1) * 256], 0, 256)]))
        pieces.append(("a_gb%d" % n, [(128, 8, 256, a_w_in[:, :, DRNN + n * 256:DRNN + (n + 1) * 256], 0, 256)]))
        gr = W["a_gate_r_w"][n].rearrange("(k p) c -> p k c", p=128)
        gi = W["a_gate_i_w"][n].rearrange("(k p) c -> p k c", p=128)
        pieces.append(("a_gt%d" % n, [(128, 2, 256, gr, 0, 256), (128, 2, 256, gi, 512, 256)]))
    a_w_out = kview(W["a_w_out"])
    for h in range(2):
        for kp in range(3):
            k0, k1 = kp * 4, min(kp * 4 + 4, 10)
            pieces.append(("a_wo%d_%d" % (h, kp), [(128, k1 - k0, 512, a_w_out[:, k0:k1, h * 512:(h + 1) * 512], 0, 512)]))

    def ffn(l):
        w1 = kview(W["ffn_w1_%d" % l])
        w2 = kview(W["ffn_w2_%d" % l])
        for pc in range(16):
            pieces.append(("f%d_w1_%d" % (l, pc), [(128, 8, 256, w1[:, :, pc * 256:(pc + 1) * 256], 0, 256)]))
        for h in range(2):
            for kp in range(8):
                pieces.append(("f%d_w2_%d_%d" % (l, h, kp), [(128, 4, 512, w2[:, kp * 4:(kp + 1) * 4, h * 512:(h + 1) * 512], 0, 512)]))

    ffn(0)
    b_w_in = kview(W["b_w_in"])
    for j in range(4):
        w = min(512, B_IN - j * 512)
        for kp in range(2):
            pieces.append(("b_wi%d_%d" % (j, kp), [(128, 4, w, b_w_in[:, kp * 4:(kp + 1) * 4, j * 512:j * 512 + w], 0, 512)]))
    b_w_out = kview(W["b_w_out"])
    for h in range(2):
        for kp in range(2):
            pieces.append(("b_wo%d_%d" % (h, kp), [(128, 4, 512, b_w_out[:, kp * 4:(kp + 1) * 4, h * 512:(h + 1) * 512], 0, 512)]))
    ffn(1)
    return pieces


def build_nc(S_run=S_FULL, debug=None):
    NT = S_run // T
    nc = bass.Bass("TRN2", target_bir_lowering=False)
    dbg = debug or {}

    def din(name, shape, dt=F32):
        return nc.dram_tensor(name, list(shape), dt, kind="ExternalInput").ap()

    x_d = din("x", [S_FULL, D])
    ccol_d = din("ccol", [128, 8])
    ident_d = din("ident", [128, 128])
    caus_d = din("caus", [128, 128])
    cos_d = din("cosT", [S_FULL, 32])
    sin_d = din("sinT", [S_FULL, 32])
    pw2_d = din("pw2", [128, NITER])
    W = {}
    W["norm_mix_g"] = din("norm_mix_g", [2, D])
    W["norm_ffn_g"] = din("norm_ffn_g", [2, D])
    W["ada_w"] = din("ada_w", [2, D, 6 * D])
    W["ada_b"] = din("ada_b", [2, 6 * D])
    W["a_w_in"] = din("a_w_in", [D, 2 * DRNN])
    W["a_conv_w"] = din("a_conv_w", [4, DRNN])
    W["a_conv_b"] = din("a_conv_b", [1, DRNN])
    W["a_gate_r_w"] = din("a_gate_r_w", [5, 256, 256])
    W["a_gate_r_b"] = din("a_gate_r_b", [1, DRNN])
    W["a_gate_i_w"] = din("a_gate_i_w", [5, 256, 256])
    W["a_gate_i_b"] = din("a_gate_i_b", [1, DRNN])
    W["a_lambda"] = din("a_lambda", [1, DRNN])
    W["a_w_out"] = din("a_w_out", [DRNN, D])
    W["b_w_in"] = din("b_w_in", [D, B_IN])
    W["b_q_norm_g"] = din("b_q_norm_g", [1, 64])
    W["b_k_norm_g"] = din("b_k_norm_g", [1, 64])
    W["b_w_out"] = din("b_w_out", [D, D])
    for l in range(2):
        W["ffn_w1_%d" % l] = din("ffn_w1_%d" % l, [D, DFF])
        W["ffn_w2_%d" % l] = din("ffn_w2_%d" % l, [DFF, D])
    out_d = nc.dram_tensor("out", [S_FULL, D], F32, kind="ExternalOutput").ap()

    pieces = piece_list(W)
    NP_ = len(pieces)
    pidx = {p[0]: i for i, p in enumerate(pieces)}
    tape = nc.dram_tensor("tape", [NP_, 128, PIECE], BF16, kind="Internal").ap()

    dbg_out = {}

    def dbg_tensor(name, shape):
        dbg_out[name] = nc.dram_tensor("dbg_" + name, list(shape), F32, kind="ExternalOutput").ap()
        return dbg_out[name]

    P = Prog(nc)
    es = ExitStack()

    def sb(name, shape, dt=F32):
        return es.enter_context(nc.sbuf_tensor(name, list(shape), dt))

    XT = sb("XT", [128, NB, D])
    HT = sb("HT", [128, 8, T], BF16)
    WS = sb("WS", [128, NSLOT, PIECE], BF16)
    BIG = sb("BIG", [128, 4224])
    ARENA = sb("ARENA", [128, 8192])
    KTE = sb("KTE", [128, S_FULL], BF16)
    KTO = sb("KTO", [128, S_FULL], BF16)
    KITE = sb("KITE", [128, S_FULL], BF16)
    KITO = sb("KITO", [128, S_FULL], BF16)
    VW = sb("VW", [128, 32, 130], BF16)
    COS = sb("COS", [128, NB, 32])
    SIN = sb("SIN", [128, NB, 32])
    IDENT = sb("IDENT", [128, 128])
    CAUS = sb("CAUS", [128, 128])
    ONES = sb("ONES", [128, 128])
    PW2 = sb("PW2", [128, NITER])
    FCOL = sb("FCOL", [128, 2, 4, 8])
    GCOL = sb("GCOL", [128, 2, 2, 8])
    CCOL = sb("CCOL", [128, 8])
    CONDB = sb("CONDB", [128, 8, 128])
    LCOL = sb("LCOL", [128, 13, 10])
    CARRY = sb("CARRY", [128, 10, 3])
    HST = sb("HST", [128, 10])
    QG = sb("QG", [128, 64])
    KG = sb("KG", [128, 64])
    STAT = sb("STAT", [128, 64])
    BIS = sb("BIS", [128, 10 + NITER])
    L1T = sb("L1T", [128, 4, 512])
    L1U = sb("L1U", [128, 4, 512])
    _g01 = L1T[:].rearrange("p a b -> p (a b)")
    _g23 = L1U[:].rearrange("p a b -> p (a b)")
    GATEV = [_g01[:, 0:1024], _g01[:, 1024:2048], _g23[:, 0:1024], _g23[:, 1024:2048]]
    EB = sb("EB", [128, 4, 512], BF16)
    PT = sb("PT", [128, 4, 512], BF16)
    SEL = sb("SEL", [128, 128])
    RTQ = sb("RTQ", [128, NB, 4, 32])
    RTK = sb("RTK", [128, NB, 4, 32])
    L0X = sb("L0X", [128, 1024])
    SEL2 = sb("SEL2", [128, 128])
    DRT = sb("DRT", [128, NB, 128])
    IDENTB = sb("IDENTB", [128, 128], BF16)
    DSG = sb("DSG", [128, 8, 128], BF16)
    RLB = sb("RLB", [128, 4, 512], BF16)
    QIT = sb("QIT", [128, 4, T], BF16)
    WIS = sb("WIS", [128, NB, 3, 8])
    LROW = sb("LROW", [128, 512])
    RB = sb("RB", [128, 512])
    LROWB = sb("LROWB", [128, 512])
    RBB = sb("RBB", [128, 512])

    PS = [es.enter_context(nc.psum_tensor("PS%d" % i, [128, 512], F32)) for i in range(8)]
    ps_rr = [0]

    def bank():
        b = PS[ps_rr[0] % 8]
        ps_rr[0] += 1
        return b

    ARENA_bf = ARENA[:].bitcast(BF16)
    H1T = ARENA_bf.rearrange("p (c t) -> p c t", t=T)
    YT = H1T[:, 0:10, :]
    QT = H1T[:, 0:8, :]
    OUTT2 = H1T[:, 8:16, :]
    MASKT = ARENA_bf[:, 24 * 512:32 * 512].rearrange("p (k q) -> p k q", q=128)
    JUNK = ARENA_bf[:, 24 * 512:32 * 512]

    rr = {"dve_pool": 0}

    def alt(*names):
        i = rr.get(names, 0)
        rr[names] = i + 1
        return names[i % len(names)]

    cch = [0]

    def cdma(out, in_, slow=False):
        ch = "c%d" % (cch[0] % 6)
        cch[0] += 1
        P.dma("sp", ch, out, in_, slow=slow)

    cdma(IDENT[:], ident_d)
    cdma(CAUS[:], caus_d)
    cdma(PW2[:], pw2_d)
    cdma(CCOL[:], ccol_d)
    P.I("pool", "memset", ap=ONES[:], constant=1.0)
    P.I("pool", "memset", ap=CARRY[:], constant=0.0)
    P.I("pool", "memset", ap=HST[:], constant=0.0)
    P.I("pool", "memset", ap=VW[:], constant=0.0)
    P.I("pool", "memset", ap=VW[:, :, 0:1], constant=1.0)
    P.I("pool", "memset", ap=VW[:, :, 128:129], constant=1.0)
    P.I("pool", "memset", ap=SEL2[:], constant=0.0)
    P.I("pool", "memset", ap=SEL2[0:1, :], constant=1.0)
    for tz in (KTE, KTO, KITE, KITO):
        P.I("pool", "memset", ap=tz[:], constant=0.0)
    P.I("pool", "memset", ap=SEL[:], constant=0.0)
    P.I("pool", "memset", ap=SEL[64:65, :], constant=1.0)
    P.I("pool", "memset", ap=LROW[:], constant=0.0)
    P.I("pool", "memset", ap=LROWB[:], constant=0.0)
    P.I("dve", "tensor_copy", out=IDENTB[:], in_=IDENT[:])
    for j in range(4):
        cdma(LCOL[:, j, :], W["a_conv_w"][j].rearrange("(c p) -> p c", p=128), slow=True)
    for j, nm in enumerate(["a_conv_b", "a_gate_r_b", "a_gate_i_b", "a_lambda"]):
        cdma(LCOL[:, 4 + j, :], W[nm][0].rearrange("(c p) -> p c", p=128), slow=True)
    for l in range(2):
        cdma(GCOL[:, l, 0, :], W["norm_mix_g"][l].rearrange("(c p) -> p c", p=128), slow=True)
        cdma(GCOL[:, l, 1, :], W["norm_ffn_g"][l].rearrange("(c p) -> p c", p=128), slow=True)
    cdma(QG[:], W["b_q_norm_g"][0:1, :].partition_broadcast(128))
    cdma(KG[:], W["b_k_norm_g"][0:1, :].partition_broadcast(128))

    P.I("act", "activation", out=LCOL[:, 8, :], in_=LCOL[:, 7, :], func=AF.Exp, scale=-1.0)
    P.I("act", "activation", out=LCOL[:, 8, :], in_=LCOL[:, 8, :], func=AF.Ln, bias=1.0)
    P.I("dve", "tensor_scalar", out=LCOL[:, 8, :], in0=LCOL[:, 8, :], scalar1=-8.0, scalar2=None, op0=ALU.mult)
    P.I("dve", "tensor_scalar", out=LCOL[:, 9, :], in0=LCOL[:, 5, :], scalar1=0.5, scalar2=None, op0=ALU.mult)
    P.I("dve", "tensor_scalar", out=LCOL[:, 10, :], in0=LCOL[:, 6, :], scalar1=0.5, scalar2=None, op0=ALU.mult)
    P.I("dve", "tensor_scalar", out=LCOL[:, 11, :], in0=LCOL[:, 8, :], scalar1=2.0, scalar2=None, op0=ALU.mult)
    P.I("pool", "memset", ap=STAT[:, 48:49], constant=0.5)
    P.I("dve", "tensor_scalar", out=LCOL[:, 12, :], in0=LCOL[:, 8, :], scalar1=0.5, scalar2=None, op0=ALU.mult)

    P.I("act", "activation", out=STAT[:, 0:8], in_=CCOL[:], func=AF.Sigmoid)
    P.I("dve", "tensor_tensor", out=CCOL[:], in0=CCOL[:], in1=STAT[:, 0:8], op=ALU.mult)
    P.I("dve", "tensor_copy", out=CONDB[:], in_=CCOL[:].unsqueeze(2).to_broadcast([128, 8, 128]))

    XTF0 = XT[:].rearrange("p a b -> p (a b)")
    STGA = [BIG[:, 0:2048], BIG[:, 2048:4096], XTF0[:, 0:2048], XTF0[:, 2048:4096]]
    MODP = ARENA[:, 0:2048]
    ADAB = ARENA[:, 2048:4096]
    ada_steps = [(l, cp, k) for l in range(2) for cp in range(3) for k in range(8)]

    def ada_load(i):
        l, cp, k = ada_steps[i]
        P.dma("sp", "stg%d" % (i % 4), STGA[i % 4], W["ada_w"][l, k * 128:(k + 1) * 128, cp * 2048:(cp + 1) * 2048])

    for i in range(3):
        ada_load(i)
    ada_i = [0]
    for l in range(2):
        for cp in range(3):
            cdma(ADAB, W["ada_b"][l:l + 1, cp * 2048:(cp + 1) * 2048].partition_broadcast(128))
            banks = [bank() for _ in range(4)]
            for k in range(8):
                i = ada_i[0]
                ada_i[0] += 1
                stg = STGA[i % 4]
                if i + 3 < len(ada_steps):
                    ada_load(i + 3)
                for q in range(4):
                    P.I("pe", "matmul", out=banks[q][:], lhsT=CONDB[:, k, :], rhs=stg[:, q * 512:(q + 1) * 512],
                        start=(k == 0), stop=(k == 7))
            for q in range(4):
                P.I("dve", "tensor_tensor", out=MODP[:, q * 512:(q + 1) * 512], in0=banks[q][:],
                    in1=ADAB[:, q * 512:(q + 1) * 512], op=ALU.add)
            for sgi in range(2):
                seg = 2 * cp + sgi
                src = MODP[:, sgi * 1024:(sgi + 1) * 1024]
                if seg == 2:
                    P.I("pool", "tensor_copy", out=GATEV[2 * l + 0], in_=src)
                elif seg == 5:
                    P.I("pool", "tensor_copy", out=GATEV[2 * l + 1], in_=src)
                else:
                    slot = {0: 1, 1: 0, 3: 3, 4: 2}[seg]
                    b = bank()
                    for c in range(8):
                        P.I("pe", "transpose", out=b[:, c * 8:c * 8 + 8], in_=MODP[0:8, sgi * 1024 + c * 128:sgi * 1024 + (c + 1) * 128],
                            identity=IDENT[0:8, 0:8])
                    P.I("act", "activation", out=FCOL[:, l, slot, :], in_=b[:, 0:64].rearrange("p (c j) -> p c j", j=8)[:, :, 0],
                        func=AF.Identity)
        for (a_i, g_i) in ((0, 0), (2, 1)):
            P.I("dve", "tensor_scalar", out=FCOL[:, l, a_i, :], in0=FCOL[:, l, a_i, :], scalar1=1.0, scalar2=None, op0=ALU.add)
            P.I("dve", "tensor_tensor", out=FCOL[:, l, a_i, :], in0=FCOL[:, l, a_i, :], in1=GCOL[:, l, g_i, :], op=ALU.mult)

    CB = [ARENA_bf[:, 8192 + i * PIECE:8192 + (i + 1) * PIECE] for i in range(4)]
    XTF = XT[:].rearrange("p a b -> p (a b)")
    STG4 = [BIG[:, 0:2048], BIG[:, 2048:4096], XTF[:, 0:2048], XTF[:, 2048:4096]]
    def tape_load(i):
        nm, parts = pieces[i]
        stg = STG4[i % 4]
        for (npart, kk_, cc_, src, off, cst) in parts:
            dst = stg[0:npart, off:off + kk_ * cst].rearrange("p (k c) -> p k c", c=cst)[:, :, 0:cc_]
            P.dma("sp", "stg%d" % (i % 4), dst, src)

    for i in range(min(3, NP_)):
        tape_load(i)
    for i in range(NP_):
        stg = STG4[i % 4]
        cb = CB[i % 4]
        nm_i = pieces[i][0]
        gi = None
        if nm_i.startswith("a_wo"):
            gi, hh = 0, int(nm_i[4])
        elif nm_i.startswith("b_wo"):
            gi, hh = 2, int(nm_i[4])
        elif "_w2_" in nm_i:
            gi, hh = 2 * int(nm_i[1]) + 1, int(nm_i.split("_")[2])
        if gi is not None:
            P.I("dve", "tensor_tensor", out=cb.rearrange("p (k c) -> p k c", c=512), in0=stg.rearrange("p (k c) -> p k c", c=512),
                in1=GATEV[gi][:, hh * 512:(hh + 1) * 512].unsqueeze(1).to_broadcast([128, 4, 512]), op=ALU.mult)
        elif i % 2 == 0:
            P.I("act", "activation", out=cb, in_=stg, func=AF.Copy)
        else:
            P.I("dve", "tensor_copy", out=cb, in_=stg)
        if i + 3 < NP_:
            tape_load(i + 3)
        P.dma("sp", "tp%d" % (i % 4), tape[i], cb)

    wq = {"n": 0}

    def wload(name):
        s = wq["n"] % NSLOT
        wq["n"] += 1
        slot = WS[:, s, :]
        P.dma("sp", "w%d" % s, slot, tape[pidx[name]])
        return slot

    class Stream:
        def __init__(self, names, depth=NSLOT - 2):
            self.names = names
            self.depth = depth
            self.issued = []
            self.pos = 0
            self.released = set()
            self.auto_prev = None

        def _prefetch(self):
            while len(self.issued) < min(len(self.names), self.pos + self.depth):
                p = len(self.issued)
                prior = p - NSLOT
                if prior >= 0 and prior not in self.released:
                    break
                self.issued.append(wload(self.names[p]))

        def get(self, hold=False):
            if self.auto_prev is not None:
                self.released.add(self.auto_prev)
                self.auto_prev = None
            self._prefetch()
            assert len(self.issued) > self.pos, "weight ring deadlock"
            s = self.issued[self.pos]
            if not hold:
                self.auto_prev = self.pos
            tok = self.pos
            self.pos += 1
            return (s, tok) if hold else s

        def done(self, tok):
            self.released.add(tok)

    stop_after = dbg.get("stop_after", None)
    phases = {"l0mix": ("a_",), "l0": ("a_", "f0"), "l1mix": ("a_", "f0", "b_"), None: ("a_", "f0", "b_", "f1")}[stop_after]
    tile_names = [p[0] for p in pieces if p[0][:2] in phases]
    all_names = tile_names * NT
    WST = Stream(all_names)

    def run_pipelined(gen_fns, width=2):
        it = iter(gen_fns)
        active = []

        def start():
            try:
                f = next(it)
            except StopIteration:
                return
            active.append(f())

        for _ in range(width):
            start()
        while active:
            for g in list(active):
                try:
                    next(g)
                except StopIteration:
                    active.remove(g)
                    start()

    def rms_to_HT(l, which):
        a_i, s_i = (0, 1) if which == 0 else (2, 3)
        XN = L1T
        for b in range(NB):
            ss = STAT[:, b:b + 1]
            P.I("act", "activation", out=JUNK[:, 0:D], in_=XT[:, b, :], func=AF.Square, accum_out=ss)
            P.I("act", "activation", out=STAT[:, 8 + b:9 + b], in_=ss, func=AF.Sqrt, scale=1.0 / D, bias=EPS)
            P.I("dve", "reciprocal", out=STAT[:, 16 + b:17 + b], in_=STAT[:, 8 + b:9 + b])
        for b in range(NB):
            P.I("dve", "tensor_scalar", out=DRT[:, b, :], in0=IDENT[:], scalar1=STAT[:, 16 + b:17 + b], scalar2=None, op0=ALU.mult)
        for cg in range(2):
            banks4 = [bank() for _ in range(4)]
            for ci in range(4):
                c = cg * 4 + ci
                for b in range(NB):
                    P.I("pe", "matmul", out=banks4[ci][:, b * 128:(b + 1) * 128], lhsT=XT[:, b, c * 128:(c + 1) * 128], rhs=DRT[:, b, :],
                        start=True, stop=True)
                P.I("act", "activation", out=HT[:, c, :], in_=banks4[ci][:], func=AF.Identity,
                    scale=FCOL[:, l, a_i, c:c + 1], bias=FCOL[:, l, s_i, c:c + 1])

    def residual(banks4, h, gate_idx):
        for b in range(NB):
            P.I("dve", "tensor_tensor", out=XT[:, b, h * 512:(h + 1) * 512], in0=banks4[b][:], in1=XT[:, b, h * 512:(h + 1) * 512], op=ALU.add)

    def ffn(l):
        rms_to_HT(l, 1)
        RL = L1T
        for pc in range(16):
            slot = WST.get().rearrange("p (k c) -> p k c", c=256)
            for fc in range(2):
                pb = bank()
                for k in range(8):
                    P.I("pe", "matmul", out=pb[:], lhsT=slot[:, k, fc * 128:(fc + 1) * 128], rhs=HT[:, k, :],
                        start=(k == 0), stop=(k == 7))
                r = RL[:, (pc * 2 + fc) % 4, :]
                P.I("act", "activation", out=r, in_=pb[:], func=AF.Relu)
                P.I("pool", "tensor_tensor", out=H1T[:, pc * 2 + fc, :], in0=r, in1=r, op=ALU.mult)
        for h in range(2):
            banks4 = [bank() for _ in range(4)]
            for kp in range(8):
                slot = WST.get().rearrange("p (k c) -> p k c", c=512)
                for b in range(NB):
                    for kk in range(4):
                        kc = kp * 4 + kk
                        P.I("pe", "matmul", out=banks4[b][:], lhsT=H1T[:, kc, b * 128:(b + 1) * 128], rhs=slot[:, kk, :],
                            start=(kc == 0), stop=(kc == 31))
            residual(banks4, h, 2 * l + 1)

    XBP = BIG[:, 0:1030].rearrange("p (c t) -> p c t", t=515)
    XC2 = BIG[:, 1030:3078].rearrange("p (s c t) -> p s c t", s=2, t=512)
    AF32 = ARENA[:, 2560:8192]
    _cs0 = [BIG[:, 3078:3590], BIG[:, 3590:4102]] + [AF32[:, i * 512:(i + 1) * 512] for i in range(4)]
    _cs1 = [AF32[:, 2048 + i * 512:2048 + (i + 1) * 512] for i in range(6)]
    CSET = [_cs0, _cs1]
    XCB = ARENA_bf[:, 2 * (2560 + 5120):2 * (2560 + 5120) + 1024].rearrange("p (c t) -> p c t", t=512)
    HALFB = STAT[:, 48:49].to_broadcast([128, 512])

    def l0_mixer():
        rms_to_HT(0, 0)
        st = {"bs_done": set(), "gates": {}, "cdone": set(), "slots": {}}

        def bstage(n):
            while n >= 1 and (n - 1) not in st["bs_done"]:
                yield
            (s_xb_, t_xb) = WST.get(hold=True)
            (s_gb_, t_gb) = WST.get(hold=True)
            (s_gt_, t_gt) = WST.get(hold=True)
            s_xb = s_xb_.rearrange("p (k c) -> p k c", c=256)
            st["slots"][n] = (s_gb_.rearrange("p (k c) -> p k c", c=256), t_gb,
                              s_gt_[:, 0:1024].rearrange("p (g k c) -> p g k c", g=2, k=2), t_gt)
            XC = XC2[:, n % 2]
            while n >= 2 and not ((n - 2, 0) in st["cdone"] and (n - 2, 1) in st["cdone"]):
                yield
            for ci in range(2):
                c = 2 * n + ci
                pb = bank()
                for k in range(8):
                    P.I("pe", "matmul", out=pb[:], lhsT=s_xb[:, k, ci * 128:(ci + 1) * 128], rhs=HT[:, k, :],
                        start=(k == 0), stop=(k == 7))
                P.I("pool", "tensor_copy", out=XBP[:, ci, 0:3], in_=CARRY[:, c, :])
                P.I("act", "activation", out=XBP[:, ci, 3:515], in_=pb[:], func=AF.Copy)
                P.I("act", "activation", out=XC[:, ci, :], in_=pb[:], func=AF.Identity, scale=LCOL[:, 3, c:c + 1],
                    bias=LCOL[:, 4, c:c + 1])
                yield
                P.I("pool", "tensor_copy", out=CARRY[:, c, :], in_=XBP[:, ci, 512:515])
                for j in range(3):
                    P.I("dve", "scalar_tensor_tensor", out=XC[:, ci, :], in0=XBP[:, ci, j:j + 512], scalar=LCOL[:, j, c:c + 1],
                        in1=XC[:, ci, :], op0=ALU.mult, op1=ALU.add)
                    yield
            WST.done(t_xb)
            while n >= 1 and st["gates"].get(n - 1, 0) < 2:
                yield
            P.I("dve", "tensor_copy", out=XCB[:], in_=XC[:])
            st["bs_done"].add(n)
            yield

        def chunk(n, oc):
            c = 2 * n + oc
            R_, I_, B_, H_, GB_, T1_ = CSET[c % 2]
            while n not in st["bs_done"]:
                yield
            while c >= 2 and ((c - 2) // 2, (c - 2) % 2) not in st["cdone"]:
                yield
            s_gb, t_gb, s_gt, t_gt = st["slots"][n]
            XC = XC2[:, n % 2]
            for g, dst, bj in ((0, R_, 9), (1, I_, 10)):
                pb = bank()
                for kc in range(2):
                    P.I("pe", "matmul", out=pb[:], lhsT=s_gt[:, g, kc, oc * 128:(oc + 1) * 128], rhs=XCB[:, kc, :],
                        start=(kc == 0), stop=(kc == 1))
                P.I("act", "activation", out=dst, in_=pb[:], func=AF.Tanh, scale=0.5, bias=LCOL[:, bj, c:c + 1])
            st["gates"][n] = st["gates"].get(n, 0) + 1
            if st["gates"][n] == 2:
                WST.done(t_gt)
            yield
            pb = bank()
            for k in range(8):
                P.I("pe", "matmul", out=pb[:], lhsT=s_gb[:, k, oc * 128:(oc + 1) * 128], rhs=HT[:, k, :],
                    start=(k == 0), stop=(k == 7))
            P.I("act", "activation", out=GB_, in_=pb[:], func=AF.Copy)
            P.I("act", "activation", out=T1_, in_=pb[:], func=AF.Square, scale=math.sqrt(0.044715))
            if oc == 1:
                WST.done(t_gb)
            yield
            P.I("act", "activation", out=B_, in_=R_, func=AF.Exp, scale=LCOL[:, 8, c:c + 1], bias=LCOL[:, 8, c:c + 1])
            yield
            P.I("act", "activation", out=R_, in_=R_, func=AF.Exp, scale=LCOL[:, 12, c:c + 1], bias=LCOL[:, 12, c:c + 1])
            yield
            sqq = st.setdefault(("sq", n), [])
            sqq.append(B_)
            if len(sqq) == 2:
                for bb in sqq:
                    P.I("act", "activation", out=bb, in_=bb, func=AF.Sqrt, scale=-1.0, bias=1.0)
                st[("sqdone", n)] = True
            while not st.get(("sqdone", n)):
                yield
            yield
            P.I("dve", "scalar_tensor_tensor", out=I_, in0=I_, scalar=1.0, in1=XC[:, oc, :], op0=ALU.add, op1=ALU.mult)
            yield
            P.I("dve", "scalar_tensor_tensor", out=B_, in0=B_, scalar=0.5, in1=I_, op0=ALU.mult, op1=ALU.mult)
            yield
            P.I("dve", "scalar_tensor_tensor", out=T1_, in0=T1_, scalar=1.0, in1=GB_, op0=ALU.add, op1=ALU.mult)
            yield
            P.I("act", "activation", out=T1_, in_=T1_, func=AF.Tanh, scale=0.7978845608028654)
            yield
            P.I("dve", "scalar_tensor_tensor", out=T1_, in0=T1_, scalar=1.0, in1=GB_, op0=ALU.add, op1=ALU.mult)
            yield
            P.I("dve", "tensor_tensor_scan", out=H_, data0=R_, data1=B_, initial=HST[:, c:c + 1], op0=ALU.mult, op1=ALU.add)
            yield
            P.I("dve", "tensor_copy", out=HST[:, c:c + 1], in_=H_[:, 511:512])
            P.I("dve", "scalar_tensor_tensor", out=YT[:, c, :], in0=H_, scalar=0.5, in1=T1_, op0=ALU.mult, op1=ALU.mult)
            st["cdone"].add((n, oc))
            yield

        gens = []
        for n in range(5):
            gens.append(lambda n=n: bstage(n))
            gens.append(lambda n=n: chunk(n, 0))
            gens.append(lambda n=n: chunk(n, 1))
        run_pipelined(gens, width=3)
        for h in range(2):
            banks4 = [bank() for _ in range(4)]
            for kp in range(3):
                slot = WST.get().rearrange("p (k c) -> p k c", c=512)
                for b in range(NB):
                    for kk in range(min(4, 10 - kp * 4)):
                        kc = kp * 4 + kk
                        P.I("pe", "matmul", out=banks4[b][:], lhsT=YT[:, kc, b * 128:(b + 1) * 128], rhs=slot[:, kk, :],
                            start=(kc == 0), stop=(kc == 9))
            residual(banks4, h, 0)

    def rope(dst, src, b, nh, eng_a, eng_b, tmp, tabs=None):
        if tabs is None:
            tabs = (COS[:, b:b + 1, :], SIN[:, b:b + 1, :], COS[:, b:b + 1, :], SIN[:, b:b + 1, :])
        c1, s2, c2, s1 = (tt.to_broadcast([128, nh, 32]) for tt in tabs)
        x1, x2 = src[:, :, 0:32], src[:, :, 32:64]
        P.I(eng_a, "tensor_tensor", out=tmp[:, :, 0:32], in0=x2, in1=s2, op=ALU.mult)
        P.I(eng_b, "tensor_tensor", out=dst[:, :, 0:32], in0=x1, in1=c1, op=ALU.mult)
        yield
        P.I(eng_a, "tensor_tensor", out=tmp[:, :, 32:64], in0=x1, in1=s1, op=ALU.mult)
        P.I(eng_b, "tensor_tensor", out=dst[:, :, 32:64], in0=x2, in1=c2, op=ALU.mult)
        yield
        P.I(eng_b, "tensor_tensor", out=dst[:, :, 0:32], in0=dst[:, :, 0:32], in1=tmp[:, :, 0:32], op=ALU.subtract)
        yield
        P.I(eng_b, "tensor_tensor", out=dst[:, :, 32:64], in0=dst[:, :, 32:64], in1=tmp[:, :, 32:64], op=ALU.add)
        yield

    def headnorm(q3, nh, st0, SQ):
        sq = SQ[:, 0:nh * 64].rearrange("p (h d) -> p h d", d=64)
        ss = STAT[:, st0:st0 + nh]
        P.I("act", "activation", out=sq, in_=q3, func=AF.Square)
        yield
        P.I("dve", "tensor_reduce", out=ss, in_=sq, axis=AX.X, op=ALU.add)
        yield
        P.I("act", "activation", out=ss, in_=ss, func=AF.Sqrt, scale=1.0 / 64, bias=EPS)
        yield
        P.I("dve", "reciprocal", out=ss, in_=ss)
        yield
        P.I("dve", "tensor_tensor", out=q3, in0=q3, in1=ss.unsqueeze(2).to_broadcast([128, nh, 64]), op=ALU.mult)
        yield

    def l1_mixer(t):
        rms_to_HT(1, 0)
        P.dma("sp", "rope0", COS[:], cos_d[t * T:(t + 1) * T, :].rearrange("(b p) f -> p b f", p=128))
        P.dma("sp", "rope1", SIN[:], sin_d[t * T:(t + 1) * T, :].rearrange("(b p) f -> p b f", p=128))
        for RT_, G_t in ((RTQ, QG), (RTK, KG)):
            g1 = G_t[:, 0:32].unsqueeze(1).to_broadcast([128, NB, 32])
            g2 = G_t[:, 32:64].unsqueeze(1).to_broadcast([128, NB, 32])
            P.I("pool", "tensor_tensor", out=RT_[:, :, 0, :], in0=COS[:], in1=g1, op=ALU.mult)
            P.I("pool", "tensor_tensor", out=RT_[:, :, 1, :], in0=SIN[:], in1=g2, op=ALU.mult)
            P.I("pool", "tensor_tensor", out=RT_[:, :, 2, :], in0=COS[:], in1=g2, op=ALU.mult)
            P.I("pool", "tensor_tensor", out=RT_[:, :, 3, :], in0=SIN[:], in1=g1, op=ALU.mult)
        QIR = BIG[:, 0:2048].rearrange("p (b f) -> p b f", f=512)
        jbanks = {}
        evac_count = {}

        def mm_j(j):
            w = min(512, B_IN - j * 512)
            banks4 = [bank() for _ in range(4)]
            jbanks[j] = banks4
            for kp in range(2):
                slot = WST.get().rearrange("p (k c) -> p k c", c=512)
                for b in range(NB):
                    for kk in range(4):
                        kc = kp * 4 + kk
                        P.I("pe", "matmul", out=banks4[b][:, 0:w], lhsT=HT[:, kc, b * 128:(b + 1) * 128], rhs=slot[:, kk, 0:w],
                            start=(kc == 0), stop=(kc == 7))

        def step(j, b):
            w = min(512, B_IN - j * 512)
            if b == 0:
                while j > 0 and evac_count[j - 1] < NB:
                    yield
                ps_rr[0] = 4
                mm_j(j)
                ps_rr[0] = 0
            while j not in jbanks:
                yield
            banks4 = jbanks[j]
            gb = t * NB + b
            par = (j * NB + b) % 2
            LX = L1T if par == 0 else L1U
            QF = LX[:, 0, :]
            QR = LX[:, 1, :]
            TM = LX[:, 2, :]
            SQ = LX[:, 3, :]
            P.I("act", "activation", out=QF[:, 0:w], in_=banks4[b][:, 0:w], func=AF.Copy)
            evac_count[j] = evac_count.get(j, 0) + 1
            yield
            if j < 2:
                q3 = QF.rearrange("p (h d) -> p h d", d=64)
                yield from headnorm(q3, 8, 24 + 8 * par, SQ)
                yield from rope(QR.rearrange("p (h d) -> p h d", d=64), q3, b, 8, "pool", "dve", TM.rearrange("p (h d) -> p h d", d=64),
                                tabs=tuple(RTQ[:, b:b + 1, i, :] for i in range(4)))
                pb = PS[par]
                for pr in range(4):
                    P.I("pe", "transpose", out=pb[:, pr * 128:(pr + 1) * 128], in_=QR[:, pr * 128:(pr + 1) * 128], identity=IDENT[:])
                yield
                P.I("act", "activation", out=QT[:, 4 * j:4 * j + 4, b * 128:(b + 1) * 128],
                    in_=pb[:].rearrange("p (a q) -> p a q", q=128), func=AF.Copy)
                yield
            elif j == 2:
                k3 = QF[:, 0:64].rearrange("p (h d) -> p h d", d=64)
                yield from headnorm(k3, 1, 40 + par, SQ)
                KR2 = QR[:, 0:128].rearrange("p (h d) -> p h d", d=64)
                yield from rope(KR2[:, 0:1, :], k3, b, 1, "pool", "dve", TM[:, 0:64].rearrange("p (h d) -> p h d", d=64),
                                tabs=tuple(RTK[:, b:b + 1, i, :] for i in range(4)))
                P.I("pool", "tensor_copy", out=KR2[:, 1:2, :], in_=KR2[:, 0:1, :])
                yield
                pb = PS[par]
                P.I("pe", "transpose", out=pb[:, 0:128], in_=QR[:, 0:128], identity=IDENT[:])
                yield
                P.I("act", "activation", out=KTE[0:64, gb * 128:(gb + 1) * 128], in_=pb[0:64, 0:128], func=AF.Copy)
                P.I("act", "activation", out=KTO[64:128, gb * 128:(gb + 1) * 128], in_=pb[64:128, 0:128], func=AF.Copy)
                P.I("pool", "tensor_copy", out=VW[:, gb, 64:128], in_=QF[:, 64:128])
                yield
                yield from rope(QIR[:, b, 0:384].rearrange("p (h d) -> p h d", d=64), QF[:, 128:512].rearrange("p (h d) -> p h d", d=64),
                                b, 6, "pool", "dve", TM[:, 128:512].rearrange("p (h d) -> p h d", d=64))
            else:
                yield from rope(QIR[:, b, 384:512].rearrange("p (h d) -> p h d", d=64), QF[:, 0:128].rearrange("p (h d) -> p h d", d=64),
                                b, 2, "pool", "dve", TM[:, 0:128].rearrange("p (h d) -> p h d", d=64))
                KI2 = QR[:, 0:128].rearrange("p (h d) -> p h d", d=64)
                yield from rope(KI2[:, 0:1, :], QF[:, 128:192].rearrange("p (h d) -> p h d", d=64), b, 1, "pool", "dve",
                                TM[:, 128:192].rearrange("p (h d) -> p h d", d=64))
                P.I("pool", "tensor_copy", out=KI2[:, 1:2, :], in_=KI2[:, 0:1, :])
                yield
                pb = PS[par]
                P.I("pe", "transpose", out=pb[:, 0:128], in_=QR[:, 0:128], identity=IDENT[:])
                yield
                P.I("act", "activation", out=KITE[0:64, gb * 128:(gb + 1) * 128], in_=pb[0:64, 0:128], func=AF.Copy)
                P.I("act", "activation", out=KITO[64:128, gb * 128:(gb + 1) * 128], in_=pb[64:128, 0:128], func=AF.Copy)
                yield
                wsc = (8 ** -0.5) / 8.0
                P.I("dve", "tensor_scalar", out=WIS[:, b, 0, :], in0=QF[:, 192:200], scalar1=wsc, scalar2=None, op0=ALU.mult)
                yield
                P.I("dve", "tensor_scalar", out=WIS[:, b, 2, :], in0=WIS[:, b, 0, :], scalar1=0.0, scalar2=2.0, op0=ALU.is_ge, op1=ALU.mult)
                yield
                P.I("dve", "tensor_scalar", out=WIS[:, b, 2, :], in0=WIS[:, b, 2, :], scalar1=-1.0, scalar2=None, op0=ALU.add)
                yield
                P.I("dve", "tensor_tensor", out=WIS[:, b, 1, :], in0=WIS[:, b, 0, :], in1=WIS[:, b, 2, :], op=ALU.mult)
                yield
                pb = PS[2 + par]
                for pr in range(4):
                    P.I("pe", "transpose", out=pb[:, pr * 128:(pr + 1) * 128], in_=QIR[:, b, pr * 128:(pr + 1) * 128], identity=IDENT[:])
                yield
                P.I("act", "activation", out=QIT[:, :, b * 128:(b + 1) * 128], in_=pb[:].rearrange("p (a q) -> p a q", q=128), func=AF.Copy)
                yield

        run_pipelined([(lambda j=j, b=b: step(j, b)) for j in range(4) for b in range(NB)], width=2)

        L1TF = L1T[:].rearrange("p a b -> p (a b)")
        L1UF = L1U[:].rearrange("p a b -> p (a b)")
        LO, HI, CNT, G_, MID = (BIS[:, i:i + 1] for i in range(5))
        SGN, TT_, LO2, HI2 = BIS[:, 5:6], BIS[:, 6:7], BIS[:, 7:8], BIS[:, 8 + NITER:9 + NITER]
        WK = BIS[:, 8:8 + NITER]

        def sc(buf, lo, hi):
            if buf == 0:
                return BIG[:, lo:hi]
            if hi <= 2048:
                return L1TF[:, lo:hi]
            assert lo >= 2048
            return L1UF[:, lo - 2048:hi - 2048]

        def idx_gen(b):
            gb = t * NB + b
            buf = gb % 2
            nk = (gb + 1) * 128
            nkc = (nk + 511) // 512
            P.I("dve", "tensor_tensor", out=DSG[:], in0=IDENTB[:].unsqueeze(1).to_broadcast([128, 8, 128]),
                in1=WIS[:, b, 2, :].unsqueeze(2).to_broadcast([128, 8, 128]), op=ALU.mult)
            yield
            ixs = [(kc, h) for kc in range(nkc) for h in range(8)]

            def emit_L(i):
                kc, h = ixs[i]
                w = min(512, nk - kc * 512)
                P.I("pe", "matmul", out=PS[i % 4][:, 0:w], lhsT=QIT[:, h // 2, b * 128:(b + 1) * 128],
                    rhs=(KITE if h % 2 == 0 else KITO)[:, kc * 512:kc * 512 + w], start=True, stop=True)

            for i in range(min(3, len(ixs))):
                emit_L(i)
            for i, (kc, h) in enumerate(ixs):
                w = min(512, nk - kc * 512)
                rl = RLB[:, i % 4, 0:w]
                if i % 3 != 2:
                    P.I("act", "activation", out=rl, in_=PS[i % 4][:, 0:w], func=AF.Relu, scale=WIS[:, b, 1, h:h + 1])
                else:
                    P.I("dve", "tensor_scalar", out=rl, in0=PS[i % 4][:, 0:w], scalar1=0.0, scalar2=WIS[:, b, 1, h:h + 1],
                        op0=ALU.max, op1=ALU.mult)
                scb = PS[4 + (kc % 2)]
                if i + 3 < len(ixs):
                    emit_L(i + 3)
                P.I("pe", "matmul", out=scb[:, 0:w], lhsT=DSG[:, h, :], rhs=rl, start=(h == 0), stop=(h == 7))
                if h == 7:
                    P.I("act", "activation", out=sc(buf, kc * 512, kc * 512 + w), in_=scb[:, 0:w], func=AF.Copy)
                yield

        def bisect_gen(b):
            gb = t * NB + b
            buf = gb % 2
            nk = (gb + 1) * 128
            split = buf == 1 and nk > 2048
            if nk > TOPK:
                if split:
                    P.I("dve", "tensor_reduce", out=LO, in_=sc(buf, 0, 2048), axis=AX.X, op=ALU.min)
                    P.I("dve", "tensor_reduce", out=LO2, in_=sc(buf, 2048, nk), axis=AX.X, op=ALU.min)
                    yield
                    P.I("dve", "tensor_tensor", out=LO, in0=LO, in1=LO2, op=ALU.min)
                else:
                    P.I("dve", "tensor_reduce", out=LO, in_=sc(buf, 0, nk), axis=AX.X, op=ALU.min)
                yield
            dg = sc(buf, gb * 128, (gb + 1) * 128)
            P.I("dve", "tensor_tensor", out=dg, in0=dg, in1=CAUS[:], op=ALU.add)
            yield
            if nk > TOPK:
                if split:
                    P.I("dve", "tensor_reduce", out=HI, in_=sc(buf, 0, 2048), axis=AX.X, op=ALU.max)
                    P.I("dve", "tensor_reduce", out=HI2, in_=sc(buf, 2048, nk), axis=AX.X, op=ALU.max)
                    yield
                    P.I("dve", "tensor_tensor", out=HI, in0=HI, in1=HI2, op=ALU.max)
                else:
                    P.I("dve", "tensor_reduce", out=HI, in_=sc(buf, 0, nk), axis=AX.X, op=ALU.max)
                yield
                P.I("dve", "tensor_tensor", out=HI, in0=HI, in1=LO, op=ALU.subtract)
                yield
                P.I("dve", "tensor_tensor", out=WK, in0=PW2[:], in1=HI.to_broadcast([128, NITER]), op=ALU.mult)
                yield
                n1 = 2048 if split else max(128, (int(nk * 0.45) // 128) * 128)
                n2 = nk - n1
                P.I("dve", "tensor_tensor", out=MID, in0=LO, in1=WK[:, 0:1], op=ALU.add)
                yield
                thr_c = float(2 * TOPK - n2)
                for it in range(NITER):
                    P.I("act", "activation", out=JUNK[:, n1:nk], in_=sc(buf, n1, nk), func=AF.Sign, scale=-1.0, bias=MID, accum_out=SGN)
                    P.I("dve", "tensor_scalar", out=JUNK[:, 0:n1], in0=sc(buf, 0, n1), scalar1=MID, scalar2=None, op0=ALU.is_ge,
                        op1=ALU.add, accum_out=CNT)
                    yield
                    yield
                    P.I("dve", "scalar_tensor_tensor", out=TT_, in0=CNT, scalar=2.0, in1=SGN, op0=ALU.mult, op1=ALU.subtract)
                    yield
                    if it + 1 < NITER:
                        P.I("dve", "scalar_tensor_tensor", out=G_, in0=TT_, scalar=thr_c, in1=WK[:, it:it + 1], op0=ALU.is_ge, op1=ALU.mult)
                        yield
                        P.I("dve", "scalar_tensor_tensor", out=MID, in0=MID, scalar=WK[:, it + 1:it + 2], in1=G_, op0=ALU.subtract, op1=ALU.add)
                        yield
                    else:
                        P.I("dve", "scalar_tensor_tensor", out=G_, in0=TT_, scalar=thr_c, in1=WK[:, it:it + 1], op0=ALU.is_lt, op1=ALU.mult)
                        yield
                        P.I("dve", "tensor_tensor", out=LO, in0=MID, in1=G_, op=ALU.subtract)
                        yield
            else:
                P.I("dve", "memset", ap=LO, constant=-1.0e29)
                yield

        def idx_len(b):
            nk = (t * NB + b + 1) * 128
            return 8 * ((nk + 511) // 512) + 1

        def bisect_len(b):
            nk = (t * NB + b + 1) * 128
            return (6 * NITER + 8) if nk > TOPK else 2

        def mask_build(b):
            gb = t * NB + b
            buf = gb % 2
            nk = (gb + 1) * 128
            nkc = (nk + 511) // 512
            for kc in range(nkc):
                w = min(512, nk - kc * 512)
                mk = L0X[:, (kc % 2) * 512:(kc % 2) * 512 + w]
                P.I("dve", "tensor_scalar", out=mk, in0=sc(buf, kc * 512, kc * 512 + w), scalar1=LO, scalar2=None, op0=ALU.is_ge)
                pb = PS[2 + (kc % 2)]
                for q in range(w // 128):
                    P.I("pe", "transpose", out=pb[:, q * 128:(q + 1) * 128], in_=mk[:, q * 128:(q + 1) * 128], identity=IDENT[:])
                P.I("act", "activation", out=MASKT[:, kc * 4:kc * 4 + w // 128, :], in_=pb[:, 0:w].rearrange("p (a q) -> p a q", q=128), func=AF.Copy)

        def attention(b):
            gb = t * NB + b
            ACC = PS[4:8]
            its = [(kt, g) for kt in range(gb + 1) for g in range(4)]
            DEPTH = 3

            def emit_S(i):
                kt, g = its[i]
                half, pairset = g // 2, g % 2
                P.I("pe", "matmul", out=PS[i % 4][:], lhsT=(KTE if half == 0 else KTO)[:, kt * 128:(kt + 1) * 128],
                    rhs=QT[:, pairset * 4:pairset * 4 + 4, b * 128:(b + 1) * 128], start=True, stop=True)

            for i in range(min(DEPTH, len(its))):
                emit_S(i)
            for i, (kt, g) in enumerate(its):
                eb = EB[:, i % 4, :]
                pt = PT[:, i % 4, :]
                P.I("act", "activation", out=eb, in_=PS[i % 4][:], func=AF.Exp, scale=0.125)
                P.I("dve", "tensor_tensor", out=pt.rearrange("p (a q) -> p a q", q=128), in0=eb.rearrange("p (a q) -> p a q", q=128),
                    in1=MASKT[:, kt:kt + 1, :].to_broadcast([128, 4, 128]), op=ALU.mult)
                if i + DEPTH < len(its):
                    emit_S(i + DEPTH)
                if g // 2 == 0:
                    P.I("pe", "matmul", out=ACC[g][0:65, :], lhsT=VW[:, kt, 64:129], rhs=pt, start=(kt == 0), stop=(kt == gb))
                else:
                    P.I("pe", "matmul", out=ACC[g][:], lhsT=VW[:, kt, 0:128], rhs=pt, start=(kt == 0), stop=(kt == gb))
            for g in range(4):
                half, pairset = g // 2, g % 2
                lr = 64 if half == 0 else 0
                p0, p1 = (0, 64) if half == 0 else (64, 128)
                LR_ = LROW if g % 2 == 0 else LROWB
                RB_ = RB if g % 2 == 0 else RBB
                P.I("act", "activation", out=LR_[lr:lr + 1, :], in_=ACC[g][lr:lr + 1, :], func=AF.Copy)
                pb = PS[g % 4]
                P.I("pe", "matmul", out=pb[:], lhsT=(SEL if half == 0 else SEL2)[:], rhs=LR_[:], start=True, stop=True)
                P.I("dve", "reciprocal", out=RB_[p0:p1, :], in_=pb[p0:p1, :])
                P.I("dve", "tensor_tensor", out=OUTT2[p0:p1, pairset * 4:(pairset + 1) * 4, b * 128:(b + 1) * 128],
                    in0=ACC[g][p0:p1, :].rearrange("p (a q) -> p a q", q=128), in1=RB_[p0:p1, :].rearrange("p (a q) -> p a q", q=128), op=ALU.mult)

        def run_weighted(ga, na, gb_, nb):
            ia = ib = 0
            a_done = b_done = False
            while not (a_done and b_done):
                if b_done or (not a_done and ia * nb <= ib * na):
                    try:
                        next(ga)
                        ia += 1
                    except StopIteration:
                        a_done = True
                else:
                    try:
                        next(gb_)
                        ib += 1
                    except StopIteration:
                        b_done = True

        for _ in idx_gen(0):
            pass
        for b in range(NB):
            if b + 1 < NB:
                run_weighted(bisect_gen(b), bisect_len(b), idx_gen(b + 1), idx_len(b + 1))
            else:
                for _ in bisect_gen(b):
                    pass
            mask_build(b)
            attention(b)
        for h in range(2):
            banks4 = [bank() for _ in range(4)]
            for kp in range(2):
                slot = WST.get().rearrange("p (k c) -> p k c", c=512)
                for b in range(NB):
                    for kk in range(4):
                        hs = kp * 4 + kk
                        P.I("pe", "matmul", out=banks4[b][:], lhsT=OUTT2[:, hs, b * 128:(b + 1) * 128], rhs=slot[:, kk, :],
                            start=(hs == 0), stop=(hs == 7))
            residual(banks4, h, 2)

    for t in range(NT):
        for b in range(NB):
            P.dma("sp", "xin%d" % b, XT[:, b, :], x_d[t * T + b * 128:t * T + (b + 1) * 128, :])
        l0_mixer()
        if stop_after != "l0mix":
            ffn(0)
            if stop_after != "l0":
                l1_mixer(t)
                if stop_after != "l1mix":
                    ffn(1)
        for b in range(NB):
            P.dma("sp", "xout%d" % b, out_d[t * T + b * 128:t * T + (b + 1) * 128, :], XT[:, b, :])
    P.fence("sp", [out_d[0:NT * T, :]])
    if stop_after is not None:
        pass
    P.emit()
    es.close()
    return nc


def host_consts():
    ident = np.eye(128, dtype=np.float32)
    q = np.arange(128)[:, None]
    k = np.arange(128)[None, :]
    caus = np.where(k <= q, 0.0, NEG).astype(np.float32)
    inv = (np.float32(10000.0) ** (-(np.arange(0, 64, 2, dtype=np.float32)) / np.float32(64))).astype(np.float32)
    ang = (np.arange(S_FULL, dtype=np.float32)[:, None] * inv[None, :]).astype(np.float32)
    cosT = np.cos(ang).astype(np.float32)
    sinT = np.sin(ang).astype(np.float32)
    pw2 = np.tile((0.5 ** np.arange(1, NITER + 1)).astype(np.float32)[None, :], (128, 1))
    return {"ident": ident, "caus": caus, "cosT": cosT, "sinT": sinT, "pw2": pw2}


def make_in_map(inputs, b, consts):
    m = dict(consts)
    m["x"] = np.ascontiguousarray(inputs["x"][b])
    m["ccol"] = np.ascontiguousarray(inputs["c"][b].reshape(8, 128).T)
    for nm in ("norm_mix_g", "norm_ffn_g", "ada_w", "ada_b"):
        m[nm] = np.ascontiguousarray(inputs[nm])
    for nm in ("a_w_in", "a_conv_w", "a_gate_r_w", "a_gate_i_w", "a_w_out", "b_w_in", "b_w_out"):
        m[nm] = np.ascontiguousarray(inputs[nm][0])
    for nm in ("a_conv_b", "a_gate_r_b", "a_gate_i_b", "a_lambda", "b_q_norm_g", "b_k_norm_g"):
        m[nm] = np.ascontiguousarray(inputs[nm])
    for l in range(2):
        m["ffn_w1_%d" % l] = np.ascontiguousarray(inputs["ffn_w1"][l])
        m["ffn_w2_%d" % l] = np.ascontiguousarray(inputs["ffn_w2"][l])
    return m


_NC_CACHE = {}


def kernel(**inputs):
    inputs = {k: np.asarray(v) for k, v in inputs.items()}
    if "full" not in _NC_CACHE:
        _NC_CACHE["full"] = build_nc(S_FULL)
    nc = _NC_CACHE["full"]
    consts = host_consts()
    nb = inputs["x"].shape[0]
    in_maps = [make_in_map(inputs, b, consts) for b in range(nb)]
    res = run_bass_kernel_spmd(nc, in_maps, core_ids=list(range(nb)))
    out = np.stack([np.asarray(r["out"]) for r in res.results], axis=0)
    return out.astype(np.float32)
```

```python
import math
from contextlib import ExitStack

import numpy as np
import concourse.bass as bass
import concourse.mybir as mybir
from concourse.bass_utils import run_bass_kernel_spmd

F32 = mybir.dt.float32
BF16 = mybir.dt.bfloat16
ALU = mybir.AluOpType
AF = mybir.ActivationFunctionType
AX = mybir.AxisListType

S_FULL = 4096
D = 1024
T = 512
NB = 4
DFF = 4096
DRNN = 1280
B_IN = 1736
NSLOT = 8
PIECE = 2048
NITER = 10
TOPK = 256
EPS = 1e-6
NEG = -1.0e30

_ESZ = {F32: 4, BF16: 2}
_WKEYS = ("out", "accum_out", "ap")


class _Op:
    __slots__ = ("stream", "chan", "fn", "is_dma", "cpos", "sig", "waits", "idx")


def _is_ap(v):
    return hasattr(v, "ap") and hasattr(v, "tensor") and hasattr(v, "offset")


class Prog:
    STREAMS = ("pe", "act", "dve", "pool", "sp")

    def __init__(self, nc):
        self.nc = nc
        self.ops = []
        self.stream_ops = {s: [] for s in self.STREAMS}
        self.chan_count = {}
        self.chan_last = {}
        self.track = {}
        self.waited = {s: {} for s in self.STREAMS}
        self.chan_ops = {}

    def region(self, ap):
        name = ap.tensor.name
        esz = _ESZ.get(ap.dtype, 4)
        dims = ap.ap
        off = ap.offset
        space = str(ap.space)
        if "PSUM" in space:
            return (name, 0, 128, 0, 1 << 30, True)
        if "SB" in space.upper():
            pstep, npart = dims[0]
            if pstep == 0:
                p0, free0 = 0, off
                p1 = 128
            else:
                p0 = off // pstep
                free0 = off % pstep
                p1 = p0 + npart
            ext = 0
            for st, n in dims[1:]:
                ext += abs(st) * (n - 1)
            return (name, p0, p1, free0 * esz, (free0 + ext + 1) * esz, False)
        ext = 0
        for st, n in dims:
            ext += abs(st) * (n - 1)
        return (name, 0, 1, off * esz, (off + ext + 1) * esz, False)

    def add(self, stream, fn, reads, writes, chan=None):
        op = _Op()
        op.idx = len(self.ops)
        op.stream = stream
        op.is_dma = chan is not None
        op.chan = chan if chan is not None else stream
        op.fn = fn
        op.sig = op.is_dma
        op.waits = []
        self.chan_count[op.chan] = self.chan_count.get(op.chan, 0) + 1
        op.cpos = self.chan_count[op.chan]
        deps = {}

        def add_dep(pidx):
            p = self.ops[pidx]
            if deps.get(p.chan, (0, None))[0] < p.cpos:
                deps[p.chan] = (p.cpos, p)

        if op.is_dma and chan in self.chan_last:
            add_dep(self.chan_last[chan])
        for real_w, regs in ((False, reads), (True, writes)):
            for reg in regs:
                name, p0, p1, lo, hi, psum = reg
                eff_w = real_w or psum
                for ent in self.track.get(name, ()):
                    ep0, ep1, elo, ehi, eidx, e_eff, e_real = ent
                    overlap = ep0 < p1 and p0 < ep1 and elo < hi and lo < ehi
                    if overlap and (eff_w or e_eff):
                        prod = self.ops[eidx]
                        same = (not prod.is_dma) and (not op.is_dma) and prod.stream == stream
                        if same:
                            if stream != "pe" and (e_real or real_w):
                                add_dep(eidx)
                        else:
                            add_dep(eidx)
        for real_w, regs in ((False, reads), (True, writes)):
            for reg in regs:
                name, p0, p1, lo, hi, psum = reg
                eff_w = real_w or psum
                lst = self.track.get(name, [])
                new = []
                for ent in lst:
                    ep0, ep1, elo, ehi, eidx, e_eff, e_real = ent
                    covered = p0 <= ep0 and ep1 <= p1 and lo <= elo and ehi <= hi
                    if eidx == op.idx:
                        if covered:
                            continue
                        new.append(ent)
                        continue
                    if covered and eff_w:
                        continue
                    if covered and (not e_eff) and (not eff_w):
                        prod = self.ops[eidx]
                        if (not prod.is_dma) and (not op.is_dma) and prod.stream == stream:
                            continue
                    new.append(ent)
                new.append((p0, p1, lo, hi, op.idx, eff_w, real_w))
                self.track[name] = new
        w = self.waited[stream]
        for ch, (cpos, prod) in deps.items():
            if w.get(ch, 0) >= cpos:
                continue
            w[ch] = cpos
            prod.sig = True
            op.waits.append(prod)
        if not op.is_dma:
            pass
        self.ops.append(op)
        self.stream_ops[stream].append(op)
        self.chan_ops.setdefault(op.chan, []).append(op)
        if op.is_dma:
            self.chan_last[chan] = op.idx
        return op

    def I(self, stream, method, chan=None, xr=(), xw=(), **kw):
        reads, writes = [], []
        for k, v in kw.items():
            if _is_ap(v):
                (writes if k in _WKEYS else reads).append(self.region(v))
        for v in xr:
            reads.append(self.region(v))
        for v in xw:
            writes.append(self.region(v))

        def fn(e, method=method, kw=kw):
            return getattr(e, method)(**kw)

        return self.add(stream, fn, reads, writes, chan=chan)

    def dma(self, stream, chan, out, in_, slow=False):
        if slow:
            return self.I(stream, "dma_start", chan=chan, out=out, in_=in_, allow_slow_non_contiguous=True)
        return self.I(stream, "dma_start", chan=chan, out=out, in_=in_)

    def fence(self, stream, aps):
        return self.add(stream, None, [self.region(a) for a in aps], [])

    def emit(self):
        nc = self.nc
        sigval = {}
        for ch, ops in self.chan_ops.items():
            c = 0
            for op in ops:
                if op.sig:
                    c += 16 if op.is_dma else 1
                sigval[op.idx] = c
        with ExitStack() as es:
            sems = {}
            for ch in self.chan_ops:
                sems[ch] = es.enter_context(nc.semaphore("s_" + ch))
            block = es.enter_context(nc.Block())

            def run(stream, e):
                for op in self.stream_ops[stream]:
                    for prod in op.waits:
                        e.wait_ge(sems[prod.chan], sigval[prod.idx])
                    if op.fn is None:
                        continue
                    ins = op.fn(e)
                    if op.sig:
                        ins.then_inc(sems[op.chan], 16 if op.is_dma else 1)

            @block.tensor
            def _(e):
                run("pe", e)

            @block.scalar
            def _(e):
                run("act", e)

            @block.vector
            def _(e):
                run("dve", e)

            @block.gpsimd
            def _(e):
                run("pool", e)

            @block.sync
            def _(e):
                run("sp", e)


def piece_list(nc_in):
    W = nc_in
    pieces = []

    def kview(w):
        return w.rearrange("(k p) c -> p k c", p=128)

    a_w_in = kview(W["a_w_in"])
    for n in range(5):
        pieces.append(("a_xb%d" % n, [(128, 8, 256, a_w_in[:, :, n * 256:(n + 1) * 256], 0, 256)]))
        pieces.append(("a_gb%d" % n, [(128, 8, 256, a_w_in[:, :, DRNN + n * 256:DRNN + (n + 1) * 256], 0, 256)]))
        gr = W["a_gate_r_w"][n].rearrange("(k p) c -> p k c", p=128)
        gi = W["a_gate_i_w"][n].rearrange("(k p) c -> p k c", p=128)
        pieces.append(("a_gt%d" % n, [(128, 2, 256, gr, 0, 256), (128, 2, 256, gi, 512, 256)]))
    a_w_out = kview(W["a_w_out"])
    for h in range(2):
        for kp in range(3):
            k0, k1 = kp * 4, min(kp * 4 + 4, 10)
            pieces.append(("a_wo%d_%d" % (h, kp), [(128, k1 - k0, 512, a_w_out[:, k0:k1, h * 512:(h + 1) * 512], 0, 512)]))

    def ffn(l):
        w1 = kview(W["ffn_w1_%d" % l])
        w2 = kview(W["ffn_w2_%d" % l])
        for pc in range(16):
            pieces.append(("f%d_w1_%d" % (l, pc), [(128, 8, 256, w1[:, :, pc * 256:(pc + 1) * 256], 0, 256)]))
        for h in range(2):
            for kp in range(8):
                pieces.append(("f%d_w2_%d_%d" % (l, h, kp), [(128, 4, 512, w2[:, kp * 4:(kp + 1) * 4, h * 512:(h + 1) * 512], 0, 512)]))

    ffn(0)
    b_w_in = kview(W["b_w_in"])
    for j in range(4):
        w = min(512, B_IN - j * 512)
        for kp in range(2):
            pieces.append(("b_wi%d_%d" % (j, kp), [(128, 4, w, b_w_in[:, kp * 4:(kp + 1) * 4, j * 512:j * 512 + w], 0, 512)]))
    b_w_out = kview(W["b_w_out"])
    for h in range(2):
        for kp in range(2):
            pieces.append(("b_wo%d_%d" % (h, kp), [(128, 4, 512, b_w_out[:, kp * 4:(kp + 1) * 4, h * 512:(h + 1) * 512], 0, 512)]))
    ffn(1)
    return pieces


def build_nc(S_run=S_FULL, debug=None):
    NT = S_run // T
    nc = bass.Bass("TRN2", target_bir_lowering=False)
    dbg = debug or {}

    def din(name, shape, dt=F32):
        return nc.dram_tensor(name, list(shape), dt, kind="ExternalInput").ap()

    x_d = din("x", [S_FULL, D])
    ccol_d = din("ccol", [128, 8])
    ident_d = din("ident", [128, 128])
    caus_d = din("caus", [128, 128])
    cos_d = din("cosT", [S_FULL, 32])
    sin_d = din("sinT", [S_FULL, 32])
    pw2_d = din("pw2", [128, NITER])
    W = {}
    W["norm_mix_g"] = din("norm_mix_g", [2, D])
    W["norm_ffn_g"] = din("norm_ffn_g", [2, D])
    W["ada_w"] = din("ada_w", [2, D, 6 * D])
    W["ada_b"] = din("ada_b", [2, 6 * D])
    W["a_w_in"] = din("a_w_in", [D, 2 * DRNN])
    W["a_conv_w"] = din("a_conv_w", [4, DRNN])
    W["a_conv_b"] = din("a_conv_b", [1, DRNN])
    W["a_gate_r_w"] = din("a_gate_r_w", [5, 256, 256])
    W["a_gate_r_b"] = din("a_gate_r_b", [1, DRNN])
    W["a_gate_i_w"] = din("a_gate_i_w", [5, 256, 256])
    W["a_gate_i_b"] = din("a_gate_i_b", [1, DRNN])
    W["a_lambda"] = din("a_lambda", [1, DRNN])
    W["a_w_out"] = din("a_w_out", [DRNN, D])
    W["b_w_in"] = din("b_w_in", [D, B_IN])
    W["b_q_norm_g"] = din("b_q_norm_g", [1, 64])
    W["b_k_norm_g"] = din("b_k_norm_g", [1, 64])
    W["b_w_out"] = din("b_w_out", [D, D])
    for l in range(2):
        W["ffn_w1_%d" % l] = din("ffn_w1_%d" % l, [D, DFF])
        W["ffn_w2_%d" % l] = din("ffn_w2_%d" % l, [DFF, D])
    out_d = nc.dram_tensor("out", [S_FULL, D], F32, kind="ExternalOutput").ap()

    pieces = piece_list(W)
    NP_ = len(pieces)
    pidx = {p[0]: i for i, p in enumerate(pieces)}
    tape = nc.dram_tensor("tape", [NP_, 128, PIECE], BF16, kind="Internal").ap()

    dbg_out = {}

    def dbg_tensor(name, shape):
        dbg_out[name] = nc.dram_tensor("dbg_" + name, list(shape), F32, kind="ExternalOutput").ap()
        return dbg_out[name]

    P = Prog(nc)
    es = ExitStack()

    def sb(name, shape, dt=F32):
        return es.enter_context(nc.sbuf_tensor(name, list(shape), dt))

    XT = sb("XT", [128, NB, D])
    HT = sb("HT", [128, 8, T], BF16)
    WS = sb("WS", [128, NSLOT, PIECE], BF16)
    BIG = sb("BIG", [128, 4224])
    ARENA = sb("ARENA", [128, 8192])
    KTE = sb("KTE", [128, S_FULL], BF16)
    KTO = sb("KTO", [128, S_FULL], BF16)
    KITE = sb("KITE", [128, S_FULL], BF16)
    KITO = sb("KITO", [128, S_FULL], BF16)
    VW = sb("VW", [128, 32, 130], BF16)
    COS = sb("COS", [128, NB, 32])
    SIN = sb("SIN", [128, NB, 32])
    IDENT = sb("IDENT", [128, 128])
    CAUS = sb("CAUS", [128, 128])
    ONES = sb("ONES", [128, 128])
    PW2 = sb("PW2", [128, NITER])
    FCOL = sb("FCOL", [128, 2, 4, 8])
    GCOL = sb("GCOL", [128, 2, 2, 8])
    CCOL = sb("CCOL", [128, 8])
    CONDB = sb("CONDB", [128, 8, 128])
    LCOL = sb("LCOL", [128, 13, 10])
    CARRY = sb("CARRY", [128, 10, 3])
    HST = sb("HST", [128, 10])
    QG = sb("QG", [128, 64])
    KG = sb("KG", [128, 64])
    STAT = sb("STAT", [128, 64])
    BIS = sb("BIS", [128, 10 + NITER])
    L1T = sb("L1T", [128, 4, 512])
    L1U = sb("L1U", [128, 4, 512])
    _g01 = L1T[:].rearrange("p a b -> p (a b)")
    _g23 = L1U[:].rearrange("p a b -> p (a b)")
    GATEV = [_g01[:, 0:1024], _g01[:, 1024:2048], _g23[:, 0:1024], _g23[:, 1024:2048]]
    EB = sb("EB", [128, 4, 512], BF16)
    PT = sb("PT", [128, 4, 512], BF16)
    SEL = sb("SEL", [128, 128])
    RTQ = sb("RTQ", [128, NB, 4, 32])
    RTK = sb("RTK", [128, NB, 4, 32])
    L0X = sb("L0X", [128, 1024])
    SEL2 = sb("SEL2", [128, 128])
    DRT = sb("DRT", [128, NB, 128])
    IDENTB = sb("IDENTB", [128, 128], BF16)
    DSG = sb("DSG", [128, 8, 128], BF16)
    RLB = sb("RLB", [128, 4, 512], BF16)
    QIT = sb("QIT", [128, 4, T], BF16)
    WIS = sb("WIS", [128, NB, 3, 8])
    LROW = sb("LROW", [128, 512])
    RB = sb("RB", [128, 512])
    LROWB = sb("LROWB", [128, 512])
    RBB = sb("RBB", [128, 512])

    PS = [es.enter_context(nc.psum_tensor("PS%d" % i, [128, 512], F32)) for i in range(8)]
    ps_rr = [0]

    def bank():
        b = PS[ps_rr[0] % 8]
        ps_rr[0] += 1
        return b

    ARENA_bf = ARENA[:].bitcast(BF16)
    H1T = ARENA_bf.rearrange("p (c t) -> p c t", t=T)
    YT = H1T[:, 0:10, :]
    QT = H1T[:, 0:8, :]
    OUTT2 = H1T[:, 8:16, :]
    MASKT = ARENA_bf[:, 24 * 512:32 * 512].rearrange("p (k q) -> p k q", q=128)
    JUNK = ARENA_bf[:, 24 * 512:32 * 512]

    rr = {"dve_pool": 0}

    def alt(*names):
        i = rr.get(names, 0)
        rr[names] = i + 1
        return names[i % len(names)]

    cch = [0]

    def cdma(out, in_, slow=False):
        ch = "c%d" % (cch[0] % 6)
        cch[0] += 1
        P.dma("sp", ch, out, in_, slow=slow)

    cdma(IDENT[:], ident_d)
    cdma(CAUS[:], caus_d)
    cdma(PW2[:], pw2_d)
    cdma(CCOL[:], ccol_d)
    P.I("pool", "memset", ap=ONES[:], constant=1.0)
    P.I("pool", "memset", ap=CARRY[:], constant=0.0)
    P.I("pool", "memset", ap=HST[:], constant=0.0)
    P.I("pool", "memset", ap=VW[:], constant=0.0)
    P.I("pool", "memset", ap=VW[:, :, 0:1], constant=1.0)
    P.I("pool", "memset", ap=VW[:, :, 128:129], constant=1.0)
    P.I("pool", "memset", ap=SEL2[:], constant=0.0)
    P.I("pool", "memset", ap=SEL2[0:1, :], constant=1.0)
    for tz in (KTE, KTO, KITE, KITO):
        P.I("pool", "memset", ap=tz[:], constant=0.0)
    P.I("pool", "memset", ap=SEL[:], constant=0.0)
    P.I("pool", "memset", ap=SEL[64:65, :], constant=1.0)
    P.I("pool", "memset", ap=LROW[:], constant=0.0)
    P.I("pool", "memset", ap=LROWB[:], constant=0.0)
    P.I("dve", "tensor_copy", out=IDENTB[:], in_=IDENT[:])
    for j in range(4):
        cdma(LCOL[:, j, :], W["a_conv_w"][j].rearrange("(c p) -> p c", p=128), slow=True)
    for j, nm in enumerate(["a_conv_b", "a_gate_r_b", "a_gate_i_b", "a_lambda"]):
        cdma(LCOL[:, 4 + j, :], W[nm][0].rearrange("(c p) -> p c", p=128), slow=True)
    for l in range(2):
        cdma(GCOL[:, l, 0, :], W["norm_mix_g"][l].rearrange("(c p) -> p c", p=128), slow=True)
        cdma(GCOL[:, l, 1, :], W["norm_ffn_g"][l].rearrange("(c p) -> p c", p=128), slow=True)
    cdma(QG[:], W["b_q_norm_g"][0:1, :].partition_broadcast(128))
    cdma(KG[:], W["b_k_norm_g"][0:1, :].partition_broadcast(128))

    P.I("act", "activation", out=LCOL[:, 8, :], in_=LCOL[:, 7, :], func=AF.Exp, scale=-1.0)
    P.I("act", "activation", out=LCOL[:, 8, :], in_=LCOL[:, 8, :], func=AF.Ln, bias=1.0)
    P.I("dve", "tensor_scalar", out=LCOL[:, 8, :], in0=LCOL[:, 8, :], scalar1=-8.0, scalar2=None, op0=ALU.mult)
    P.I("dve", "tensor_scalar", out=LCOL[:, 9, :], in0=LCOL[:, 5, :], scalar1=0.5, scalar2=None, op0=ALU.mult)
    P.I("dve", "tensor_scalar", out=LCOL[:, 10, :], in0=LCOL[:, 6, :], scalar1=0.5, scalar2=None, op0=ALU.mult)
    P.I("dve", "tensor_scalar", out=LCOL[:, 11, :], in0=LCOL[:, 8, :], scalar1=2.0, scalar2=None, op0=ALU.mult)
    P.I("pool", "memset", ap=STAT[:, 48:49], constant=0.5)
    P.I("dve", "tensor_scalar", out=LCOL[:, 12, :], in0=LCOL[:, 8, :], scalar1=0.5, scalar2=None, op0=ALU.mult)

    P.I("act", "activation", out=STAT[:, 0:8], in_=CCOL[:], func=AF.Sigmoid)
    P.I("dve", "tensor_tensor", out=CCOL[:], in0=CCOL[:], in1=STAT[:, 0:8], op=ALU.mult)
    P.I("dve", "tensor_copy", out=CONDB[:], in_=CCOL[:].unsqueeze(2).to_broadcast([128, 8, 128]))

    XTF0 = XT[:].rearrange("p a b -> p (a b)")
    STGA = [BIG[:, 0:2048], BIG[:, 2048:4096], XTF0[:, 0:2048], XTF0[:, 2048:4096]]
    MODP = ARENA[:, 0:2048]
    ADAB = ARENA[:, 2048:4096]
    ada_steps = [(l, cp, k) for l in range(2) for cp in range(3) for k in range(8)]

    def ada_load(i):
        l, cp, k = ada_steps[i]
        P.dma("sp", "stg%d" % (i % 4), STGA[i % 4], W["ada_w"][l, k * 128:(k + 1) * 128, cp * 2048:(cp + 1) * 2048])

    for i in range(3):
        ada_load(i)
    ada_i = [0]
    for l in range(2):
        for cp in range(3):
            cdma(ADAB, W["ada_b"][l:l + 1, cp * 2048:(cp + 1) * 2048].partition_broadcast(128))
            banks = [bank() for _ in range(4)]
            for k in range(8):
                i = ada_i[0]
                ada_i[0] += 1
                stg = STGA[i % 4]
                if i + 3 < len(ada_steps):
                    ada_load(i + 3)
                for q in range(4):
                    P.I("pe", "matmul", out=banks[q][:], lhsT=CONDB[:, k, :], rhs=stg[:, q * 512:(q + 1) * 512],
                        start=(k == 0), stop=(k == 7))
            for q in range(4):
                P.I("dve", "tensor_tensor", out=MODP[:, q * 512:(q + 1) * 512], in0=banks[q][:],
                    in1=ADAB[:, q * 512:(q + 1) * 512], op=ALU.add)
            for sgi in range(2):
                seg = 2 * cp + sgi
                src = MODP[:, sgi * 1024:(sgi + 1) * 1024]
                if seg == 2:
                    P.I("pool", "tensor_copy", out=GATEV[2 * l + 0], in_=src)
                elif seg == 5:
                    P.I("pool", "tensor_copy", out=GATEV[2 * l + 1], in_=src)
                else:
                    slot = {0: 1, 1: 0, 3: 3, 4: 2}[seg]
                    b = bank()
                    for c in range(8):
                        P.I("pe", "transpose", out=b[:, c * 8:c * 8 + 8], in_=MODP[0:8, sgi * 1024 + c * 128:sgi * 1024 + (c + 1) * 128],
                            identity=IDENT[0:8, 0:8])
                    P.I("act", "activation", out=FCOL[:, l, slot, :], in_=b[:, 0:64].rearrange("p (c j) -> p c j", j=8)[:, :, 0],
                        func=AF.Identity)
        for (a_i, g_i) in ((0, 0), (2, 1)):
            P.I("dve", "tensor_scalar", out=FCOL[:, l, a_i, :], in0=FCOL[:, l, a_i, :], scalar1=1.0, scalar2=None, op0=ALU.add)
            P.I("dve", "tensor_tensor", out=FCOL[:, l, a_i, :], in0=FCOL[:, l, a_i, :], in1=GCOL[:, l, g_i, :], op=ALU.mult)

    CB = [ARENA_bf[:, 8192 + i * PIECE:8192 + (i + 1) * PIECE] for i in range(4)]
    XTF = XT[:].rearrange("p a b -> p (a b)")
    STG4 = [BIG[:, 0:2048], BIG[:, 2048:4096], XTF[:, 0:2048], XTF[:, 2048:4096]]
    def tape_load(i):
        nm, parts = pieces[i]
        stg = STG4[i % 4]
        for (npart, kk_, cc_, src, off, cst) in parts:
            dst = stg[0:npart, off:off + kk_ * cst].rearrange("p (k c) -> p k c", c=cst)[:, :, 0:cc_]
            P.dma("sp", "stg%d" % (i % 4), dst, src)

    for i in range(min(3, NP_)):
        tape_load(i)
    for i in range(NP_):
        stg = STG4[i % 4]
        cb = CB[i % 4]
        nm_i = pieces[i][0]
        gi = None
        if nm_i.startswith("a_wo"):
            gi, hh = 0, int(nm_i[4])
        elif nm_i.startswith("b_wo"):
            gi, hh = 2, int(nm_i[4])
        elif "_w2_" in nm_i:
            gi, hh = 2 * int(nm_i[1]) + 1, int(nm_i.split("_")[2])
        if gi is not None:
            P.I("dve", "tensor_tensor", out=cb.rearrange("p (k c) -> p k c", c=512), in0=stg.rearrange("p (k c) -> p k c", c=512),
                in1=GATEV[gi][:, hh * 512:(hh + 1) * 512].unsqueeze(1).to_broadcast([128, 4, 512]), op=ALU.mult)
        elif i % 2 == 0:
            P.I("act", "activation", out=cb, in_=stg, func=AF.Copy)
        else:
            P.I("dve", "tensor_copy", out=cb, in_=stg)
        if i + 3 < NP_:
            tape_load(i + 3)
        P.dma("sp", "tp%d" % (i % 4), tape[i], cb)

    wq = {"n": 0}

    def wload(name):
        s = wq["n"] % NSLOT
        wq["n"] += 1
        slot = WS[:, s, :]
        P.dma("sp", "w%d" % s, slot, tape[pidx[name]])
        return slot

    class Stream:
        def __init__(self, names, depth=NSLOT - 2):
            self.names = names
            self.depth = depth
            self.issued = []
            self.pos = 0
            self.released = set()
            self.auto_prev = None

        def _prefetch(self):
            while len(self.issued) < min(len(self.names), self.pos + self.depth):
                p = len(self.issued)
                prior = p - NSLOT
                if prior >= 0 and prior not in self.released:
                    break
                self.issued.append(wload(self.names[p]))

        def get(self, hold=False):
            if self.auto_prev is not None:
                self.released.add(self.auto_prev)
                self.auto_prev = None
            self._prefetch()
            assert len(self.issued) > self.pos, "weight ring deadlock"
            s = self.issued[self.pos]
            if not hold:
                self.auto_prev = self.pos
            tok = self.pos
            self.pos += 1
            return (s, tok) if hold else s

        def done(self, tok):
            self.released.add(tok)

    stop_after = dbg.get("stop_after", None)
    phases = {"l0mix": ("a_",), "l0": ("a_", "f0"), "l1mix": ("a_", "f0", "b_"), None: ("a_", "f0", "b_", "f1")}[stop_after]
    tile_names = [p[0] for p in pieces if p[0][:2] in phases]
    all_names = tile_names * NT
    WST = Stream(all_names)

    def run_pipelined(gen_fns, width=2):
        it = iter(gen_fns)
        active = []

        def start():
            try:
                f = next(it)
            except StopIteration:
                return
            active.append(f())

        for _ in range(width):
            start()
        while active:
            for g in list(active):
                try:
                    next(g)
                except StopIteration:
                    active.remove(g)
                    start()

    def rms_to_HT(l, which):
        a_i, s_i = (0, 1) if which == 0 else (2, 3)
        XN = L1T
        for b in range(NB):
            ss = STAT[:, b:b + 1]
            P.I("act", "activation", out=JUNK[:, 0:D], in_=XT[:, b, :], func=AF.Square, accum_out=ss)
            P.I("act", "activation", out=STAT[:, 8 + b:9 + b], in_=ss, func=AF.Sqrt, scale=1.0 / D, bias=EPS)
            P.I("dve", "reciprocal", out=STAT[:, 16 + b:17 + b], in_=STAT[:, 8 + b:9 + b])
        for b in range(NB):
            P.I("dve", "tensor_scalar", out=DRT[:, b, :], in0=IDENT[:], scalar1=STAT[:, 16 + b:17 + b], scalar2=None, op0=ALU.mult)
        for cg in range(2):
            banks4 = [bank() for _ in range(4)]
            for ci in range(4):
                c = cg * 4 + ci
                for b in range(NB):
                    P.I("pe", "matmul", out=banks4[ci][:, b * 128:(b + 1) * 128], lhsT=XT[:, b, c * 128:(c + 1) * 128], rhs=DRT[:, b, :],
                        start=True, stop=True)
                P.I("act", "activation", out=HT[:, c, :], in_=banks4[ci][:], func=AF.Identity,
                    scale=FCOL[:, l, a_i, c:c + 1], bias=FCOL[:, l, s_i, c:c + 1])

    def residual(banks4, h, gate_idx):
        for b in range(NB):
            P.I("dve", "tensor_tensor", out=XT[:, b, h * 512:(h + 1) * 512], in0=banks4[b][:], in1=XT[:, b, h * 512:(h + 1) * 512], op=ALU.add)

    def ffn(l):
        rms_to_HT(l, 1)
        RL = L1T
        for pc in range(16):
            slot = WST.get().rearrange("p (k c) -> p k c", c=256)
            for fc in range(2):
                pb = bank()
                for k in range(8):
                    P.I("pe", "matmul", out=pb[:], lhsT=slot[:, k, fc * 128:(fc + 1) * 128], rhs=HT[:, k, :],
                        start=(k == 0), stop=(k == 7))
                r = RL[:, (pc * 2 + fc) % 4, :]
                P.I("act", "activation", out=r, in_=pb[:], func=AF.Relu)
                P.I("pool", "tensor_tensor", out=H1T[:, pc * 2 + fc, :], in0=r, in1=r, op=ALU.mult)
        for h in range(2):
            banks4 = [bank() for _ in range(4)]
            for kp in range(8):
                slot = WST.get().rearrange("p (k c) -> p k c", c=512)
                for b in range(NB):
                    for kk in range(4):
                        kc = kp * 4 + kk
                        P.I("pe", "matmul", out=banks4[b][:], lhsT=H1T[:, kc, b * 128:(b + 1) * 128], rhs=slot[:, kk, :],
                            start=(kc == 0), stop=(kc == 31))
            residual(banks4, h, 2 * l + 1)

    XBP = BIG[:, 0:1030].rearrange("p (c t) -> p c t", t=515)
    XC2 = BIG[:, 1030:3078].rearrange("p (s c t) -> p s c t", s=2, t=512)
    AF32 = ARENA[:, 2560:8192]
    _cs0 = [BIG[:, 3078:3590], BIG[:, 3590:4102]] + [AF32[:, i * 512:(i + 1) * 512] for i in range(4)]
    _cs1 = [AF32[:, 2048 + i * 512:2048 + (i + 1) * 512] for i in range(6)]
    CSET = [_cs0, _cs1]
    XCB = ARENA_bf[:, 2 * (2560 + 5120):2 * (2560 + 5120) + 1024].rearrange("p (c t) -> p c t", t=512)
    HALFB = STAT[:, 48:49].to_broadcast([128, 512])

    def l0_mixer():
        rms_to_HT(0, 0)
        st = {"bs_done": set(), "gates": {}, "cdone": set(), "slots": {}}

        def bstage(n):
            while n >= 1 and (n - 1) not in st["bs_done"]:
                yield
            (s_xb_, t_xb) = WST.get(hold=True)
            (s_gb_, t_gb) = WST.get(hold=True)
            (s_gt_, t_gt) = WST.get(hold=True)
            s_xb = s_xb_.rearrange("p (k c) -> p k c", c=256)
            st["slots"][n] = (s_gb_.rearrange("p (k c) -> p k c", c=256), t_gb,
                              s_gt_[:, 0:1024].rearrange("p (g k c) -> p g k c", g=2, k=2), t_gt)
            XC = XC2[:, n % 2]
            while n >= 2 and not ((n - 2, 0) in st["cdone"] and (n - 2, 1) in st["cdone"]):
                yield
            for ci in range(2):
                c = 2 * n + ci
                pb = bank()
                for k in range(8):
                    P.I("pe", "matmul", out=pb[:], lhsT=s_xb[:, k, ci * 128:(ci + 1) * 128], rhs=HT[:, k, :],
                        start=(k == 0), stop=(k == 7))
                P.I("pool", "tensor_copy", out=XBP[:, ci, 0:3], in_=CARRY[:, c, :])
                P.I("act", "activation", out=XBP[:, ci, 3:515], in_=pb[:], func=AF.Copy)
                P.I("act", "activation", out=XC[:, ci, :], in_=pb[:], func=AF.Identity, scale=LCOL[:, 3, c:c + 1],
                    bias=LCOL[:, 4, c:c + 1])
                yield
                P.I("pool", "tensor_copy", out=CARRY[:, c, :], in_=XBP[:, ci, 512:515])
                for j in range(3):
                    P.I("dve", "scalar_tensor_tensor", out=XC[:, ci, :], in0=XBP[:, ci, j:j + 512], scalar=LCOL[:, j, c:c + 1],
                        in1=XC[:, ci, :], op0=ALU.mult, op1=ALU.add)
                    yield
            WST.done(t_xb)
            while n >= 1 and st["gates"].get(n - 1, 0) < 2:
                yield
            P.I("dve", "tensor_copy", out=XCB[:], in_=XC[:])
            st["bs_done"].add(n)
            yield

        def chunk(n, oc):
            c = 2 * n + oc
            R_, I_, B_, H_, GB_, T1_ = CSET[c % 2]
            while n not in st["bs_done"]:
                yield
            while c >= 2 and ((c - 2) // 2, (c - 2) % 2) not in st["cdone"]:
                yield
            s_gb, t_gb, s_gt, t_gt = st["slots"][n]
            XC = XC2[:, n % 2]
            pb = bank()
            for k in range(8):
                P.I("pe", "matmul", out=pb[:], lhsT=s_gb[:, k, oc * 128:(oc + 1) * 128], rhs=HT[:, k, :],
                    start=(k == 0), stop=(k == 7))
            P.I("act", "activation", out=GB_, in_=pb[:], func=AF.Copy)
            P.I("act", "activation", out=T1_, in_=pb[:], func=AF.Square, scale=math.sqrt(0.044715))
            if oc == 1:
                WST.done(t_gb)
            yield
            for g, dst, bj in ((0, R_, 9), (1, I_, 10)):
                pb = bank()
                for kc in range(2):
                    P.I("pe", "matmul", out=pb[:], lhsT=s_gt[:, g, kc, oc * 128:(oc + 1) * 128], rhs=XCB[:, kc, :],
                        start=(kc == 0), stop=(kc == 1))
                P.I("act", "activation", out=dst, in_=pb[:], func=AF.Tanh, scale=0.5, bias=LCOL[:, bj, c:c + 1])
            st["gates"][n] = st["gates"].get(n, 0) + 1
            if st["gates"][n] == 2:
                WST.done(t_gt)
            yield
            P.I("act", "activation", out=B_, in_=R_, func=AF.Exp, scale=LCOL[:, 8, c:c + 1], bias=LCOL[:, 8, c:c + 1])
            yield
            P.I("act", "activation", out=R_, in_=R_, func=AF.Exp, scale=LCOL[:, 12, c:c + 1], bias=LCOL[:, 12, c:c + 1])
            yield
            sqq = st.setdefault(("sq", n), [])
            sqq.append(B_)
            if len(sqq) == 2:
                for bb in sqq:
                    P.I("act", "activation", out=bb, in_=bb, func=AF.Sqrt, scale=-1.0, bias=1.0)
                st[("sqdone", n)] = True
            while not st.get(("sqdone", n)):
                yield
            yield
            P.I("dve", "scalar_tensor_tensor", out=I_, in0=I_, scalar=1.0, in1=XC[:, oc, :], op0=ALU.add, op1=ALU.mult)
            yield
            P.I("dve", "scalar_tensor_tensor", out=B_, in0=B_, scalar=0.5, in1=I_, op0=ALU.mult, op1=ALU.mult)
            yield
            P.I("dve", "scalar_tensor_tensor", out=T1_, in0=T1_, scalar=1.0, in1=GB_, op0=ALU.add, op1=ALU.mult)
            yield
            P.I("act", "activation", out=T1_, in_=T1_, func=AF.Tanh, scale=0.7978845608028654)
            yield
            P.I("dve", "scalar_tensor_tensor", out=T1_, in0=T1_, scalar=1.0, in1=GB_, op0=ALU.add, op1=ALU.mult)
            yield
            P.I("dve", "tensor_tensor_scan", out=H_, data0=R_, data1=B_, initial=HST[:, c:c + 1], op0=ALU.mult, op1=ALU.add)
            yield
            P.I("dve", "tensor_copy", out=HST[:, c:c + 1], in_=H_[:, 511:512])
            P.I("dve", "scalar_tensor_tensor", out=YT[:, c, :], in0=H_, scalar=0.5, in1=T1_, op0=ALU.mult, op1=ALU.mult)
            st["cdone"].add((n, oc))
            yield

        gens = []
        for n in range(5):
            gens.append(lambda n=n: bstage(n))
            gens.append(lambda n=n: chunk(n, 0))
            gens.append(lambda n=n: chunk(n, 1))
        run_pipelined(gens, width=3)
        for h in range(2):
            banks4 = [bank() for _ in range(4)]
            for kp in range(3):
                slot = WST.get().rearrange("p (k c) -> p k c", c=512)
                for b in range(NB):
                    for kk in range(min(4, 10 - kp * 4)):
                        kc = kp * 4 + kk
                        P.I("pe", "matmul", out=banks4[b][:], lhsT=YT[:, kc, b * 128:(b + 1) * 128], rhs=slot[:, kk, :],
                            start=(kc == 0), stop=(kc == 9))
            residual(banks4, h, 0)

    def rope(dst, src, b, nh, eng_a, eng_b, tmp, tabs=None):
        if tabs is None:
            tabs = (COS[:, b:b + 1, :], SIN[:, b:b + 1, :], COS[:, b:b + 1, :], SIN[:, b:b + 1, :])
        c1, s2, c2, s1 = (tt.to_broadcast([128, nh, 32]) for tt in tabs)
        x1, x2 = src[:, :, 0:32], src[:, :, 32:64]
        P.I(eng_a, "tensor_tensor", out=tmp[:, :, 0:32], in0=x2, in1=s2, op=ALU.mult)
        P.I(eng_b, "tensor_tensor", out=dst[:, :, 0:32], in0=x1, in1=c1, op=ALU.mult)
        yield
        P.I(eng_a, "tensor_tensor", out=tmp[:, :, 32:64], in0=x1, in1=s1, op=ALU.mult)
        P.I(eng_b, "tensor_tensor", out=dst[:, :, 32:64], in0=x2, in1=c2, op=ALU.mult)
        yield
        P.I(eng_b, "tensor_tensor", out=dst[:, :, 0:32], in0=dst[:, :, 0:32], in1=tmp[:, :, 0:32], op=ALU.subtract)
        yield
        P.I(eng_b, "tensor_tensor", out=dst[:, :, 32:64], in0=dst[:, :, 32:64], in1=tmp[:, :, 32:64], op=ALU.add)
        yield

    def headnorm(q3, nh, st0, SQ):
        sq = SQ[:, 0:nh * 64].rearrange("p (h d) -> p h d", d=64)
        ss = STAT[:, st0:st0 + nh]
        P.I("act", "activation", out=sq, in_=q3, func=AF.Square)
        yield
        P.I("dve", "tensor_reduce", out=ss, in_=sq, axis=AX.X, op=ALU.add)
        yield
        P.I("act", "activation", out=ss, in_=ss, func=AF.Sqrt, scale=1.0 / 64, bias=EPS)
        yield
        P.I("dve", "reciprocal", out=ss, in_=ss)
        yield
        P.I("dve", "tensor_tensor", out=q3, in0=q3, in1=ss.unsqueeze(2).to_broadcast([128, nh, 64]), op=ALU.mult)
        yield

    def l1_mixer(t):
        rms_to_HT(1, 0)
        P.dma("sp", "rope0", COS[:], cos_d[t * T:(t + 1) * T, :].rearrange("(b p) f -> p b f", p=128))
        P.dma("sp", "rope1", SIN[:], sin_d[t * T:(t + 1) * T, :].rearrange("(b p) f -> p b f", p=128))
        for RT_, G_t in ((RTQ, QG), (RTK, KG)):
            g1 = G_t[:, 0:32].unsqueeze(1).to_broadcast([128, NB, 32])
            g2 = G_t[:, 32:64].unsqueeze(1).to_broadcast([128, NB, 32])
            P.I("pool", "tensor_tensor", out=RT_[:, :, 0, :], in0=COS[:], in1=g1, op=ALU.mult)
            P.I("pool", "tensor_tensor", out=RT_[:, :, 1, :], in0=SIN[:], in1=g2, op=ALU.mult)
            P.I("pool", "tensor_tensor", out=RT_[:, :, 2, :], in0=COS[:], in1=g2, op=ALU.mult)
            P.I("pool", "tensor_tensor", out=RT_[:, :, 3, :], in0=SIN[:], in1=g1, op=ALU.mult)
        QIR = BIG[:, 0:2048].rearrange("p (b f) -> p b f", f=512)
        jbanks = {}
        evac_count = {}

        def mm_j(j):
            w = min(512, B_IN - j * 512)
            banks4 = [bank() for _ in range(4)]
            jbanks[j] = banks4
            for kp in range(2):
                slot = WST.get().rearrange("p (k c) -> p k c", c=512)
                for b in range(NB):
                    for kk in range(4):
                        kc = kp * 4 + kk
                        P.I("pe", "matmul", out=banks4[b][:, 0:w], lhsT=HT[:, kc, b * 128:(b + 1) * 128], rhs=slot[:, kk, 0:w],
                            start=(kc == 0), stop=(kc == 7))

        def step(j, b):
            w = min(512, B_IN - j * 512)
            if b == 0:
                while j > 0 and evac_count[j - 1] < NB:
                    yield
                ps_rr[0] = 4
                mm_j(j)
                ps_rr[0] = 0
            while j not in jbanks:
                yield
            banks4 = jbanks[j]
            gb = t * NB + b
            par = (j * NB + b) % 2
            LX = L1T if par == 0 else L1U
            QF = LX[:, 0, :]
            QR = LX[:, 1, :]
            TM = LX[:, 2, :]
            SQ = LX[:, 3, :]
            P.I("act", "activation", out=QF[:, 0:w], in_=banks4[b][:, 0:w], func=AF.Copy)
            evac_count[j] = evac_count.get(j, 0) + 1
            yield
            if j < 2:
                q3 = QF.rearrange("p (h d) -> p h d", d=64)
                yield from headnorm(q3, 8, 24 + 8 * par, SQ)
                yield from rope(QR.rearrange("p (h d) -> p h d", d=64), q3, b, 8, "pool", "dve", TM.rearrange("p (h d) -> p h d", d=64),
                                tabs=tuple(RTQ[:, b:b + 1, i, :] for i in range(4)))
                pb = PS[par]
                for pr in range(4):
                    P.I("pe", "transpose", out=pb[:, pr * 128:(pr + 1) * 128], in_=QR[:, pr * 128:(pr + 1) * 128], identity=IDENT[:])
                yield
                P.I("act", "activation", out=QT[:, 4 * j:4 * j + 4, b * 128:(b + 1) * 128],
                    in_=pb[:].rearrange("p (a q) -> p a q", q=128), func=AF.Copy)
                yield
            elif j == 2:
                k3 = QF[:, 0:64].rearrange("p (h d) -> p h d", d=64)
                yield from headnorm(k3, 1, 40 + par, SQ)
                KR2 = QR[:, 0:128].rearrange("p (h d) -> p h d", d=64)
                yield from rope(KR2[:, 0:1, :], k3, b, 1, "pool", "dve", TM[:, 0:64].rearrange("p (h d) -> p h d", d=64),
                                tabs=tuple(RTK[:, b:b + 1, i, :] for i in range(4)))
                P.I("pool", "tensor_copy", out=KR2[:, 1:2, :], in_=KR2[:, 0:1, :])
                yield
                pb = PS[par]
                P.I("pe", "transpose", out=pb[:, 0:128], in_=QR[:, 0:128], identity=IDENT[:])
                yield
                P.I("act", "activation", out=KTE[0:64, gb * 128:(gb + 1) * 128], in_=pb[0:64, 0:128], func=AF.Copy)
                P.I("act", "activation", out=KTO[64:128, gb * 128:(gb + 1) * 128], in_=pb[64:128, 0:128], func=AF.Copy)
                P.I("pool", "tensor_copy", out=VW[:, gb, 64:128], in_=QF[:, 64:128])
                yield
                yield from rope(QIR[:, b, 0:384].rearrange("p (h d) -> p h d", d=64), QF[:, 128:512].rearrange("p (h d) -> p h d", d=64),
                                b, 6, "pool", "dve", TM[:, 128:512].rearrange("p (h d) -> p h d", d=64))
            else:
                yield from rope(QIR[:, b, 384:512].rearrange("p (h d) -> p h d", d=64), QF[:, 0:128].rearrange("p (h d) -> p h d", d=64),
                                b, 2, "pool", "dve", TM[:, 0:128].rearrange("p (h d) -> p h d", d=64))
                KI2 = QR[:, 0:128].rearrange("p (h d) -> p h d", d=64)
                yield from rope(KI2[:, 0:1, :], QF[:, 128:192].rearrange("p (h d) -> p h d", d=64), b, 1, "pool", "dve",
                                TM[:, 128:192].rearrange("p (h d) -> p h d", d=64))
                P.I("pool", "tensor_copy", out=KI2[:, 1:2, :], in_=KI2[:, 0:1, :])
                yield
                pb = PS[par]
                P.I("pe", "transpose", out=pb[:, 0:128], in_=QR[:, 0:128], identity=IDENT[:])
                yield
                P.I("act", "activation", out=KITE[0:64, gb * 128:(gb + 1) * 128], in_=pb[0:64, 0:128], func=AF.Copy)
                P.I("act", "activation", out=KITO[64:128, gb * 128:(gb + 1) * 128], in_=pb[64:128, 0:128], func=AF.Copy)
                yield
                wsc = (8 ** -0.5) / 8.0
                P.I("dve", "tensor_scalar", out=WIS[:, b, 0, :], in0=QF[:, 192:200], scalar1=wsc, scalar2=None, op0=ALU.mult)
                yield
                P.I("dve", "tensor_scalar", out=WIS[:, b, 2, :], in0=WIS[:, b, 0, :], scalar1=0.0, scalar2=2.0, op0=ALU.is_ge, op1=ALU.mult)
                yield
                P.I("dve", "tensor_scalar", out=WIS[:, b, 2, :], in0=WIS[:, b, 2, :], scalar1=-1.0, scalar2=None, op0=ALU.add)
                yield
                P.I("dve", "tensor_tensor", out=WIS[:, b, 1, :], in0=WIS[:, b, 0, :], in1=WIS[:, b, 2, :], op=ALU.mult)
                yield
                pb = PS[2 + par]
                for pr in range(4):
                    P.I("pe", "transpose", out=pb[:, pr * 128:(pr + 1) * 128], in_=QIR[:, b, pr * 128:(pr + 1) * 128], identity=IDENT[:])
                yield
                P.I("act", "activation", out=QIT[:, :, b * 128:(b + 1) * 128], in_=pb[:].rearrange("p (a q) -> p a q", q=128), func=AF.Copy)
                yield

        run_pipelined([(lambda j=j, b=b: step(j, b)) for j in range(4) for b in range(NB)], width=2)

        L1TF = L1T[:].rearrange("p a b -> p (a b)")
        L1UF = L1U[:].rearrange("p a b -> p (a b)")
        LO, HI, CNT, G_, MID = (BIS[:, i:i + 1] for i in range(5))
        SGN, TT_, LO2, HI2 = BIS[:, 5:6], BIS[:, 6:7], BIS[:, 7:8], BIS[:, 8 + NITER:9 + NITER]
        WK = BIS[:, 8:8 + NITER]

        def sc(buf, lo, hi):
            if buf == 0:
                return BIG[:, lo:hi]
            if hi <= 2048:
                return L1TF[:, lo:hi]
            assert lo >= 2048
            return L1UF[:, lo - 2048:hi - 2048]

        def idx_gen(b):
            gb = t * NB + b
            buf = gb % 2
            nk = (gb + 1) * 128
            nkc = (nk + 511) // 512
            P.I("dve", "tensor_tensor", out=DSG[:], in0=IDENTB[:].unsqueeze(1).to_broadcast([128, 8, 128]),
                in1=WIS[:, b, 2, :].unsqueeze(2).to_broadcast([128, 8, 128]), op=ALU.mult)
            yield
            ixs = [(kc, h) for kc in range(nkc) for h in range(8)]

            def emit_L(i):
                kc, h = ixs[i]
                w = min(512, nk - kc * 512)
                P.I("pe", "matmul", out=PS[i % 4][:, 0:w], lhsT=QIT[:, h // 2, b * 128:(b + 1) * 128],
                    rhs=(KITE if h % 2 == 0 else KITO)[:, kc * 512:kc * 512 + w], start=True, stop=True)

            for i in range(min(3, len(ixs))):
                emit_L(i)
            for i, (kc, h) in enumerate(ixs):
                w = min(512, nk - kc * 512)
                rl = RLB[:, i % 4, 0:w]
                if i % 3 != 2:
                    P.I("act", "activation", out=rl, in_=PS[i % 4][:, 0:w], func=AF.Relu, scale=WIS[:, b, 1, h:h + 1])
                else:
                    P.I("dve", "tensor_scalar", out=rl, in0=PS[i % 4][:, 0:w], scalar1=0.0, scalar2=WIS[:, b, 1, h:h + 1],
                        op0=ALU.max, op1=ALU.mult)
                scb = PS[4 + (kc % 2)]
                if i + 3 < len(ixs):
                    emit_L(i + 3)
                P.I("pe", "matmul", out=scb[:, 0:w], lhsT=DSG[:, h, :], rhs=rl, start=(h == 0), stop=(h == 7))
                if h == 7:
                    P.I("act", "activation", out=sc(buf, kc * 512, kc * 512 + w), in_=scb[:, 0:w], func=AF.Copy)
                yield

        def bisect_gen(b):
            gb = t * NB + b
            buf = gb % 2
            nk = (gb + 1) * 128
            split = buf == 1 and nk > 2048
            if nk > TOPK:
                if split:
                    P.I("dve", "tensor_reduce", out=LO, in_=sc(buf, 0, 2048), axis=AX.X, op=ALU.min)
                    P.I("dve", "tensor_reduce", out=LO2, in_=sc(buf, 2048, nk), axis=AX.X, op=ALU.min)
                    yield
                    P.I("dve", "tensor_tensor", out=LO, in0=LO, in1=LO2, op=ALU.min)
                else:
                    P.I("dve", "tensor_reduce", out=LO, in_=sc(buf, 0, nk), axis=AX.X, op=ALU.min)
                yield
            dg = sc(buf, gb * 128, (gb + 1) * 128)
            P.I("dve", "tensor_tensor", out=dg, in0=dg, in1=CAUS[:], op=ALU.add)
            yield
            if nk > TOPK:
                if split:
                    P.I("dve", "tensor_reduce", out=HI, in_=sc(buf, 0, 2048), axis=AX.X, op=ALU.max)
                    P.I("dve", "tensor_reduce", out=HI2, in_=sc(buf, 2048, nk), axis=AX.X, op=ALU.max)
                    yield
                    P.I("dve", "tensor_tensor", out=HI, in0=HI, in1=HI2, op=ALU.max)
                else:
                    P.I("dve", "tensor_reduce", out=HI, in_=sc(buf, 0, nk), axis=AX.X, op=ALU.max)
                yield
                P.I("dve", "tensor_tensor", out=HI, in0=HI, in1=LO, op=ALU.subtract)
                yield
                P.I("dve", "tensor_tensor", out=WK, in0=PW2[:], in1=HI.to_broadcast([128, NITER]), op=ALU.mult)
                yield
                n1 = 2048 if split else max(128, (int(nk * 0.45) // 128) * 128)
                n2 = nk - n1
                P.I("dve", "tensor_tensor", out=MID, in0=LO, in1=WK[:, 0:1], op=ALU.add)
                yield
                thr_c = float(2 * TOPK - n2)
                for it in range(NITER):
                    P.I("act", "activation", out=JUNK[:, n1:nk], in_=sc(buf, n1, nk), func=AF.Sign, scale=-1.0, bias=MID, accum_out=SGN)
                    P.I("dve", "tensor_scalar", out=JUNK[:, 0:n1], in0=sc(buf, 0, n1), scalar1=MID, scalar2=None, op0=ALU.is_ge,
                        op1=ALU.add, accum_out=CNT)
                    yield
                    yield
                    P.I("dve", "scalar_tensor_tensor", out=TT_, in0=CNT, scalar=2.0, in1=SGN, op0=ALU.mult, op1=ALU.subtract)
                    yield
                    if it + 1 < NITER:
                        P.I("dve", "scalar_tensor_tensor", out=G_, in0=TT_, scalar=thr_c, in1=WK[:, it:it + 1], op0=ALU.is_ge, op1=ALU.mult)
                        yield
                        P.I("dve", "scalar_tensor_tensor", out=MID, in0=MID, scalar=WK[:, it + 1:it + 2], in1=G_, op0=ALU.subtract, op1=ALU.add)
                        yield
                    else:
                        P.I("dve", "scalar_tensor_tensor", out=G_, in0=TT_, scalar=thr_c, in1=WK[:, it:it + 1], op0=ALU.is_lt, op1=ALU.mult)
                        yield
                        P.I("dve", "tensor_tensor", out=LO, in0=MID, in1=G_, op=ALU.subtract)
                        yield
            else:
                P.I("dve", "memset", ap=LO, constant=-1.0e29)
                yield

        def idx_len(b):
            nk = (t * NB + b + 1) * 128
            return 8 * ((nk + 511) // 512) + 1

        def bisect_len(b):
            nk = (t * NB + b + 1) * 128
            return (6 * NITER + 8) if nk > TOPK else 2

        def mask_build(b):
            gb = t * NB + b
            buf = gb % 2
            nk = (gb + 1) * 128
            nkc = (nk + 511) // 512
            for kc in range(nkc):
                w = min(512, nk - kc * 512)
                mk = L0X[:, (kc % 2) * 512:(kc % 2) * 512 + w]
                P.I("dve", "tensor_scalar", out=mk, in0=sc(buf, kc * 512, kc * 512 + w), scalar1=LO, scalar2=None, op0=ALU.is_ge)
                pb = PS[2 + (kc % 2)]
                for q in range(w // 128):
                    P.I("pe", "transpose", out=pb[:, q * 128:(q + 1) * 128], in_=mk[:, q * 128:(q + 1) * 128], identity=IDENT[:])
                P.I("act", "activation", out=MASKT[:, kc * 4:kc * 4 + w // 128, :], in_=pb[:, 0:w].rearrange("p (a q) -> p a q", q=128), func=AF.Copy)

        def attention(b):
            gb = t * NB + b
            ACC = PS[4:8]
            its = [(kt, g) for kt in range(gb + 1) for g in range(4)]
            DEPTH = 3

            def emit_S(i):
                kt, g = its[i]
                half, pairset = g // 2, g % 2
                P.I("pe", "matmul", out=PS[i % 4][:], lhsT=(KTE if half == 0 else KTO)[:, kt * 128:(kt + 1) * 128],
                    rhs=QT[:, pairset * 4:pairset * 4 + 4, b * 128:(b + 1) * 128], start=True, stop=True)

            for i in range(min(DEPTH, len(its))):
                emit_S(i)
            for i, (kt, g) in enumerate(its):
                eb = EB[:, i % 4, :]
                pt = PT[:, i % 4, :]
                P.I("act", "activation", out=eb, in_=PS[i % 4][:], func=AF.Exp, scale=0.125)
                P.I("dve", "tensor_tensor", out=pt.rearrange("p (a q) -> p a q", q=128), in0=eb.rearrange("p (a q) -> p a q", q=128),
                    in1=MASKT[:, kt:kt + 1, :].to_broadcast([128, 4, 128]), op=ALU.mult)
                if i + DEPTH < len(its):
                    emit_S(i + DEPTH)
                if g // 2 == 0:
                    P.I("pe", "matmul", out=ACC[g][0:65, :], lhsT=VW[:, kt, 64:129], rhs=pt, start=(kt == 0), stop=(kt == gb))
                else:
                    P.I("pe", "matmul", out=ACC[g][:], lhsT=VW[:, kt, 0:128], rhs=pt, start=(kt == 0), stop=(kt == gb))
            for g in range(4):
                half, pairset = g // 2, g % 2
                lr = 64 if half == 0 else 0
                p0, p1 = (0, 64) if half == 0 else (64, 128)
                LR_ = LROW if g % 2 == 0 else LROWB
                RB_ = RB if g % 2 == 0 else RBB
                P.I("act", "activation", out=LR_[lr:lr + 1, :], in_=ACC[g][lr:lr + 1, :], func=AF.Copy)
                pb = PS[g % 4]
                P.I("pe", "matmul", out=pb[:], lhsT=(SEL if half == 0 else SEL2)[:], rhs=LR_[:], start=True, stop=True)
                P.I("dve", "reciprocal", out=RB_[p0:p1, :], in_=pb[p0:p1, :])
                P.I("dve", "tensor_tensor", out=OUTT2[p0:p1, pairset * 4:(pairset + 1) * 4, b * 128:(b + 1) * 128],
                    in0=ACC[g][p0:p1, :].rearrange("p (a q) -> p a q", q=128), in1=RB_[p0:p1, :].rearrange("p (a q) -> p a q", q=128), op=ALU.mult)

        def run_weighted(ga, na, gb_, nb):
            ia = ib = 0
            a_done = b_done = False
            while not (a_done and b_done):
                if b_done or (not a_done and ia * nb <= ib * na):
                    try:
                        next(ga)
                        ia += 1
                    except StopIteration:
                        a_done = True
                else:
                    try:
                        next(gb_)
                        ib += 1
                    except StopIteration:
                        b_done = True

        for _ in idx_gen(0):
            pass
        for b in range(NB):
            if b + 1 < NB:
                run_weighted(bisect_gen(b), bisect_len(b), idx_gen(b + 1), idx_len(b + 1))
            else:
                for _ in bisect_gen(b):
                    pass
            mask_build(b)
            attention(b)
        for h in range(2):
            banks4 = [bank() for _ in range(4)]
            for kp in range(2):
                slot = WST.get().rearrange("p (k c) -> p k c", c=512)
                for b in range(NB):
                    for kk in range(4):
                        hs = kp * 4 + kk
                        P.I("pe", "matmul", out=banks4[b][:], lhsT=OUTT2[:, hs, b * 128:(b + 1) * 128], rhs=slot[:, kk, :],
                            start=(hs == 0), stop=(hs == 7))
            residual(banks4, h, 2)

    for t in range(NT):
        for b in range(NB):
            P.dma("sp", "xin%d" % b, XT[:, b, :], x_d[t * T + b * 128:t * T + (b + 1) * 128, :])
        l0_mixer()
        if stop_after != "l0mix":
            ffn(0)
            if stop_after != "l0":
                l1_mixer(t)
                if stop_after != "l1mix":
                    ffn(1)
        for b in range(NB):
            P.dma("sp", "xout%d" % b, out_d[t * T + b * 128:t * T + (b + 1) * 128, :], XT[:, b, :])
    P.fence("sp", [out_d[0:NT * T, :]])
    if stop_after is not None:
        pass
    P.emit()
    es.close()
    return nc


def host_consts():
    ident = np.eye(128, dtype=np.float32)
    q = np.arange(128)[:, None]
    k = np.arange(128)[None, :]
    caus = np.where(k <= q, 0.0, NEG).astype(np.float32)
    inv = (np.float32(10000.0) ** (-(np.arange(0, 64, 2, dtype=np.float32)) / np.float32(64))).astype(np.float32)
    ang = (np.arange(S_FULL, dtype=np.float32)[:, None] * inv[None, :]).astype(np.float32)
    cosT = np.cos(ang).astype(np.float32)
    sinT = np.sin(ang).astype(np.float32)
    pw2 = np.tile((0.5 ** np.arange(1, NITER + 1)).astype(np.float32)[None, :], (128, 1))
    return {"ident": ident, "caus": caus, "cosT": cosT, "sinT": sinT, "pw2": pw2}


def make_in_map(inputs, b, consts):
    m = dict(consts)
    m["x"] = np.ascontiguousarray(inputs["x"][b])
    m["ccol"] = np.ascontiguousarray(inputs["c"][b].reshape(8, 128).T)
    for nm in ("norm_mix_g", "norm_ffn_g", "ada_w", "ada_b"):
        m[nm] = np.ascontiguousarray(inputs[nm])
    for nm in ("a_w_in", "a_conv_w", "a_gate_r_w", "a_gate_i_w", "a_w_out", "b_w_in", "b_w_out"):
        m[nm] = np.ascontiguousarray(inputs[nm][0])
    for nm in ("a_conv_b", "a_gate_r_b", "a_gate_i_b", "a_lambda", "b_q_norm_g", "b_k_norm_g"):
        m[nm] = np.ascontiguousarray(inputs[nm])
    for l in range(2):
        m["ffn_w1_%d" % l] = np.ascontiguousarray(inputs["ffn_w1"][l])
        m["ffn_w2_%d" % l] = np.ascontiguousarray(inputs["ffn_w2"][l])
    return m


_NC_CACHE = {}


def kernel(**inputs):
    inputs = {k: np.asarray(v) for k, v in inputs.items()}
    if "full" not in _NC_CACHE:
        _NC_CACHE["full"] = build_nc(S_FULL)
    nc = _NC_CACHE["full"]
    consts = host_consts()
    nb = inputs["x"].shape[0]
    in_maps = [make_in_map(inputs, b, consts) for b in range(nb)]
    res = run_bass_kernel_spmd(nc, in_maps, core_ids=list(range(nb)))
    out = np.stack([np.asarray(r["out"]) for r in res.results], axis=0)
    return out.astype(np.float32)
```

```python
import math
from contextlib import ExitStack

import numpy as np
import concourse.bass as bass
import concourse.mybir as mybir
from concourse.bass_utils import run_bass_kernel_spmd

F32 = mybir.dt.float32
BF16 = mybir.dt.bfloat16
ALU = mybir.AluOpType
AF = mybir.ActivationFunctionType
AX = mybir.AxisListType

S_FULL = 4096
D = 1024
T = 512
NB = 4
DFF = 4096
DRNN = 1280
B_IN = 1736
NSLOT = 8
PIECE = 2048
NITER = 10
TOPK = 256
EPS = 1e-6
NEG = -1.0e30

_ESZ = {F32: 4, BF16: 2}
_WKEYS = ("out", "accum_out", "ap")


class _Op:
    __slots__ = ("stream", "chan", "fn", "is_dma", "cpos", "sig", "waits", "idx")


def _is_ap(v):
    return hasattr(v, "ap") and hasattr(v, "tensor") and hasattr(v, "offset")


class Prog:
    STREAMS = ("pe", "act", "dve", "pool", "sp")

    def __init__(self, nc):
        self.nc = nc
        self.ops = []
        self.stream_ops = {s: [] for s in self.STREAMS}
        self.chan_count = {}
        self.chan_last = {}
        self.track = {}
        self.waited = {s: {} for s in self.STREAMS}
        self.chan_ops = {}

    def region(self, ap):
        name = ap.tensor.name
        esz = _ESZ.get(ap.dtype, 4)
        dims = ap.ap
        off = ap.offset
        space = str(ap.space)
        if "PSUM" in space:
            return (name, 0, 128, 0, 1 << 30, True)
        if "SB" in space.upper():
            pstep, npart = dims[0]
            if pstep == 0:
                p0, free0 = 0, off
                p1 = 128
            else:
                p0 = off // pstep
                free0 = off % pstep
                p1 = p0 + npart
            ext = 0
            for st, n in dims[1:]:
                ext += abs(st) * (n - 1)
            return (name, p0, p1, free0 * esz, (free0 + ext + 1) * esz, False)
        ext = 0
        for st, n in dims:
            ext += abs(st) * (n - 1)
        return (name, 0, 1, off * esz, (off + ext + 1) * esz, False)

    def add(self, stream, fn, reads, writes, chan=None):
        op = _Op()
        op.idx = len(self.ops)
        op.stream = stream
        op.is_dma = chan is not None
        op.chan = chan if chan is not None else stream
        op.fn = fn
        op.sig = op.is_dma
        op.waits = []
        self.chan_count[op.chan] = self.chan_count.get(op.chan, 0) + 1
        op.cpos = self.chan_count[op.chan]
        deps = {}

        def add_dep(pidx):
            p = self.ops[pidx]
            if deps.get(p.chan, (0, None))[0] < p.cpos:
                deps[p.chan] = (p.cpos, p)

        if op.is_dma and chan in self.chan_last:
            add_dep(self.chan_last[chan])
        for real_w, regs in ((False, reads), (True, writes)):
            for reg in regs:
                name, p0, p1, lo, hi, psum = reg
                eff_w = real_w or psum
                for ent in self.track.get(name, ()):
                    ep0, ep1, elo, ehi, eidx, e_eff, e_real = ent
                    overlap = ep0 < p1 and p0 < ep1 and elo < hi and lo < ehi
                    if overlap and (eff_w or e_eff):
                        prod = self.ops[eidx]
                        same = (not prod.is_dma) and (not op.is_dma) and prod.stream == stream
                        if same:
                            if stream != "pe" and (e_real or real_w):
                                add_dep(eidx)
                        else:
                            add_dep(eidx)
        for real_w, regs in ((False, reads), (True, writes)):
            for reg in regs:
                name, p0, p1, lo, hi, psum = reg
                eff_w = real_w or psum
                lst = self.track.get(name, [])
                new = []
                for ent in lst:
                    ep0, ep1, elo, ehi, eidx, e_eff, e_real = ent
                    covered = p0 <= ep0 and ep1 <= p1 and lo <= elo and ehi <= hi
                    if eidx == op.idx:
                        if covered:
                            continue
                        new.append(ent)
                        continue
                    if covered and eff_w:
                        continue
                    if covered and (not e_eff) and (not eff_w):
                        prod = self.ops[eidx]
                        if (not prod.is_dma) and (not op.is_dma) and prod.stream == stream:
                            continue
                    new.append(ent)
                new.append((p0, p1, lo, hi, op.idx, eff_w, real_w))
                self.track[name] = new
        w = self.waited[stream]
        for ch, (cpos, prod) in deps.items():
            if w.get(ch, 0) >= cpos:
                continue
            w[ch] = cpos
            prod.sig = True
            op.waits.append(prod)
        if not op.is_dma:
            pass
        self.ops.append(op)
        self.stream_ops[stream].append(op)
        self.chan_ops.setdefault(op.chan, []).append(op)
        if op.is_dma:
            self.chan_last[chan] = op.idx
        return op

    def I(self, stream, method, chan=None, xr=(), xw=(), **kw):
        reads, writes = [], []
        for k, v in kw.items():
            if _is_ap(v):
                (writes if k in _WKEYS else reads).append(self.region(v))
        for v in xr:
            reads.append(self.region(v))
        for v in xw:
            writes.append(self.region(v))

        def fn(e, method=method, kw=kw):
            return getattr(e, method)(**kw)

        return self.add(stream, fn, reads, writes, chan=chan)

    def dma(self, stream, chan, out, in_, slow=False):
        if slow:
            return self.I(stream, "dma_start", chan=chan, out=out, in_=in_, allow_slow_non_contiguous=True)
        return self.I(stream, "dma_start", chan=chan, out=out, in_=in_)

    def fence(self, stream, aps):
        return self.add(stream, None, [self.region(a) for a in aps], [])

    def emit(self):
        nc = self.nc
        sigval = {}
        for ch, ops in self.chan_ops.items():
            c = 0
            for op in ops:
                if op.sig:
                    c += 16 if op.is_dma else 1
                sigval[op.idx] = c
        with ExitStack() as es:
            sems = {}
            for ch in self.chan_ops:
                sems[ch] = es.enter_context(nc.semaphore("s_" + ch))
            block = es.enter_context(nc.Block())

            def run(stream, e):
                for op in self.stream_ops[stream]:
                    for prod in op.waits:
                        e.wait_ge(sems[prod.chan], sigval[prod.idx])
                    if op.fn is None:
                        continue
                    ins = op.fn(e)
                    if op.sig:
                        ins.then_inc(sems[op.chan], 16 if op.is_dma else 1)

            @block.tensor
            def _(e):
                run("pe", e)

            @block.scalar
            def _(e):
                run("act", e)

            @block.vector
            def _(e):
                run("dve", e)

            @block.gpsimd
            def _(e):
                run("pool", e)

            @block.sync
            def _(e):
                run("sp", e)


def piece_list(nc_in):
    W = nc_in
    pieces = []

    def kview(w):
        return w.rearrange("(k p) c -> p k c", p=128)

    a_w_in = kview(W["a_w_in"])
    for n in range(5):
        pieces.append(("a_xb%d" % n, [(128, 8, 256, a_w_in[:, :, n * 256:(n + 1) * 256], 0, 256)]))
        pieces.append(("a_gb%d" % n, [(128, 8, 256, a_w_in[:, :, DRNN + n * 256:DRNN + (n + 1) * 256], 0, 256)]))
        gr = W["a_gate_r_w"][n].rearrange("(k p) c -> p k c", p=128)
        gi = W["a_gate_i_w"][n].rearrange("(k p) c -> p k c", p=128)
        pieces.append(("a_gt%d" % n, [(128, 2, 256, gr, 0, 256), (128, 2, 256, gi, 512, 256)]))
    a_w_out = kview(W["a_w_out"])
    for h in range(2):
        for kp in range(3):
            k0, k1 = kp * 4, min(kp * 4 + 4, 10)
            pieces.append(("a_wo%d_%d" % (h, kp), [(128, k1 - k0, 512, a_w_out[:, k0:k1, h * 512:(h + 1) * 512], 0, 512)]))

    def ffn(l):
        w1 = kview(W["ffn_w1_%d" % l])
        w2 = kview(W["ffn_w2_%d" % l])
        for pc in range(16):
            pieces.append(("f%d_w1_%d" % (l, pc), [(128, 8, 256, w1[:, :, pc * 256:(pc + 1) * 256], 0, 256)]))
        for h in range(2):
            for kp in range(8):
                pieces.append(("f%d_w2_%d_%d" % (l, h, kp), [(128, 4, 512, w2[:, kp * 4:(kp + 1) * 4, h * 512:(h + 1) * 512], 0, 512)]))

    ffn(0)
    b_w_in = kview(W["b_w_in"])
    for j in range(4):
        w = min(512, B_IN - j * 512)
        for kp in range(2):
            pieces.append(("b_wi%d_%d" % (j, kp), [(128, 4, w, b_w_in[:, kp * 4:(kp + 1) * 4, j * 512:j * 512 + w], 0, 512)]))
    b_w_out = kview(W["b_w_out"])
    for h in range(2):
        for kp in range(2):
            pieces.append(("b_wo%d_%d" % (h, kp), [(128, 4, 512, b_w_out[:, kp * 4:(kp + 1) * 4, h * 512:(h + 1) * 512], 0, 512)]))
    ffn(1)
    return pieces


def build_nc(S_run=S_FULL, debug=None):
    NT = S_run // T
    nc = bass.Bass("TRN2", target_bir_lowering=False)
    dbg = debug or {}

    def din(name, shape, dt=F32):
        return nc.dram_tensor(name, list(shape), dt, kind="ExternalInput").ap()

    x_d = din("x", [S_FULL, D])
    ccol_d = din("ccol", [128, 8])
    ident_d = din("ident", [128, 128])
    caus_d = din("caus", [128, 128])
    cos_d = din("cosT", [S_FULL, 32])
    sin_d = din("sinT", [S_FULL, 32])
    pw2_d = din("pw2", [128, NITER])
    W = {}
    W["norm_mix_g"] = din("norm_mix_g", [2, D])
    W["norm_ffn_g"] = din("norm_ffn_g", [2, D])
    W["ada_w"] = din("ada_w", [2, D, 6 * D])
    W["ada_b"] = din("ada_b", [2, 6 * D])
    W["a_w_in"] = din("a_w_in", [D, 2 * DRNN])
    W["a_conv_w"] = din("a_conv_w", [4, DRNN])
    W["a_conv_b"] = din("a_conv_b", [1, DRNN])
    W["a_gate_r_w"] = din("a_gate_r_w", [5, 256, 256])
    W["a_gate_r_b"] = din("a_gate_r_b", [1, DRNN])
    W["a_gate_i_w"] = din("a_gate_i_w", [5, 256, 256])
    W["a_gate_i_b"] = din("a_gate_i_b", [1, DRNN])
    W["a_lambda"] = din("a_lambda", [1, DRNN])
    W["a_w_out"] = din("a_w_out", [DRNN, D])
    W["b_w_in"] = din("b_w_in", [D, B_IN])
    W["b_q_norm_g"] = din("b_q_norm_g", [1, 64])
    W["b_k_norm_g"] = din("b_k_norm_g", [1, 64])
    W["b_w_out"] = din("b_w_out", [D, D])
    for l in range(2):
        W["ffn_w1_%d" % l] = din("ffn_w1_%d" % l, [D, DFF])
        W["ffn_w2_%d" % l] = din("ffn_w2_%d" % l, [DFF, D])
    out_d = nc.dram_tensor("out", [S_FULL, D], F32, kind="ExternalOutput").ap()

    pieces = piece_list(W)
    NP_ = len(pieces)
    pidx = {p[0]: i for i, p in enumerate(pieces)}
    tape = nc.dram_tensor("tape", [NP_, 128, PIECE], BF16, kind="Internal").ap()

    dbg_out = {}

    def dbg_tensor(name, shape):
        dbg_out[name] = nc.dram_tensor("dbg_" + name, list(shape), F32, kind="ExternalOutput").ap()
        return dbg_out[name]

    P = Prog(nc)
    es = ExitStack()

    def sb(name, shape, dt=F32):
        return es.enter_context(nc.sbuf_tensor(name, list(shape), dt))

    XT = sb("XT", [128, NB, D])
    HT = sb("HT", [128, 8, T], BF16)
    WS = sb("WS", [128, NSLOT, PIECE], BF16)
    BIG = sb("BIG", [128, 4224])
    ARENA = sb("ARENA", [128, 8192])
    KTE = sb("KTE", [128, S_FULL], BF16)
    KTO = sb("KTO", [128, S_FULL], BF16)
    KITE = sb("KITE", [128, S_FULL], BF16)
    KITO = sb("KITO", [128, S_FULL], BF16)
    VW = sb("VW", [128, 32, 130], BF16)
    COS = sb("COS", [128, NB, 32])
    SIN = sb("SIN", [128, NB, 32])
    IDENT = sb("IDENT", [128, 128])
    CAUS = sb("CAUS", [128, 128])
    ONES = sb("ONES", [128, 128])
    PW2 = sb("PW2", [128, NITER])
    FCOL = sb("FCOL", [128, 2, 4, 8])
    GCOL = sb("GCOL", [128, 2, 2, 8])
    CCOL = sb("CCOL", [128, 8])
    CONDB = sb("CONDB", [128, 8, 128])
    LCOL = sb("LCOL", [128, 13, 10])
    CARRY = sb("CARRY", [128, 10, 3])
    HST = sb("HST", [128, 10])
    QG = sb("QG", [128, 64])
    KG = sb("KG", [128, 64])
    STAT = sb("STAT", [128, 64])
    BIS = sb("BIS", [128, 10 + NITER])
    L1T = sb("L1T", [128, 4, 512])
    L1U = sb("L1U", [128, 4, 512])
    _g01 = L1T[:].rearrange("p a b -> p (a b)")
    _g23 = L1U[:].rearrange("p a b -> p (a b)")
    GATEV = [_g01[:, 0:1024], _g01[:, 1024:2048], _g23[:, 0:1024], _g23[:, 1024:2048]]
    EB = sb("EB", [128, 4, 512], BF16)
    PT = sb("PT", [128, 4, 512], BF16)
    SEL = sb("SEL", [128, 128])
    RTQ = sb("RTQ", [128, NB, 4, 32])
    RTK = sb("RTK", [128, NB, 4, 32])
    L0X = sb("L0X", [128, 1024])
    SEL2 = sb("SEL2", [128, 128])
    DRT = sb("DRT", [128, NB, 128])
    IDENTB = sb("IDENTB", [128, 128], BF16)
    DSG = sb("DSG", [128, 8, 128], BF16)
    RLB = sb("RLB", [128, 4, 512], BF16)
    QIT = sb("QIT", [128, 4, T], BF16)
    WIS = sb("WIS", [128, NB, 3, 8])
    LROW = sb("LROW", [128, 512])
    RB = sb("RB", [128, 512])
    LROWB = sb("LROWB", [128, 512])
    RBB = sb("RBB", [128, 512])

    PS = [es.enter_context(nc.psum_tensor("PS%d" % i, [128, 512], F32)) for i in range(8)]
    ps_rr = [0]

    def bank():
        b = PS[ps_rr[0] % 8]
        ps_rr[0] += 1
        return b

    ARENA_bf = ARENA[:].bitcast(BF16)
    H1T = ARENA_bf.rearrange("p (c t) -> p c t", t=T)
    YT = H1T[:, 0:10, :]
    QT = H1T[:, 0:8, :]
    OUTT2 = H1T[:, 8:16, :]
    MASKT = ARENA_bf[:, 24 * 512:32 * 512].rearrange("p (k q) -> p k q", q=128)
    JUNK = ARENA_bf[:, 24 * 512:32 * 512]

    rr = {"dve_pool": 0}

    def alt(*names):
        i = rr.get(names, 0)
        rr[names] = i + 1
        return names[i % len(names)]

    cch = [0]

    def cdma(out, in_, slow=False):
        ch = "c%d" % (cch[0] % 6)
        cch[0] += 1
        P.dma("sp", ch, out, in_, slow=slow)

    cdma(IDENT[:], ident_d)
    cdma(CAUS[:], caus_d)
    cdma(PW2[:], pw2_d)
    cdma(CCOL[:], ccol_d)
    P.I("pool", "memset", ap=ONES[:], constant=1.0)
    P.I("pool", "memset", ap=CARRY[:], constant=0.0)
    P.I("pool", "memset", ap=HST[:], constant=0.0)
    P.I("pool", "memset", ap=VW[:], constant=0.0)
    P.I("pool", "memset", ap=VW[:, :, 0:1], constant=1.0)
    P.I("pool", "memset", ap=VW[:, :, 128:129], constant=1.0)
    P.I("pool", "memset", ap=SEL2[:], constant=0.0)
    P.I("pool", "memset", ap=SEL2[0:1, :], constant=1.0)
    for tz in (KTE, KTO, KITE, KITO):
        P.I("pool", "memset", ap=tz[:], constant=0.0)
    P.I("pool", "memset", ap=SEL[:], constant=0.0)
    P.I("pool", "memset", ap=SEL[64:65, :], constant=1.0)
    P.I("pool", "memset", ap=LROW[:], constant=0.0)
    P.I("pool", "memset", ap=LROWB[:], constant=0.0)
    P.I("dve", "tensor_copy", out=IDENTB[:], in_=IDENT[:])
    for j in range(4):
        cdma(LCOL[:, j, :], W["a_conv_w"][j].rearrange("(c p) -> p c", p=128), slow=True)
    for j, nm in enumerate(["a_conv_b", "a_gate_r_b", "a_gate_i_b", "a_lambda"]):
        cdma(LCOL[:, 4 + j, :], W[nm][0].rearrange("(c p) -> p c", p=128), slow=True)
    for l in range(2):
        cdma(GCOL[:, l, 0, :], W["norm_mix_g"][l].rearrange("(c p) -> p c", p=128), slow=True)
        cdma(GCOL[:, l, 1, :], W["norm_ffn_g"][l].rearrange("(c p) -> p c", p=128), slow=True)
    cdma(QG[:], W["b_q_norm_g"][0:1, :].partition_broadcast(128))
    cdma(KG[:], W["b_k_norm_g"][0:1, :].partition_broadcast(128))

    P.I("act", "activation", out=LCOL[:, 8, :], in_=LCOL[:, 7, :], func=AF.Exp, scale=-1.0)
    P.I("act", "activation", out=LCOL[:, 8, :], in_=LCOL[:, 8, :], func=AF.Ln, bias=1.0)
    P.I("dve", "tensor_scalar", out=LCOL[:, 8, :], in0=LCOL[:, 8, :], scalar1=-8.0, scalar2=None, op0=ALU.mult)
    P.I("dve", "tensor_scalar", out=LCOL[:, 9, :], in0=LCOL[:, 5, :], scalar1=0.5, scalar2=None, op0=ALU.mult)
    P.I("dve", "tensor_scalar", out=LCOL[:, 10, :], in0=LCOL[:, 6, :], scalar1=0.5, scalar2=None, op0=ALU.mult)
    P.I("dve", "tensor_scalar", out=LCOL[:, 11, :], in0=LCOL[:, 8, :], scalar1=2.0, scalar2=None, op0=ALU.mult)
    P.I("pool", "memset", ap=STAT[:, 48:49], constant=0.5)
    P.I("dve", "tensor_scalar", out=LCOL[:, 12, :], in0=LCOL[:, 8, :], scalar1=0.5, scalar2=None, op0=ALU.mult)

    P.I("act", "activation", out=STAT[:, 0:8], in_=CCOL[:], func=AF.Sigmoid)
    P.I("dve", "tensor_tensor", out=CCOL[:], in0=CCOL[:], in1=STAT[:, 0:8], op=ALU.mult)
    P.I("dve", "tensor_copy", out=CONDB[:], in_=CCOL[:].unsqueeze(2).to_broadcast([128, 8, 128]))

    XTF0 = XT[:].rearrange("p a b -> p (a b)")
    STGA = [BIG[:, 0:2048], BIG[:, 2048:4096], XTF0[:, 0:2048], XTF0[:, 2048:4096]]
    MODP = ARENA[:, 0:2048]
    ADAB = ARENA[:, 2048:4096]
    ada_steps = [(l, cp, k) for l in range(2) for cp in range(3) for k in range(8)]

    def ada_load(i):
        l, cp, k = ada_steps[i]
        P.dma("sp", "stg%d" % (i % 4), STGA[i % 4], W["ada_w"][l, k * 128:(k + 1) * 128, cp * 2048:(cp + 1) * 2048])

    for i in range(3):
        ada_load(i)
    ada_i = [0]
    for l in range(2):
        for cp in range(3):
            cdma(ADAB, W["ada_b"][l:l + 1, cp * 2048:(cp + 1) * 2048].partition_broadcast(128))
            banks = [bank() for _ in range(4)]
            for k in range(8):
                i = ada_i[0]
                ada_i[0] += 1
                stg = STGA[i % 4]
                if i + 3 < len(ada_steps):
                    ada_load(i + 3)
                for q in range(4):
                    P.I("pe", "matmul", out=banks[q][:], lhsT=CONDB[:, k, :], rhs=stg[:, q * 512:(q + 1) * 512],
                        start=(k == 0), stop=(k == 7))
            for q in range(4):
                P.I("dve", "tensor_tensor", out=MODP[:, q * 512:(q + 1) * 512], in0=banks[q][:],
                    in1=ADAB[:, q * 512:(q + 1) * 512], op=ALU.add)
            for sgi in range(2):
                seg = 2 * cp + sgi
                src = MODP[:, sgi * 1024:(sgi + 1) * 1024]
                if seg == 2:
                    P.I("pool", "tensor_copy", out=GATEV[2 * l + 0], in_=src)
                elif seg == 5:
                    P.I("pool", "tensor_copy", out=GATEV[2 * l + 1], in_=src)
                else:
                    slot = {0: 1, 1: 0, 3: 3, 4: 2}[seg]
                    b = bank()
                    for c in range(8):
                        P.I("pe", "transpose", out=b[:, c * 8:c * 8 + 8], in_=MODP[0:8, sgi * 1024 + c * 128:sgi * 1024 + (c + 1) * 128],
                            identity=IDENT[0:8, 0:8])
                    P.I("act", "activation", out=FCOL[:, l, slot, :], in_=b[:, 0:64].rearrange("p (c j) -> p c j", j=8)[:, :, 0],
                        func=AF.Identity)
        for (a_i, g_i) in ((0, 0), (2, 1)):
            P.I("dve", "tensor_scalar", out=FCOL[:, l, a_i, :], in0=FCOL[:, l, a_i, :], scalar1=1.0, scalar2=None, op0=ALU.add)
            P.I("dve", "tensor_tensor", out=FCOL[:, l, a_i, :], in0=FCOL[:, l, a_i, :], in1=GCOL[:, l, g_i, :], op=ALU.mult)

    CB = [ARENA_bf[:, 8192 + i * PIECE:8192 + (i + 1) * PIECE] for i in range(4)]
    XTF = XT[:].rearrange("p a b -> p (a b)")
    STG4 = [BIG[:, 0:2048], BIG[:, 2048:4096], XTF[:, 0:2048], XTF[:, 2048:4096]]
    def tape_load(i):
        nm, parts = pieces[i]
        stg = STG4[i % 4]
        for (npart, kk_, cc_, src, off, cst) in parts:
            dst = stg[0:npart, off:off + kk_ * cst].rearrange("p (k c) -> p k c", c=cst)[:, :, 0:cc_]
            P.dma("sp", "stg%d" % (i % 4), dst, src)

    for i in range(min(3, NP_)):
        tape_load(i)
    for i in range(NP_):
        stg = STG4[i % 4]
        cb = CB[i % 4]
        nm_i = pieces[i][0]
        gi = None
        if nm_i.startswith("a_wo"):
            gi, hh = 0, int(nm_i[4])
        elif nm_i.startswith("b_wo"):
            gi, hh = 2, int(nm_i[4])
        elif "_w2_" in nm_i:
            gi, hh = 2 * int(nm_i[1]) + 1, int(nm_i.split("_")[2])
        if gi is not None:
            P.I("dve", "tensor_tensor", out=cb.rearrange("p (k c) -> p k c", c=512), in0=stg.rearrange("p (k c) -> p k c", c=512),
                in1=GATEV[gi][:, hh * 512:(hh + 1) * 512].unsqueeze(1).to_broadcast([128, 4, 512]), op=ALU.mult)
        elif i % 2 == 0:
            P.I("act", "activation", out=cb, in_=stg, func=AF.Copy)
        else:
            P.I("dve", "tensor_copy", out=cb, in_=stg)
        if i + 3 < NP_:
            tape_load(i + 3)
        P.dma("sp", "tp%d" % (i % 4), tape[i], cb)

    wq = {"n": 0}

    def wload(name):
        s = wq["n"] % NSLOT
        wq["n"] += 1
        slot = WS[:, s, :]
        P.dma("sp", "w%d" % s, slot, tape[pidx[name]])
        return slot

    class Stream:
        def __init__(self, names, depth=NSLOT - 2):
            self.names = names
            self.depth = depth
            self.issued = []
            self.pos = 0
            self.released = set()
            self.auto_prev = None

        def _prefetch(self):
            while len(self.issued) < min(len(self.names), self.pos + self.depth):
                p = len(self.issued)
                prior = p - NSLOT
                if prior >= 0 and prior not in self.released:
                    break
                self.issued.append(wload(self.names[p]))

        def get(self, hold=False):
            if self.auto_prev is not None:
                self.released.add(self.auto_prev)
                self.auto_prev = None
            self._prefetch()
            assert len(self.issued) > self.pos, "weight ring deadlock"
            s = self.issued[self.pos]
            if not hold:
                self.auto_prev = self.pos
            tok = self.pos
            self.pos += 1
            return (s, tok) if hold else s

        def done(self, tok):
            self.released.add(tok)

    stop_after = dbg.get("stop_after", None)
    phases = {"l0mix": ("a_",), "l0": ("a_", "f0"), "l1mix": ("a_", "f0", "b_"), None: ("a_", "f0", "b_", "f1")}[stop_after]
    tile_names = [p[0] for p in pieces if p[0][:2] in phases]
    all_names = tile_names * NT
    WST = Stream(all_names)

    def run_pipelined(gen_fns, width=2):
        it = iter(gen_fns)
        active = []

        def start():
            try:
                f = next(it)
            except StopIteration:
                return
            active.append(f())

        for _ in range(width):
            start()
        while active:
            for g in list(active):
                try:
                    next(g)
                except StopIteration:
                    active.remove(g)
                    start()

    def rms_to_HT(l, which):
        a_i, s_i = (0, 1) if which == 0 else (2, 3)
        XN = L1T
        for b in range(NB):
            ss = STAT[:, b:b + 1]
            P.I("act", "activation", out=JUNK[:, 0:D], in_=XT[:, b, :], func=AF.Square, accum_out=ss)
            P.I("act", "activation", out=STAT[:, 8 + b:9 + b], in_=ss, func=AF.Sqrt, scale=1.0 / D, bias=EPS)
            P.I("dve", "reciprocal", out=STAT[:, 16 + b:17 + b], in_=STAT[:, 8 + b:9 + b])
        for b in range(NB):
            P.I("dve", "tensor_scalar", out=DRT[:, b, :], in0=IDENT[:], scalar1=STAT[:, 16 + b:17 + b], scalar2=None, op0=ALU.mult)
        for cg in range(2):
            banks4 = [bank() for _ in range(4)]
            for ci in range(4):
                c = cg * 4 + ci
                for b in range(NB):
                    P.I("pe", "matmul", out=banks4[ci][:, b * 128:(b + 1) * 128], lhsT=XT[:, b, c * 128:(c + 1) * 128], rhs=DRT[:, b, :],
                        start=True, stop=True)
                P.I("act", "activation", out=HT[:, c, :], in_=banks4[ci][:], func=AF.Identity,
                    scale=FCOL[:, l, a_i, c:c + 1], bias=FCOL[:, l, s_i, c:c + 1])

    def residual(banks4, h, gate_idx):
        for b in range(NB):
            P.I("dve", "tensor_tensor", out=XT[:, b, h * 512:(h + 1) * 512], in0=banks4[b][:], in1=XT[:, b, h * 512:(h + 1) * 512], op=ALU.add)

    def ffn(l):
        rms_to_HT(l, 1)
        RL = L1T
        for pc in range(16):
            slot = WST.get().rearrange("p (k c) -> p k c", c=256)
            for fc in range(2):
                pb = bank()
                for k in range(8):
                    P.I("pe", "matmul", out=pb[:], lhsT=slot[:, k, fc * 128:(fc + 1) * 128], rhs=HT[:, k, :],
                        start=(k == 0), stop=(k == 7))
                r = RL[:, (pc * 2 + fc) % 4, :]
                P.I("act", "activation", out=r, in_=pb[:], func=AF.Relu)
                P.I("pool", "tensor_tensor", out=H1T[:, pc * 2 + fc, :], in0=r, in1=r, op=ALU.mult)
        for h in range(2):
            banks4 = [bank() for _ in range(4)]
            for kp in range(8):
                slot = WST.get().rearrange("p (k c) -> p k c", c=512)
                for b in range(NB):
                    for kk in range(4):
                        kc = kp * 4 + kk
                        P.I("pe", "matmul", out=banks4[b][:], lhsT=H1T[:, kc, b * 128:(b + 1) * 128], rhs=slot[:, kk, :],
                            start=(kc == 0), stop=(kc == 31))
            residual(banks4, h, 2 * l + 1)

    XBP = BIG[:, 0:1030].rearrange("p (c t) -> p c t", t=515)
    XC2 = BIG[:, 1030:3078].rearrange("p (s c t) -> p s c t", s=2, t=512)
    AF32 = ARENA[:, 2560:8192]
    _cs0 = [BIG[:, 3078:3590], BIG[:, 3590:4102]] + [AF32[:, i * 512:(i + 1) * 512] for i in range(4)]
    _cs1 = [AF32[:, 2048 + i * 512:2048 + (i + 1) * 512] for i in range(6)]
    CSET = [_cs0, _cs1]
    XCB = ARENA_bf[:, 2 * (2560 + 5120):2 * (2560 + 5120) + 1024].rearrange("p (c t) -> p c t", t=512)
    HALFB = STAT[:, 48:49].to_broadcast([128, 512])

    def l0_mixer():
        rms_to_HT(0, 0)
        st = {"bs_done": set(), "gates": {}, "cdone": set(), "slots": {}}

        def bstage(n):
            while n >= 1 and (n - 1) not in st["bs_done"]:
                yield
            (s_xb_, t_xb) = WST.get(hold=True)
            (s_gb_, t_gb) = WST.get(hold=True)
            (s_gt_, t_gt) = WST.get(hold=True)
            s_xb = s_xb_.rearrange("p (k c) -> p k c", c=256)
            st["slots"][n] = (s_gb_.rearrange("p (k c) -> p k c", c=256), t_gb,
                              s_gt_[:, 0:1024].rearrange("p (g k c) -> p g k c", g=2, k=2), t_gt)
            XC = XC2[:, n % 2]
            while n >= 2 and not ((n - 2, 0) in st["cdone"] and (n - 2, 1) in st["cdone"]):
                yield
            for ci in range(2):
                c = 2 * n + ci
                pb = bank()
                for k in range(8):
                    P.I("pe", "matmul", out=pb[:], lhsT=s_xb[:, k, ci * 128:(ci + 1) * 128], rhs=HT[:, k, :],
                        start=(k == 0), stop=(k == 7))
                P.I("pool", "tensor_copy", out=XBP[:, ci, 0:3], in_=CARRY[:, c, :])
                P.I("act", "activation", out=XBP[:, ci, 3:515], in_=pb[:], func=AF.Copy)
                P.I("act", "activation", out=XC[:, ci, :], in_=pb[:], func=AF.Identity, scale=LCOL[:, 3, c:c + 1],
                    bias=LCOL[:, 4, c:c + 1])
                yield
                P.I("pool", "tensor_copy", out=CARRY[:, c, :], in_=XBP[:, ci, 512:515])
                for j in range(3):
                    P.I("dve", "scalar_tensor_tensor", out=XC[:, ci, :], in0=XBP[:, ci, j:j + 512], scalar=LCOL[:, j, c:c + 1],
                        in1=XC[:, ci, :], op0=ALU.mult, op1=ALU.add)
                    yield
            WST.done(t_xb)
            while n >= 1 and st["gates"].get(n - 1, 0) < 2:
                yield
            P.I("dve", "tensor_copy", out=XCB[:], in_=XC[:])
            st["bs_done"].add(n)
            yield

        def chunk(n, oc):
            c = 2 * n + oc
            R_, I_, B_, H_, GB_, T1_ = CSET[c % 2]
            while n not in st["bs_done"]:
                yield
            while c >= 2 and ((c - 2) // 2, (c - 2) % 2) not in st["cdone"]:
                yield
            s_gb, t_gb, s_gt, t_gt = st["slots"][n]
            XC = XC2[:, n % 2]
            for g, dst, bj in ((0, R_, 9), (1, I_, 10)):
                pb = bank()
                for kc in range(2):
                    P.I("pe", "matmul", out=pb[:], lhsT=s_gt[:, g, kc, oc * 128:(oc + 1) * 128], rhs=XCB[:, kc, :],
                        start=(kc == 0), stop=(kc == 1))
                P.I("act", "activation", out=dst, in_=pb[:], func=AF.Tanh, scale=0.5, bias=LCOL[:, bj, c:c + 1])
            st["gates"][n] = st["gates"].get(n, 0) + 1
            if st["gates"][n] == 2:
                WST.done(t_gt)
            yield
            pb = bank()
            for k in range(8):
                P.I("pe", "matmul", out=pb[:], lhsT=s_gb[:, k, oc * 128:(oc + 1) * 128], rhs=HT[:, k, :],
                    start=(k == 0), stop=(k == 7))
            P.I("act", "activation", out=GB_, in_=pb[:], func=AF.Copy)
            P.I("act", "activation", out=T1_, in_=pb[:], func=AF.Square, scale=math.sqrt(0.044715))
            if oc == 1:
                WST.done(t_gb)
            yield
            P.I("act", "activation", out=B_, in_=R_, func=AF.Exp, scale=LCOL[:, 8, c:c + 1], bias=LCOL[:, 8, c:c + 1])
            yield
            P.I("act", "activation", out=R_, in_=R_, func=AF.Exp, scale=LCOL[:, 12, c:c + 1], bias=LCOL[:, 12, c:c + 1])
            yield
            sqq = st.setdefault(("sq", n), [])
            sqq.append(B_)
            if len(sqq) == 2:
                for bb in sqq:
                    P.I("act", "activation", out=bb, in_=bb, func=AF.Sqrt, scale=-1.0, bias=1.0)
                st[("sqdone", n)] = True
            while not st.get(("sqdone", n)):
                yield
            yield
            P.I("dve", "scalar_tensor_tensor", out=I_, in0=I_, scalar=1.0, in1=XC[:, oc, :], op0=ALU.add, op1=ALU.mult)
            yield
            P.I("dve", "scalar_tensor_tensor", out=B_, in0=B_, scalar=0.5, in1=I_, op0=ALU.mult, op1=ALU.mult)
            yield
            P.I("dve", "scalar_tensor_tensor", out=T1_, in0=T1_, scalar=1.0, in1=GB_, op0=ALU.add, op1=ALU.mult)
            yield
            P.I("act", "activation", out=T1_, in_=T1_, func=AF.Tanh, scale=0.7978845608028654)
            yield
            P.I("dve", "scalar_tensor_tensor", out=T1_, in0=T1_, scalar=1.0, in1=GB_, op0=ALU.add, op1=ALU.mult)
            yield
            P.I("dve", "tensor_tensor_scan", out=H_, data0=R_, data1=B_, initial=HST[:, c:c + 1], op0=ALU.mult, op1=ALU.add)
            yield
            P.I("dve", "tensor_copy", out=HST[:, c:c + 1], in_=H_[:, 511:512])
            P.I("dve", "scalar_tensor_tensor", out=YT[:, c, :], in0=H_, scalar=0.5, in1=T1_, op0=ALU.mult, op1=ALU.mult)
            st["cdone"].add((n, oc))
            yield

        gens = []
        for n in range(5):
            gens.append(lambda n=n: bstage(n))
            gens.append(lambda n=n: chunk(n, 0))
            gens.append(lambda n=n: chunk(n, 1))
        run_pipelined(gens, width=3)
        for h in range(2):
            banks4 = [bank() for _ in range(4)]
            for kp in range(3):
                slot = WST.get().rearrange("p (k c) -> p k c", c=512)
                for b in range(NB):
                    for kk in range(min(4, 10 - kp * 4)):
                        kc = kp * 4 + kk
                        P.I("pe", "matmul", out=banks4[b][:], lhsT=YT[:, kc, b * 128:(b + 1) * 128], rhs=slot[:, kk, :],
                            start=(kc == 0), stop=(kc == 9))
            residual(banks4, h, 0)

    def rope(dst, src, b, nh, eng_a, eng_b, tmp, tabs=None):
        if tabs is None:
            tabs = (COS[:, b:b + 1, :], SIN[:, b:b + 1, :], COS[:, b:b + 1, :], SIN[:, b:b + 1, :])
        c1, s2, c2, s1 = (tt.to_broadcast([128, nh, 32]) for tt in tabs)
        x1, x2 = src[:, :, 0:32], src[:, :, 32:64]
        P.I(eng_a, "tensor_tensor", out=tmp[:, :, 0:32], in0=x2, in1=s2, op=ALU.mult)
        P.I(eng_b, "tensor_tensor", out=dst[:, :, 0:32], in0=x1, in1=c1, op=ALU.mult)
        yield
        P.I(eng_a, "tensor_tensor", out=tmp[:, :, 32:64], in0=x1, in1=s1, op=ALU.mult)
        P.I(eng_b, "tensor_tensor", out=dst[:, :, 32:64], in0=x2, in1=c2, op=ALU.mult)
        yield
        P.I(eng_b, "tensor_tensor", out=dst[:, :, 0:32], in0=dst[:, :, 0:32], in1=tmp[:, :, 0:32], op=ALU.subtract)
        yield
        P.I(eng_b, "tensor_tensor", out=dst[:, :, 32:64], in0=dst[:, :, 32:64], in1=tmp[:, :, 32:64], op=ALU.add)
        yield

    def headnorm(q3, nh, st0, SQ):
        sq = SQ[:, 0:nh * 64].rearrange("p (h d) -> p h d", d=64)
        ss = STAT[:, st0:st0 + nh]
        P.I("act", "activation", out=sq, in_=q3, func=AF.Square)
        yield
        P.I("dve", "tensor_reduce", out=ss, in_=sq, axis=AX.X, op=ALU.add)
        yield
        P.I("act", "activation", out=ss, in_=ss, func=AF.Sqrt, scale=1.0 / 64, bias=EPS)
        yield
        P.I("dve", "reciprocal", out=ss, in_=ss)
        yield
        P.I("dve", "tensor_tensor", out=q3, in0=q3, in1=ss.unsqueeze(2).to_broadcast([128, nh, 64]), op=ALU.mult)
        yield

    def l1_mixer(t):
        rms_to_HT(1, 0)
        P.dma("sp", "rope0", COS[:], cos_d[t * T:(t + 1) * T, :].rearrange("(b p) f -> p b f", p=128))
        P.dma("sp", "rope1", SIN[:], sin_d[t * T:(t + 1) * T, :].rearrange("(b p) f -> p b f", p=128))
        for RT_, G_t in ((RTQ, QG), (RTK, KG)):
            g1 = G_t[:, 0:32].unsqueeze(1).to_broadcast([128, NB, 32])
            g2 = G_t[:, 32:64].unsqueeze(1).to_broadcast([128, NB, 32])
            P.I("pool", "tensor_tensor", out=RT_[:, :, 0, :], in0=COS[:], in1=g1, op=ALU.mult)
            P.I("pool", "tensor_tensor", out=RT_[:, :, 1, :], in0=SIN[:], in1=g2, op=ALU.mult)
            P.I("pool", "tensor_tensor", out=RT_[:, :, 2, :], in0=COS[:], in1=g2, op=ALU.mult)
            P.I("pool", "tensor_tensor", out=RT_[:, :, 3, :], in0=SIN[:], in1=g1, op=ALU.mult)
        QIR = BIG[:, 0:2048].rearrange("p (b f) -> p b f", f=512)
        jbanks = {}
        evac_count = {}

        def mm_j(j):
            w = min(512, B_IN - j * 512)
            banks4 = [bank() for _ in range(4)]
            jbanks[j] = banks4
            for kp in range(2):
                slot = WST.get().rearrange("p (k c) -> p k c", c=512)
                for b in range(NB):
                    for kk in range(4):
                        kc = kp * 4 + kk
                        P.I("pe", "matmul", out=banks4[b][:, 0:w], lhsT=HT[:, kc, b * 128:(b + 1) * 128], rhs=slot[:, kk, 0:w],
                            start=(kc == 0), stop=(kc == 7))

        def step(j, b):
            w = min(512, B_IN - j * 512)
            if b == 0 and j == 0:
                ps_rr[0] = 4
                mm_j(0)
                ps_rr[0] = 0
            while j not in jbanks:
                yield
            banks4 = jbanks[j]
            gb = t * NB + b
            par = (j * NB + b) % 2
            LX = L1T if par == 0 else L1U
            QF = LX[:, 0, :]
            QR = LX[:, 1, :]
            TM = LX[:, 2, :]
            SQ = LX[:, 3, :]
            P.I("act", "activation", out=QF[:, 0:w], in_=banks4[b][:, 0:w], func=AF.Copy)
            evac_count[j] = evac_count.get(j, 0) + 1
            if evac_count[j] == NB and j + 1 < 4:
                save_rr = ps_rr[0]
                ps_rr[0] = 4
                mm_j(j + 1)
                ps_rr[0] = save_rr
            yield
            if j < 2:
                q3 = QF.rearrange("p (h d) -> p h d", d=64)
                yield from headnorm(q3, 8, 24 + 8 * par, SQ)
                yield from rope(QR.rearrange("p (h d) -> p h d", d=64), q3, b, 8, "pool", "dve", TM.rearrange("p (h d) -> p h d", d=64),
                                tabs=tuple(RTQ[:, b:b + 1, i, :] for i in range(4)))
                pb = PS[par]
                for pr in range(4):
                    P.I("pe", "transpose", out=pb[:, pr * 128:(pr + 1) * 128], in_=QR[:, pr * 128:(pr + 1) * 128], identity=IDENT[:])
                yield
                P.I("act", "activation", out=QT[:, 4 * j:4 * j + 4, b * 128:(b + 1) * 128],
                    in_=pb[:].rearrange("p (a q) -> p a q", q=128), func=AF.Copy)
                yield
            elif j == 2:
                k3 = QF[:, 0:64].rearrange("p (h d) -> p h d", d=64)
                yield from headnorm(k3, 1, 40 + par, SQ)
                KR2 = QR[:, 0:128].rearrange("p (h d) -> p h d", d=64)
                yield from rope(KR2[:, 0:1, :], k3, b, 1, "pool", "dve", TM[:, 0:64].rearrange("p (h d) -> p h d", d=64),
                                tabs=tuple(RTK[:, b:b + 1, i, :] for i in range(4)))
                P.I("pool", "tensor_copy", out=KR2[:, 1:2, :], in_=KR2[:, 0:1, :])
                yield
                pb = PS[par]
                P.I("pe", "transpose", out=pb[:, 0:128], in_=QR[:, 0:128], identity=IDENT[:])
                yield
                P.I("act", "activation", out=KTE[0:64, gb * 128:(gb + 1) * 128], in_=pb[0:64, 0:128], func=AF.Copy)
                P.I("act", "activation", out=KTO[64:128, gb * 128:(gb + 1) * 128], in_=pb[64:128, 0:128], func=AF.Copy)
                P.I("pool", "tensor_copy", out=VW[:, gb, 64:128], in_=QF[:, 64:128])
                yield
                yield from rope(QIR[:, b, 0:384].rearrange("p (h d) -> p h d", d=64), QF[:, 128:512].rearrange("p (h d) -> p h d", d=64),
                                b, 6, "pool", "dve", TM[:, 128:512].rearrange("p (h d) -> p h d", d=64))
            else:
                yield from rope(QIR[:, b, 384:512].rearrange("p (h d) -> p h d", d=64), QF[:, 0:128].rearrange("p (h d) -> p h d", d=64),
                                b, 2, "pool", "dve", TM[:, 0:128].rearrange("p (h d) -> p h d", d=64))
                KI2 = QR[:, 0:128].rearrange("p (h d) -> p h d", d=64)
                yield from rope(KI2[:, 0:1, :], QF[:, 128:192].rearrange("p (h d) -> p h d", d=64), b, 1, "pool", "dve",
                                TM[:, 128:192].rearrange("p (h d) -> p h d", d=64))
                P.I("pool", "tensor_copy", out=KI2[:, 1:2, :], in_=KI2[:, 0:1, :])
                yield
                pb = PS[par]
                P.I("pe", "transpose", out=pb[:, 0:128], in_=QR[:, 0:128], identity=IDENT[:])
                yield
                P.I("act", "activation", out=KITE[0:64, gb * 128:(gb + 1) * 128], in_=pb[0:64, 0:128], func=AF.Copy)
                P.I("act", "activation", out=KITO[64:128, gb * 128:(gb + 1) * 128], in_=pb[64:128, 0:128], func=AF.Copy)
                yield
                wsc = (8 ** -0.5) / 8.0
                P.I("dve", "tensor_scalar", out=WIS[:, b, 0, :], in0=QF[:, 192:200], scalar1=wsc, scalar2=None, op0=ALU.mult)
                yield
                P.I("dve", "tensor_scalar", out=WIS[:, b, 2, :], in0=WIS[:, b, 0, :], scalar1=0.0, scalar2=2.0, op0=ALU.is_ge, op1=ALU.mult)
                yield
                P.I("dve", "tensor_scalar", out=WIS[:, b, 2, :], in0=WIS[:, b, 2, :], scalar1=-1.0, scalar2=None, op0=ALU.add)
                yield
                P.I("dve", "tensor_tensor", out=WIS[:, b, 1, :], in0=WIS[:, b, 0, :], in1=WIS[:, b, 2, :], op=ALU.mult)
                yield
                pb = PS[2 + par]
                for pr in range(4):
                    P.I("pe", "transpose", out=pb[:, pr * 128:(pr + 1) * 128], in_=QIR[:, b, pr * 128:(pr + 1) * 128], identity=IDENT[:])
                yield
                P.I("act", "activation", out=QIT[:, :, b * 128:(b + 1) * 128], in_=pb[:].rearrange("p (a q) -> p a q", q=128), func=AF.Copy)
                yield

        run_pipelined([(lambda j=j, b=b: step(j, b)) for j in range(4) for b in range(NB)], width=2)

        L1TF = L1T[:].rearrange("p a b -> p (a b)")
        L1UF = L1U[:].rearrange("p a b -> p (a b)")
        LO, HI, CNT, G_, MID = (BIS[:, i:i + 1] for i in range(5))
        SGN, TT_, LO2, HI2 = BIS[:, 5:6], BIS[:, 6:7], BIS[:, 7:8], BIS[:, 8 + NITER:9 + NITER]
        WK = BIS[:, 8:8 + NITER]

        def sc(buf, lo, hi):
            if buf == 0:
                return BIG[:, lo:hi]
            if hi <= 2048:
                return L1TF[:, lo:hi]
            assert lo >= 2048
            return L1UF[:, lo - 2048:hi - 2048]

        def idx_gen(b):
            gb = t * NB + b
            buf = gb % 2
            nk = (gb + 1) * 128
            nkc = (nk + 511) // 512
            P.I("dve", "tensor_tensor", out=DSG[:], in0=IDENTB[:].unsqueeze(1).to_broadcast([128, 8, 128]),
                in1=WIS[:, b, 2, :].unsqueeze(2).to_broadcast([128, 8, 128]), op=ALU.mult)
            yield
            ixs = [(kc, h) for kc in range(nkc) for h in range(8)]

            def emit_L(i):
                kc, h = ixs[i]
                w = min(512, nk - kc * 512)
                P.I("pe", "matmul", out=PS[i % 4][:, 0:w], lhsT=QIT[:, h // 2, b * 128:(b + 1) * 128],
                    rhs=(KITE if h % 2 == 0 else KITO)[:, kc * 512:kc * 512 + w], start=True, stop=True)

            for i in range(min(3, len(ixs))):
                emit_L(i)
            for i, (kc, h) in enumerate(ixs):
                w = min(512, nk - kc * 512)
                rl = RLB[:, i % 4, 0:w]
                if i % 3 != 2:
                    P.I("act", "activation", out=rl, in_=PS[i % 4][:, 0:w], func=AF.Relu, scale=WIS[:, b, 1, h:h + 1])
                else:
                    P.I("dve", "tensor_scalar", out=rl, in0=PS[i % 4][:, 0:w], scalar1=0.0, scalar2=WIS[:, b, 1, h:h + 1],
                        op0=ALU.max, op1=ALU.mult)
                scb = PS[4 + (kc % 2)]
                if i + 3 < len(ixs):
                    emit_L(i + 3)
                P.I("pe", "matmul", out=scb[:, 0:w], lhsT=DSG[:, h, :], rhs=rl, start=(h == 0), stop=(h == 7))
                if h == 7:
                    P.I("act", "activation", out=sc(buf, kc * 512, kc * 512 + w), in_=scb[:, 0:w], func=AF.Copy)
                yield

        def bisect_gen(b):
            gb = t * NB + b
            buf = gb % 2
            nk = (gb + 1) * 128
            split = buf == 1 and nk > 2048
            if nk > TOPK:
                if split:
                    P.I("dve", "tensor_reduce", out=LO, in_=sc(buf, 0, 2048), axis=AX.X, op=ALU.min)
                    P.I("dve", "tensor_reduce", out=LO2, in_=sc(buf, 2048, nk), axis=AX.X, op=ALU.min)
                    yield
                    P.I("dve", "tensor_tensor", out=LO, in0=LO, in1=LO2, op=ALU.min)
                else:
                    P.I("dve", "tensor_reduce", out=LO, in_=sc(buf, 0, nk), axis=AX.X, op=ALU.min)
                yield
            dg = sc(buf, gb * 128, (gb + 1) * 128)
            P.I("dve", "tensor_tensor", out=dg, in0=dg, in1=CAUS[:], op=ALU.add)
            yield
            if nk > TOPK:
                if split:
                    P.I("dve", "tensor_reduce", out=HI, in_=sc(buf, 0, 2048), axis=AX.X, op=ALU.max)
                    P.I("dve", "tensor_reduce", out=HI2, in_=sc(buf, 2048, nk), axis=AX.X, op=ALU.max)
                    yield
                    P.I("dve", "tensor_tensor", out=HI, in0=HI, in1=HI2, op=ALU.max)
                else:
                    P.I("dve", "tensor_reduce", out=HI, in_=sc(buf, 0, nk), axis=AX.X, op=ALU.max)
                yield
                P.I("dve", "tensor_tensor", out=HI, in0=HI, in1=LO, op=ALU.subtract)
                yield
                P.I("dve", "tensor_tensor", out=WK, in0=PW2[:], in1=HI.to_broadcast([128, NITER]), op=ALU.mult)
                yield
                n1 = 2048 if split else max(128, (int(nk * 0.45) // 128) * 128)
                n2 = nk - n1
                P.I("dve", "tensor_tensor", out=MID, in0=LO, in1=WK[:, 0:1], op=ALU.add)
                yield
                thr_c = float(2 * TOPK - n2)
                for it in range(NITER):
                    P.I("act", "activation", out=JUNK[:, n1:nk], in_=sc(buf, n1, nk), func=AF.Sign, scale=-1.0, bias=MID, accum_out=SGN)
                    P.I("dve", "tensor_scalar", out=JUNK[:, 0:n1], in0=sc(buf, 0, n1), scalar1=MID, scalar2=None, op0=ALU.is_ge,
                        op1=ALU.add, accum_out=CNT)
                    yield
                    yield
                    P.I("dve", "scalar_tensor_tensor", out=TT_, in0=CNT, scalar=2.0, in1=SGN, op0=ALU.mult, op1=ALU.subtract)
                    yield
                    if it + 1 < NITER:
                        P.I("dve", "scalar_tensor_tensor", out=G_, in0=TT_, scalar=thr_c, in1=WK[:, it:it + 1], op0=ALU.is_ge, op1=ALU.mult)
                        yield
                        P.I("dve", "scalar_tensor_tensor", out=MID, in0=MID, scalar=WK[:, it + 1:it + 2], in1=G_, op0=ALU.subtract, op1=ALU.add)
                        yield
                    else:
                        P.I("dve", "scalar_tensor_tensor", out=G_, in0=TT_, scalar=thr_c, in1=WK[:, it:it + 1], op0=ALU.is_lt, op1=ALU.mult)
                        yield
                        P.I("dve", "tensor_tensor", out=LO, in0=MID, in1=G_, op=ALU.subtract)
                        yield
            else:
                P.I("dve", "memset", ap=LO, constant=-1.0e29)
                yield

        def idx_len(b):
            nk = (t * NB + b + 1) * 128
            return 8 * ((nk + 511) // 512) + 1

        def bisect_len(b):
            nk = (t * NB + b + 1) * 128
            return (6 * NITER + 8) if nk > TOPK else 2

        def mask_build(b):
            gb = t * NB + b
            buf = gb % 2
            nk = (gb + 1) * 128
            nkc = (nk + 511) // 512
            for kc in range(nkc):
                w = min(512, nk - kc * 512)
                mk = L0X[:, (kc % 2) * 512:(kc % 2) * 512 + w]
                P.I("dve", "tensor_scalar", out=mk, in0=sc(buf, kc * 512, kc * 512 + w), scalar1=LO, scalar2=None, op0=ALU.is_ge)
                pb = PS[2 + (kc % 2)]
                for q in range(w // 128):
                    P.I("pe", "transpose", out=pb[:, q * 128:(q + 1) * 128], in_=mk[:, q * 128:(q + 1) * 128], identity=IDENT[:])
                P.I("act", "activation", out=MASKT[:, kc * 4:kc * 4 + w // 128, :], in_=pb[:, 0:w].rearrange("p (a q) -> p a q", q=128), func=AF.Copy)

        def attention(b):
            gb = t * NB + b
            ACC = PS[4:8]
            its = [(kt, g) for kt in range(gb + 1) for g in range(4)]
            DEPTH = 3

            def emit_S(i):
                kt, g = its[i]
                half, pairset = g // 2, g % 2
                P.I("pe", "matmul", out=PS[i % 4][:], lhsT=(KTE if half == 0 else KTO)[:, kt * 128:(kt + 1) * 128],
                    rhs=QT[:, pairset * 4:pairset * 4 + 4, b * 128:(b + 1) * 128], start=True, stop=True)

            for i in range(min(DEPTH, len(its))):
                emit_S(i)
            for i, (kt, g) in enumerate(its):
                eb = EB[:, i % 4, :]
                pt = PT[:, i % 4, :]
                P.I("act", "activation", out=eb, in_=PS[i % 4][:], func=AF.Exp, scale=0.125)
                P.I("dve", "tensor_tensor", out=pt.rearrange("p (a q) -> p a q", q=128), in0=eb.rearrange("p (a q) -> p a q", q=128),
                    in1=MASKT[:, kt:kt + 1, :].to_broadcast([128, 4, 128]), op=ALU.mult)
                if i + DEPTH < len(its):
                    emit_S(i + DEPTH)
                if g // 2 == 0:
                    P.I("pe", "matmul", out=ACC[g][0:65, :], lhsT=VW[:, kt, 64:129], rhs=pt, start=(kt == 0), stop=(kt == gb))
                else:
                    P.I("pe", "matmul", out=ACC[g][:], lhsT=VW[:, kt, 0:128], rhs=pt, start=(kt == 0), stop=(kt == gb))
            for g in range(4):
                half, pairset = g // 2, g % 2
                lr = 64 if half == 0 else 0
                p0, p1 = (0, 64) if half == 0 else (64, 128)
                LR_ = LROW if g % 2 == 0 else LROWB
                RB_ = RB if g % 2 == 0 else RBB
                P.I("act", "activation", out=LR_[lr:lr + 1, :], in_=ACC[g][lr:lr + 1, :], func=AF.Copy)
                pb = PS[g % 4]
                P.I("pe", "matmul", out=pb[:], lhsT=(SEL if half == 0 else SEL2)[:], rhs=LR_[:], start=True, stop=True)
                P.I("dve", "reciprocal", out=RB_[p0:p1, :], in_=pb[p0:p1, :])
                P.I("dve", "tensor_tensor", out=OUTT2[p0:p1, pairset * 4:(pairset + 1) * 4, b * 128:(b + 1) * 128],
                    in0=ACC[g][p0:p1, :].rearrange("p (a q) -> p a q", q=128), in1=RB_[p0:p1, :].rearrange("p (a q) -> p a q", q=128), op=ALU.mult)

        def run_weighted(ga, na, gb_, nb):
            ia = ib = 0
            a_done = b_done = False
            while not (a_done and b_done):
                if b_done or (not a_done and ia * nb <= ib * na):
                    try:
                        next(ga)
                        ia += 1
                    except StopIteration:
                        a_done = True
                else:
                    try:
                        next(gb_)
                        ib += 1
                    except StopIteration:
                        b_done = True

        for _ in idx_gen(0):
            pass
        for b in range(NB):
            if b + 1 < NB:
                run_weighted(bisect_gen(b), bisect_len(b), idx_gen(b + 1), idx_len(b + 1))
            else:
                for _ in bisect_gen(b):
                    pass
            mask_build(b)
            attention(b)
        for h in range(2):
            banks4 = [bank() for _ in range(4)]
            for kp in range(2):
                slot = WST.get().rearrange("p (k c) -> p k c", c=512)
                for b in range(NB):
                    for kk in range(4):
                        hs = kp * 4 + kk
                        P.I("pe", "matmul", out=banks4[b][:], lhsT=OUTT2[:, hs, b * 128:(b + 1) * 128], rhs=slot[:, kk, :],
                            start=(hs == 0), stop=(hs == 7))
            residual(banks4, h, 2)

    for t in range(NT):
        for b in range(NB):
            P.dma("sp", "xin%d" % b, XT[:, b, :], x_d[t * T + b * 128:t * T + (b + 1) * 128, :])
        l0_mixer()
        if stop_after != "l0mix":
            ffn(0)
            if stop_after != "l0":
                l1_mixer(t)
                if stop_after != "l1mix":
                    ffn(1)
        for b in range(NB):
            P.dma("sp", "xout%d" % b, out_d[t * T + b * 128:t * T + (b + 1) * 128, :], XT[:, b, :])
    P.fence("sp", [out_d[0:NT * T, :]])
    if stop_after is not None:
        pass
    P.emit()
    es.close()
    return nc


def host_consts():
    ident = np.eye(128, dtype=np.float32)
    q = np.arange(128)[:, None]
    k = np.arange(128)[None, :]
    caus = np.where(k <= q, 0.0, NEG).astype(np.float32)
    inv = (np.float32(10000.0) ** (-(np.arange(0, 64, 2, dtype=np.float32)) / np.float32(64))).astype(np.float32)
    ang = (np.arange(S_FULL, dtype=np.float32)[:, None] * inv[None, :]).astype(np.float32)
    cosT = np.cos(ang).astype(np.float32)
    sinT = np.sin(ang).astype(np.float32)
    pw2 = np.tile((0.5 ** np.arange(1, NITER + 1)).astype(np.float32)[None, :], (128, 1))
    return {"ident": ident, "caus": caus, "cosT": cosT, "sinT": sinT, "pw2": pw2}


def make_in_map(inputs, b, consts):
    m = dict(consts)
    m["x"] = np.ascontiguousarray(inputs["x"][b])
    m["ccol"] = np.ascontiguousarray(inputs["c"][b].reshape(8, 128).T)
    for nm in ("norm_mix_g", "norm_ffn_g", "ada_w", "ada_b"):
        m[nm] = np.ascontiguousarray(inputs[nm])
    for nm in ("a_w_in", "a_conv_w", "a_gate_r_w", "a_gate_i_w", "a_w_out", "b_w_in", "b_w_out"):
        m[nm] = np.ascontiguousarray(inputs[nm][0])
    for nm in ("a_conv_b", "a_gate_r_b", "a_gate_i_b", "a_lambda", "b_q_norm_g", "b_k_norm_g"):
        m[nm] = np.ascontiguousarray(inputs[nm])
    for l in range(2):
        m["ffn_w1_%d" % l] = np.ascontiguousarray(inputs["ffn_w1"][l])
        m["ffn_w2_%d" % l] = np.ascontiguousarray(inputs["ffn_w2"][l])
    return m


_NC_CACHE = {}


def kernel(**inputs):
    inputs = {k: np.asarray(v) for k, v in inputs.items()}
    if "full" not in _NC_CACHE:
        _NC_CACHE["full"] = build_nc(S_FULL)
    nc = _NC_CACHE["full"]
    consts = host_consts()
    nb = inputs["x"].shape[0]
    in_maps = [make_in_map(inputs, b, consts) for b in range(nb)]
    res = run_bass_kernel_spmd(nc, in_maps, core_ids=list(range(nb)))
    out = np.stack([np.asarray(r["out"]) for r in res.results], axis=0)
    return out.astype(np.float32)
```
